# Optimizing a Trainium2 kernel written in Bass

```python
import math
import jax, jax.numpy as jnp
from jax import lax
import numpy as np

D_MODEL = 1024
BATCH = 16
SEQ = 256
DEPTH = 4
DEC_BATCH = 8
DEC_SEQ = 4096
PAST_LEN = 512

GRID_W = 64
Q_BLOCK = 128
ROPE_THETA = 10000.0
EPS = 1e-6
GQA_HEADS = 6
GQA_KV_HEADS = 2
GQA_REP = GQA_HEADS // GQA_KV_HEADS
GQA_HEAD_DIM = 64
MLA_HEADS = 6
MLA_Q_RANK = 256
MLA_KV_RANK = 128
MLA_NOPE_DIM = 64
MLA_ROPE_DIM = 32
MLA_QK_DIM = MLA_NOPE_DIM + MLA_ROPE_DIM
MLA_V_DIM = 64
SSM_GROUPS = 16
SSM_GROUP_CH = 16
SSM_STATE = 64
SSM_CH = SSM_GROUPS * SSM_GROUP_CH
GQA_OUT = GQA_HEADS * GQA_HEAD_DIM
MLA_OUT = MLA_HEADS * MLA_V_DIM
MIX_WIDTH = GQA_OUT + MLA_OUT + SSM_CH
IN_SPLITS = (GQA_HEADS * GQA_HEAD_DIM, GQA_KV_HEADS * GQA_HEAD_DIM, GQA_KV_HEADS * GQA_HEAD_DIM,
             MLA_Q_RANK, MLA_KV_RANK, MLA_ROPE_DIM, SSM_CH)
IN_WIDTH = sum(IN_SPLITS)
IN_OFFSETS = tuple(int(o) for o in np.cumsum(IN_SPLITS)[:-1])
D_FF = -(-8 * D_MODEL // (3 * 256)) * 256

kernel_name = 'hymba_dit_gqa_mla_s5_step'


def rms_norm(x, g):
    xf = x.astype(jnp.float32)
    y = xf * lax.rsqrt(jnp.mean(xf * xf, axis=-1, keepdims=True) + EPS)
    return (y * g.astype(jnp.float32)).astype(x.dtype)


def adaln(cond, w, b):
    mod = jax.nn.silu(cond) @ w + b
    return jnp.split(mod[..., None, :], 6, axis=-1)


def axial_rope(n_tokens, rot_dim):
    rows = n_tokens // GRID_W
    row = jnp.repeat(jnp.arange(rows, dtype=jnp.float32), GRID_W)
    col = jnp.tile(jnp.arange(GRID_W, dtype=jnp.float32), rows)
    quarter = rot_dim // 4
    inv = ROPE_THETA ** (-jnp.arange(quarter, dtype=jnp.float32) / quarter)
    ang = jnp.concatenate([row[:, None] * inv, col[:, None] * inv], axis=-1)
    return jnp.cos(ang)[:, None, :], jnp.sin(ang)[:, None, :]


def apply_rope(x, cos, sin):
    x1, x2 = jnp.split(x.astype(jnp.float32), 2, axis=-1)
    return jnp.concatenate([x1 * cos - x2 * sin, x1 * sin + x2 * cos], axis=-1).astype(x.dtype)


def block_attention(q, k, v, scale):
    B_, L, G, R, Dk = q.shape
    nb = L // Q_BLOCK
    qb = q.reshape(B_, nb, Q_BLOCK, G, R, Dk).transpose(1, 0, 2, 3, 4, 5)

    def one_block(qi):
        s = jnp.einsum('bqgrd,bkgd->bgrqk', qi, k).astype(jnp.float32) * scale
        p = jax.nn.softmax(s, axis=-1).astype(v.dtype)
        return jnp.einsum('bgrqk,bkge->bqgre', p, v)

    o = lax.map(one_block, qb)
    return o.transpose(1, 0, 2, 3, 4, 5).reshape(B_, L, G * R, v.shape[-1])


def _complex_affine_combine(e1, e2):
    a1r, a1i, b1r, b1i = e1
    a2r, a2i, b2r, b2i = e2
    return (a2r * a1r - a2i * a1i, a2r * a1i + a2i * a1r,
            a2r * b1r - a2i * b1i + b2r, a2r * b1i + a2i * b1r + b2i)


def s5_direction(u, lam_re, lam_im, log_dt, b_re, b_im, c_re, c_im, s0_re, s0_im, reverse):
    dt = jnp.exp(log_dt)[:, None]
    decay = jnp.exp(lam_re * dt)
    ab_re = decay * jnp.cos(lam_im * dt)
    ab_im = decay * jnp.sin(lam_im * dt)
    den = lam_re * lam_re + lam_im * lam_im
    num_re = ab_re - 1.0
    coef_re = (num_re * lam_re + ab_im * lam_im) / den
    coef_im = (ab_im * lam_re - num_re * lam_im) / den
    bb_re = coef_re[..., None] * b_re - coef_im[..., None] * b_im
    bb_im = coef_re[..., None] * b_im + coef_im[..., None] * b_re
    bu_re = jnp.einsum('gph,blgh->blgp', bb_re, u)
    bu_im = jnp.einsum('gph,blgh->blgp', bb_im, u)
    L = u.shape[1]
    if s0_re is not None:
        first = L - 1 if reverse else 0
        bu_re = bu_re.at[:, first].add(ab_re * s0_re - ab_im * s0_im)
        bu_im = bu_im.at[:, first].add(ab_re * s0_im + ab_im * s0_re)
    a_re = jnp.broadcast_to(ab_re, (1, L) + ab_re.shape)
    a_im = jnp.broadcast_to(ab_im, (1, L) + ab_im.shape)
    _, _, s_re, s_im = lax.associative_scan(_complex_affine_combine, (a_re, a_im, bu_re, bu_im),
                                            reverse=reverse, axis=1)
    y = jnp.einsum('ghp,blgp->blgh', c_re, s_re) - jnp.einsum('ghp,blgp->blgh', c_im, s_im)
    last = 0 if reverse else L - 1
    return y, s_re[:, last], s_im[:, last]


def s5_mixer(u, lp, s0_re, s0_im):
    f32 = jnp.float32
    B_, L, _ = u.shape
    uf = u.astype(f32).reshape(B_, L, SSM_GROUPS, SSM_GROUP_CH)
    y = lp['ssm_d'].astype(f32).reshape(SSM_GROUPS, SSM_GROUP_CH) * uf
    fin_re, fin_im = [], []
    for d, rev in enumerate((False, True)):
        yd, fr, fi = s5_direction(
            uf, lp['ssm_lam_re'][d].astype(f32), lp['ssm_lam_im'][d].astype(f32),
            lp['ssm_log_dt'][d].astype(f32), lp['ssm_b_re'][d].astype(f32), lp['ssm_b_im'][d].astype(f32),
            lp['ssm_c_re'][d].astype(f32), lp['ssm_c_im'][d].astype(f32),
            None if s0_re is None else s0_re[:, d].astype(f32),
            None if s0_im is None else s0_im[:, d].astype(f32), rev)
        y = y + yd
        fin_re.append(fr)
        fin_im.append(fi)
    y = jax.nn.gelu(y.reshape(B_, L, SSM_CH))
    y = y * jax.nn.sigmoid(y @ lp['ssm_w_glu'].astype(f32) + lp['ssm_b_glu'].astype(f32))
    return y.astype(u.dtype), jnp.stack(fin_re, axis=1), jnp.stack(fin_im, axis=1)


def trunk_layer(x, cond, lp, ctx):
    B_, L, _ = x.shape
    shift1, scale1, gate1, shift2, scale2, gate2 = adaln(cond, lp['w_ada'], lp['b_ada'])
    h = rms_norm(x, lp['norm1_g']) * (1.0 + scale1) + shift1
    gq, gk, gv, cq, ckv, kr, u = jnp.split(h @ lp['w_in'], IN_OFFSETS, axis=-1)
    q_a = rms_norm(gq.reshape(B_, L, GQA_HEADS, GQA_HEAD_DIM), lp['gqa_q_norm'])
    k_a = rms_norm(gk.reshape(B_, L, GQA_KV_HEADS, GQA_HEAD_DIM), lp['gqa_k_norm'])
    v_a = gv.reshape(B_, L, GQA_KV_HEADS, GQA_HEAD_DIM)
    q_b = (rms_norm(cq, lp['mla_q_norm']) @ lp['mla_w_uq']).reshape(B_, L, MLA_HEADS, MLA_QK_DIM)
    ckv = rms_norm(ckv, lp['mla_kv_norm'])
    ctx_out = (k_a, v_a, ckv, kr)
    if ctx is None:
        keys_a, vals_a, ckv_all, kr_all = k_a, v_a, ckv, kr
        s0_re = s0_im = None
    else:
        ck, cv, cckv, ckr, s0_re, s0_im = ctx
        cos_a, sin_a = axial_rope(L, GQA_HEAD_DIM)
        cos_b, sin_b = axial_rope(L, MLA_ROPE_DIM)
        q_a = apply_rope(q_a, cos_a, sin_a)
        keys_a = jnp.concatenate([apply_rope(k_a, cos_a, sin_a), ck], axis=1)
        vals_a = jnp.concatenate([v_a, cv], axis=1)
        q_b = jnp.concatenate([q_b[..., :MLA_NOPE_DIM],
                               apply_rope(q_b[..., MLA_NOPE_DIM:], cos_b, sin_b)], axis=-1)
        kr_lat = apply_rope(kr[:, :, None, :], cos_b, sin_b)[:, :, 0, :]
        ckv_all = jnp.concatenate([ckv, cckv], axis=1)
        kr_all = jnp.concatenate([kr_lat, ckr], axis=1)
    Lk = keys_a.shape[1]
    o_a = block_attention(q_a.reshape(B_, L, GQA_KV_HEADS, GQA_REP, GQA_HEAD_DIM),
                          keys_a, vals_a, GQA_HEAD_DIM ** -0.5)
    k_nope = (ckv_all @ lp['mla_w_uk']).reshape(B_, Lk, MLA_HEADS, MLA_NOPE_DIM)
    v_b = (ckv_all @ lp['mla_w_uv']).reshape(B_, Lk, MLA_HEADS, MLA_V_DIM)
    k_b = jnp.concatenate([k_nope, jnp.broadcast_to(kr_all[:, :, None, :],
                                                    (B_, Lk, MLA_HEADS, MLA_ROPE_DIM))], axis=-1)
    o_b = block_attention(q_b[:, :, :, None, :], k_b, v_b, MLA_QK_DIM ** -0.5)
    o_c, s_re, s_im = s5_mixer(u, lp, s0_re, s0_im)
    mix = jnp.concatenate([o_a.reshape(B_, L, GQA_OUT), o_b.reshape(B_, L, MLA_OUT), o_c], axis=-1)
    x = x + gate1 * (mix @ lp['w_out'])
    h2 = rms_norm(x, lp['norm2_g']) * (1.0 + scale2) + shift2
    g, up = jnp.split(h2 @ lp['w_ffn_in'], 2, axis=-1)
    x = x + gate2 * ((jax.nn.silu(g) * up) @ lp['w_ffn_out'])
    return x, ctx_out + (s_re, s_im)


def setup_inputs(seed: int = 0) -> dict:
    key = jax.random.key(seed)
    ks = jax.random.split(key, 36)
    f32 = jnp.float32

    def nrm(i, shape, scale=1.0):
        return jax.random.normal(ks[i], shape, f32) * scale

    G, H, P = SSM_GROUPS, SSM_GROUP_CH, SSM_STATE
    lam_im0 = math.pi * jnp.arange(P, dtype=f32)
    return {
        'x_prompt': nrm(0, (BATCH, SEQ, D_MODEL)),
        'x_sample': nrm(1, (DEC_BATCH, DEC_SEQ, D_MODEL)),
        'cache_gqa_k': nrm(2, (DEC_BATCH, DEPTH, PAST_LEN, GQA_KV_HEADS, GQA_HEAD_DIM)),
        'cache_gqa_v': nrm(3, (DEC_BATCH, DEPTH, PAST_LEN, GQA_KV_HEADS, GQA_HEAD_DIM)),
        'cache_mla_ckv': nrm(4, (DEC_BATCH, DEPTH, PAST_LEN, MLA_KV_RANK)),
        'cache_mla_krope': nrm(5, (DEC_BATCH, DEPTH, PAST_LEN, MLA_ROPE_DIM)),
        'state_ssm_re': nrm(6, (DEC_BATCH, DEPTH, 2, G, P), 0.5),
        'state_ssm_im': nrm(7, (DEC_BATCH, DEPTH, 2, G, P), 0.5),
        'c': nrm(8, (DEC_BATCH, D_MODEL)),
        'c_ctx': nrm(9, (D_MODEL,)),
        'norm1_g': 1.0 + nrm(10, (DEPTH, D_MODEL), 0.02),
        'norm2_g': 1.0 + nrm(11, (DEPTH, D_MODEL), 0.02),
        'w_ada': nrm(12, (DEPTH, D_MODEL, 6 * D_MODEL), 0.5 * D_MODEL ** -0.5),
        'b_ada': nrm(13, (DEPTH, 6 * D_MODEL), 0.01),
        'w_in': nrm(14, (DEPTH, D_MODEL, IN_WIDTH), D_MODEL ** -0.5),
        'gqa_q_norm': 1.0 + nrm(15, (DEPTH, GQA_HEAD_DIM), 0.02),
        'gqa_k_norm': 1.0 + nrm(16, (DEPTH, GQA_HEAD_DIM), 0.02),
        'mla_q_norm': 1.0 + nrm(17, (DEPTH, MLA_Q_RANK), 0.02),
        'mla_kv_norm': 1.0 + nrm(18, (DEPTH, MLA_KV_RANK), 0.02),
        'mla_w_uq': nrm(19, (DEPTH, MLA_Q_RANK, MLA_HEADS * MLA_QK_DIM), MLA_Q_RANK ** -0.5),
        'mla_w_uk': nrm(20, (DEPTH, MLA_KV_RANK, MLA_HEADS * MLA_NOPE_DIM), MLA_KV_RANK ** -0.5),
        'mla_w_uv': nrm(21, (DEPTH, MLA_KV_RANK, MLA_HEADS * MLA_V_DIM), MLA_KV_RANK ** -0.5),
        'ssm_lam_re': -0.5 + nrm(22, (DEPTH, 2, G, P), 0.01),
        'ssm_lam_im': lam_im0 + nrm(23, (DEPTH, 2, G, P), 0.01),
        'ssm_log_dt': jax.random.uniform(ks[24], (DEPTH, 2, G), f32, math.log(1e-3), math.log(1e-1)),
        'ssm_b_re': nrm(25, (DEPTH, 2, G, P, H), (2 * H) ** -0.5),
        'ssm_b_im': nrm(26, (DEPTH, 2, G, P, H), (2 * H) ** -0.5),
        'ssm_c_re': nrm(27, (DEPTH, 2, G, H, P), (2 * P) ** -0.5),
        'ssm_c_im': nrm(28, (DEPTH, 2, G, H, P), (2 * P) ** -0.5),
        'ssm_d': nrm(29, (DEPTH, SSM_CH)),
        'ssm_w_glu': nrm(30, (DEPTH, SSM_CH, SSM_CH), SSM_CH ** -0.5),
        'ssm_b_glu': nrm(31, (DEPTH, SSM_CH), 0.01),
        'w_out': nrm(32, (DEPTH, MIX_WIDTH, D_MODEL), MIX_WIDTH ** -0.5),
        'w_ffn_in': nrm(33, (DEPTH, D_MODEL, 2 * D_FF), D_MODEL ** -0.5),
        'w_ffn_out': nrm(34, (DEPTH, D_FF, D_MODEL), D_FF ** -0.5),
        'final_g': 1.0 + nrm(35, (D_MODEL,), 0.02),
    }


def reference(x_prompt, x_sample, cache_gqa_k, cache_gqa_v, cache_mla_ckv, cache_mla_krope,
              state_ssm_re, state_ssm_im, c, c_ctx, norm1_g, norm2_g, w_ada, b_ada, w_in,
              gqa_q_norm, gqa_k_norm, mla_q_norm, mla_kv_norm, mla_w_uq, mla_w_uk, mla_w_uv,
              ssm_lam_re, ssm_lam_im, ssm_log_dt, ssm_b_re, ssm_b_im, ssm_c_re, ssm_c_im,
              ssm_d, ssm_w_glu, ssm_b_glu, w_out, w_ffn_in, w_ffn_out, final_g):
    xp, xs = x_prompt, x_sample
    ks_, vs_, ckvs_, krs_, sres_, sims_ = [], [], [], [], [], []
    for l in range(DEPTH):
        lp = {
            'norm1_g': norm1_g[l], 'norm2_g': norm2_g[l], 'w_ada': w_ada[l], 'b_ada': b_ada[l],
            'w_in': w_in[l], 'gqa_q_norm': gqa_q_norm[l], 'gqa_k_norm': gqa_k_norm[l],
            'mla_q_norm': mla_q_norm[l], 'mla_kv_norm': mla_kv_norm[l], 'mla_w_uq': mla_w_uq[l],
            'mla_w_uk': mla_w_uk[l], 'mla_w_uv': mla_w_uv[l], 'ssm_lam_re': ssm_lam_re[l],
            'ssm_lam_im': ssm_lam_im[l], 'ssm_log_dt': ssm_log_dt[l], 'ssm_b_re': ssm_b_re[l],
            'ssm_b_im': ssm_b_im[l], 'ssm_c_re': ssm_c_re[l], 'ssm_c_im': ssm_c_im[l],
            'ssm_d': ssm_d[l], 'ssm_w_glu': ssm_w_glu[l], 'ssm_b_glu': ssm_b_glu[l],
            'w_out': w_out[l], 'w_ffn_in': w_ffn_in[l], 'w_ffn_out': w_ffn_out[l],
        }
        xp, (k_c, v_c, ckv_c, kr_c, sr_c, si_c) = trunk_layer(xp, c_ctx, lp, None)
        ks_.append(k_c)
        vs_.append(v_c)
        ckvs_.append(ckv_c)
        krs_.append(kr_c)
        sres_.append(sr_c)
        sims_.append(si_c)
        ctx = (cache_gqa_k[:, l], cache_gqa_v[:, l], cache_mla_ckv[:, l], cache_mla_krope[:, l],
               state_ssm_re[:, l], state_ssm_im[:, l])
        xs, _ = trunk_layer(xs, c, lp, ctx)
    y_prompt = rms_norm(xp, final_g)
    y_sample = rms_norm(xs, final_g)
    new_gqa_k = jnp.stack(ks_, axis=1)
    new_gqa_v = jnp.stack(vs_, axis=1)
    new_mla_ckv = jnp.stack(ckvs_, axis=1)
    new_mla_krope = jnp.stack(krs_, axis=1)
    new_ssm_re = jnp.stack(sres_, axis=1)
    new_ssm_im = jnp.stack(sims_, axis=1)
    return (y_prompt, y_sample, new_gqa_k, new_gqa_v, new_mla_ckv, new_mla_krope, new_ssm_re, new_ssm_im)
```

```python
import math
import os
import numpy as np
import ml_dtypes
import concourse.bass as bass
import concourse.mybir as mybir
from concourse.bass_utils import run_bass_kernel_spmd
from concourse.ap import AP

F32 = mybir.dt.float32
BF16 = mybir.dt.bfloat16
ALU = mybir.AluOpType
AF = mybir.ActivationFunctionType

D = 1024
KT = 8
LP = 256
PAST = 512
DFF = 2816
FT = 22
EPS = 1e-6
INW = 1312
NQ = 16


def prod(s):
    r = 1
    for v in s:
        r *= int(v)
    return r


def small_layout(NL):
    off = 0
    L = {}

    def add(n, *shape):
        nonlocal off
        L[n] = (off, shape)
        off += prod(shape)

    add('n1g', NL, 8)
    add('n2g', NL, 8)
    add('bada', NL, 48)
    add('qn', NL)
    add('kn', NL)
    add('mqn', NL, 2)
    add('mkvn', NL)
    add('bglu', NL, 2)
    add('fing', 8)
    add('cond', 8, 2)
    add('lre', NL, 16)
    add('lim', NL, 16)
    add('ldt', NL, 16)
    add('s0re', NL, 16)
    add('s0im', NL, 16)
    add('drep', NL, 16)
    return L, off


class Buf:
    __slots__ = ("name", "w", "r")

    def __init__(self, name):
        self.name = name
        self.w = None
        self.r = {}


class T:
    __slots__ = ("ap", "b")

    def __init__(self, ap, b):
        self.ap = ap
        self.b = b


ENGS = ("pe", "act", "dve", "pool", "sp")
NDS = 40


class Sched:
    def __init__(self, nc, stack):
        self.nc = nc
        self.sem = {}
        self.cnt = {}
        self.q = {}
        self.waited = {}
        for e in ENGS:
            self.sem[("e", e)] = stack.enter_context(nc.semaphore("sem_" + e))
            self.cnt[e] = 0
            self.q[e] = []
            self.waited[e] = {}
        self.dcnt = {}
        self.dnext = {}
        for qn in ("sp", "pool"):
            self.dnext[qn] = 0
            for i in range(NDS):
                self.sem[("d", qn, i)] = stack.enter_context(nc.semaphore("d%s%d" % (qn, i)))
                self.dcnt[(qn, i)] = 0
        self.ninstr = 0

    def _wait(self, eng, key, val):
        if self.waited[eng].get(key, 0) >= val:
            return
        if key == ("e", eng) and (val > self.cnt[eng] or eng == "pe"):
            return
        sem = self.sem[key]
        self.q[eng].append(lambda e, sem=sem, val=val: e.wait_ge(sem, val))
        self.waited[eng][key] = val
        self.ninstr += 1

    def _deps(self, eng, R, W):
        for b in R:
            if b.w is not None:
                self._wait(eng, b.w[0], b.w[1])
        for b in W:
            if b.w is not None:
                self._wait(eng, b.w[0], b.w[1])
            for k, v in b.r.items():
                self._wait(eng, k, v)

    def op(self, eng, fn, R=(), W=(), inc=True):
        self._deps(eng, R, W)
        key = ("e", eng)
        if inc:
            self.cnt[eng] += 1
            val = self.cnt[eng]
        else:
            val = self.cnt[eng] + 1
        for b in R:
            if b.r.get(key, 0) < val:
                b.r[key] = val
        for b in W:
            b.w = (key, val)
            b.r = {}
        sem = self.sem[key]
        if inc:
            self.q[eng].append(lambda e, fn=fn, sem=sem: fn(e).then_inc(sem, 1))
        else:
            self.q[eng].append(lambda e, fn=fn: fn(e))
        self.ninstr += 1

    def dma(self, qn, out, in_, R=(), W=(), **kw):
        i = self.dnext[qn]
        self.dnext[qn] = (i + 1) % NDS
        key = ("d", qn, i)
        prev = self.dcnt[(qn, i)]
        if prev > 0:
            self._wait(qn, key, prev)
        self._deps(qn, R, W)
        self.dcnt[(qn, i)] = prev + 16
        val = prev + 16
        for b in R:
            if b.r.get(key, 0) < val:
                b.r[key] = val
        for b in W:
            b.w = (key, val)
            b.r = {}
        sem = self.sem[key]
        self.q[qn].append(lambda e, out=out, in_=in_, sem=sem, kw=kw: e.dma_start(out=out, in_=in_, **kw).then_inc(sem, 16))
        self.ninstr += 1

    def barrier(self):
        for e in ENGS:
            if e != "sp":
                self._wait("sp", ("e", e), self.cnt[e])
        for (qn, i), v in self.dcnt.items():
            if v > 0:
                self._wait("sp", ("d", qn, i), v)
        self.cnt["sp"] += 1
        val = self.cnt["sp"]
        sem = self.sem[("e", "sp")]
        self.q["sp"].append(lambda e, sem=sem: e.nop().then_inc(sem, 1))
        for e in ENGS:
            if e != "sp":
                self._wait(e, ("e", "sp"), val)

    def emit(self, block):
        nc = self.nc

        def run(e, lst):
            for f in lst:
                f(e)

        block.tensor(lambda e: run(e, self.q["pe"]))
        block.scalar(lambda e: run(e, self.q["act"]))
        block.vector(lambda e: run(e, self.q["dve"]))
        block.gpsimd(lambda e: run(e, self.q["pool"]))
        block.sync(lambda e: run(e, self.q["sp"]))


class Arena:
    def __init__(self, nc, nbytes):
        self.t = nc.alloc_sbuf_tensor("arena", [128, nbytes // 2], BF16)
        self.off = 0
        self.cap = nbytes
        self.n = 0
        self.peak = 0

    def alloc(self, free_shape, dtype, name=None):
        esz = 4 if dtype == F32 else 2
        n = prod(free_shape)
        nbytes = n * esz
        self.off = (self.off + 63) // 64 * 64
        o = self.off
        self.off += nbytes
        self.peak = max(self.peak, self.off)
        assert self.off <= self.cap, "SBUF arena overflow: %d > %d (%s)" % (self.off, self.cap, name)
        ap = self.t[:, o // 2: o // 2 + nbytes // 2]
        if dtype == F32:
            ap = ap.bitcast(F32)
        if len(free_shape) > 1:
            names = ["d%d" % i for i in range(len(free_shape))]
            kw = {nm: int(s) for nm, s in zip(names, free_shape)}
            ap = ap.rearrange("p (%s) -> p %s" % (" ".join(names), " ".join(names)), **kw)
        self.n += 1
        return T(ap, Buf(name or ("t%d" % self.n)))

    def mark(self):
        return self.off

    def reset(self, m):
        self.off = m


def fv(ap, pattern, off=0):
    return AP(tensor=ap.tensor, offset=ap.offset + off, ap=[list(ap.ap[0])] + [list(p) for p in pattern])


class Prog:
    def __init__(self, LS, NL, dbg=()):
        self.LS = LS
        self.NL = NL
        self.NT = LS + 2 * LP
        self.NTILE = self.NT // 512
        self.NS = LS // 512
        self.NKEY = LS + PAST + 2 * LP
        self.NB = self.NT // 8
        self.NBS = LS // 8
        self.dbg = set(dbg)
        self.SL, self.NSP = small_layout(NL)

    def declare(self, nc):
        NL, NT, LS, NKEY = self.NL, self.NT, self.LS, self.NKEY

        def inp(name, shape, dt=F32):
            return nc.dram_tensor(name, list(shape), dt, kind="ExternalInput").ap()

        def outp(name, shape, dt=F32):
            return nc.dram_tensor(name, list(shape), dt, kind="ExternalOutput").ap()

        def scr(name, shape, dt):
            kind = "ExternalOutput" if name in self.dbg else "Internal"
            return nc.dram_tensor(name, list(shape), dt, kind=kind).ap()

        d = {}
        d['xT'] = inp('xT', [D, NT])
        d['smallp'] = inp('smallp', [128, self.NSP])
        d['ssmbc'] = inp('ssmbc', [NL, 128, 4, 256])
        d['cbf'] = inp('cbf', [128, 640], BF16)
        d['ropeG'] = inp('ropeG', [128, 2, LS])
        d['ropeM'] = inp('ropeM', [128, 2, LS])
        d['cK'] = inp('cK', [NL, 128, PAST])
        d['cV'] = inp('cV', [NL, PAST, 128])
        d['cCKV'] = inp('cCKV', [NL, 128, PAST])
        d['cKR'] = inp('cKR', [NL, 32, PAST])
        d['w_ada'] = inp('w_ada', [NL, D, 6 * D])
        d['w_in'] = inp('w_in', [NL, D, INW])
        d['w_uq'] = inp('w_uq', [NL, 256, 576])
        d['w_uk'] = inp('w_uk', [NL, 128, 384])
        d['w_uv'] = inp('w_uv', [NL, 128, 384])
        d['w_glu'] = inp('w_glu', [NL, 256, 256])
        d['w_out'] = inp('w_out', [NL, D, D])
        d['w_f1'] = inp('w_f1', [NL, D, 2 * DFF])
        d['w_f2'] = inp('w_f2', [NL, DFF, D])
        d['yT'] = outp('yT', [D, NT])
        d['o_k'] = outp('o_k', [NL, 128, 512])
        d['o_v'] = outp('o_v', [NL, 512, 128])
        d['o_ckv'] = outp('o_ckv', [NL, 128, 512])
        d['o_kr'] = outp('o_kr', [NL, 32, 512])
        d['o_ss'] = outp('o_ss', [NL, 128, 2, 2, NQ])
        d['qa'] = scr('qa', [3, 128, NT], BF16)
        d['ka'] = scr('ka', [128, NKEY], BF16)
        d['va'] = scr('va', [NT, 128], BF16)
        d['qb'] = scr('qb', [6, 96, NT], BF16)
        d['kb'] = scr('kb', [6, 96, NKEY], BF16)
        d['vb'] = scr('vb', [NKEY, 384], BF16)
        d['mix'] = scr('mix', [8, 128, NT], BF16)
        d['xm'] = scr('xm', [D, NT], F32)
        d['xs0'] = scr('xs0', [D, NT], F32)
        d['xs1'] = scr('xs1', [D, NT], F32)
        d['w1s'] = scr('w1s', [FT // 2, 128, 8, 2, 256], BF16)
        d['w2s'] = scr('w2s', [8, 128, FT, 128], BF16)
        self.d = d

    def mm(self, out, lhsT, rhs, start, stop, R, W, inc=True):
        self.S.op("pe", lambda e: e.matmul(out, lhsT, rhs, start=start, stop=stop), R, W, inc)

    def tr(self, out, in_, ident, R, W, inc=True):
        self.S.op("pe", lambda e: e.transpose(out, in_, ident), R, W, inc)

    def act(self, out, in_, func, R, W, bias=None, scale=None):
        kw = {}
        if bias is not None:
            kw['bias'] = bias
        if scale is not None:
            kw['scale'] = scale
        self.S.op("act", lambda e: e.activation(out, in_, func, **kw), R, W)

    def tt(self, out, in0, in1, op, R, W, eng="dve"):
        self.S.op(eng, lambda e: e.tensor_tensor(out, in0, in1, op), R, W)

    def ts(self, out, in0, s1, s2, op0, op1, R, W, eng="dve"):
        if s2 is None:
            self.S.op(eng, lambda e: e.tensor_scalar(out, in0, s1, None, op0), R, W)
        else:
            self.S.op(eng, lambda e: e.tensor_scalar(out, in0, s1, s2, op0, op1), R, W)

    def stt(self, out, in0, scalar, in1, op0, op1, R, W):
        self.S.op("dve", lambda e: e.scalar_tensor_tensor(out, in0, scalar, in1, op0, op1), R, W)

    def cp(self, out, in_, R, W, eng="dve"):
        if eng == "act":
            self.S.op("act", lambda e: e.activation(out, in_, AF.Copy), R, W)
        else:
            self.S.op(eng, lambda e: e.tensor_copy(out, in_), R, W)

    def recip(self, out, in_, R, W):
        self.S.op("dve", lambda e: e.reciprocal(out, in_), R, W)

    def memset(self, out, val, W, eng="dve"):
        self.S.op(eng, lambda e: e.memset(out, val), (), W)

    def ld(self, out_t, in_ap, R=(), q="sp", out_ap=None, **kw):
        self.S.dma(q, out_t.ap if out_ap is None else out_ap, in_ap, R=R, W=[out_t.b], **kw)

    def st(self, out_ap, in_t, W=(), q="pool", in_ap=None, **kw):
        self.S.dma(q, out_ap, in_t.ap if in_ap is None else in_ap, R=[in_t.b], W=W, **kw)

    def psn(self):
        t = self.PS[self.psi % 8]
        self.psi += 1
        return t

    def sm(self, name):
        off, shape = self.SL[name]
        ap = self.SM.ap[:, off:off + prod(shape)]
        if len(shape) > 1:
            names = ["d%d" % i for i in range(len(shape))]
            kw = {nm: int(s) for nm, s in zip(names, shape)}
            ap = ap.rearrange("p (%s) -> p %s" % (" ".join(names), " ".join(names)), **kw)
        return ap

    def build(self):
        from contextlib import ExitStack
        nc = bass.Bass("TRN2", target_bir_lowering=False)
        self.nc = nc
        self.declare(nc)
        stack = ExitStack()
        with stack:
            self.S = Sched(nc, stack)
            self.A = Arena(nc, 206000)
            self.PS = []
            self.PST = []
            for i in range(4):
                pt = nc.alloc_psum_tensor("psum%d" % i, [128, 1024], F32)
                self.PST.append(pt[:, :])
                for j in range(2):
                    self.PS.append(T(pt[:, j * 512:(j + 1) * 512], Buf("ps%d" % (2 * i + j))))
            self.psi = 0
            self.dbufs = {}
            self.body()
            self.S.barrier()
            block = stack.enter_context(nc.Block())
            self.S.emit(block)
        return nc

    def dbuf(self, name):
        return Buf(name)

    def body(self):
        A, S, d = self.A, self.S, self.d
        NL = self.NL
        self.SM = A.alloc([self.NSP], F32, "SM")
        self.CB = A.alloc([640], BF16, "CB")
        self.MOD = A.alloc([NL, 6, 8, 2], F32, "MOD")
        self.ld(self.SM, d['smallp'])
        self.ld(self.CB, d['cbf'])
        cb = self.CB.ap
        self.ident = cb[:, 0:128]
        self.ones = cb[:, 128:256]
        self.bd2 = cb[:, 256:384]
        self.rotG = cb[:, 384:512]
        self.rotM = cb[:, 512:640]
        self.eps = A.alloc([1], F32, "eps")
        self.memset(self.eps.ap, EPS, [self.eps.b])
        self.phase0()
        S.barrier()
        xcur = d['xT']
        for l in range(NL):
            self.l = l
            last = (l == NL - 1)
            xnext = d['yT'] if last else d['xs%d' % (l % 2)]
            m0 = A.mark()
            self.UT = A.alloc([16, self.NB], BF16, "Utilde")
            mU = A.mark()
            self.phaseP(xcur)
            S.barrier()
            A.reset(mU)
            if 'stopP' in self.dbg:
                return
            self.phaseS()
            S.barrier()
            A.reset(m0)
            if 'stopS' in self.dbg:
                return
            self.phaseGA()
            S.barrier()
            A.reset(m0)
            self.phaseMA()
            S.barrier()
            A.reset(m0)
            if 'stopA' in self.dbg:
                return
            self.phaseO(xcur)
            S.barrier()
            A.reset(m0)
            self.phaseF(xnext, last)
            S.barrier()
            A.reset(m0)
            xcur = xnext

    def phase0(self):
        A, S, d = self.A, self.S, self.d
        NL = self.NL
        m0 = A.mark()
        silc = A.alloc([16], BF16, "silc")
        condf = self.sm('cond')
        self.act(silc.ap, fv(condf, [[1, 16]]), AF.Silu, [self.SM.b], [silc.b])
        WA = [A.alloc([8, 1024], BF16, "wa%d" % i) for i in range(2)]
        MOD = self.MOD
        bada = self.sm('bada')
        for l in range(NL):
            wv = d['w_ada'][l].rearrange("(kt p) n -> p kt n", p=128)
            for j in range(6):
                wb = WA[(l * 6 + j) % 2]
                self.ld(wb, wv[:, :, j * 1024:(j + 1) * 1024], q="pool")
                ps = self.psn()
                for mt in range(8):
                    for kt in range(8):
                        self.mm(ps.ap[:, mt * 2:mt * 2 + 2], wb.ap[:, kt, mt * 128:(mt + 1) * 128],
                                silc.ap[:, kt * 2:kt * 2 + 2], kt == 0, kt == 7,
                                [wb.b, silc.b], [ps.b], inc=(mt == 7 and kt == 7))
                bsl = bada[:, l, j * 8:(j + 1) * 8]
                self.tt(MOD.ap[:, l, j], fv(ps.ap, [[2, 8], [1, 2]]), fv(bsl, [[1, 8], [0, 2]]), ALU.add,
                        [ps.b, self.SM.b], [MOD.b])
        n1g = self.sm('n1g')
        n2g = self.sm('n2g')
        for l in range(NL):
            self.stt(MOD.ap[:, l, 1], MOD.ap[:, l, 1], 1.0, fv(n1g[:, l, :], [[1, 8], [0, 2]]), ALU.add, ALU.mult,
                     [MOD.b, self.SM.b], [MOD.b])
            self.stt(MOD.ap[:, l, 4], MOD.ap[:, l, 4], 1.0, fv(n2g[:, l, :], [[1, 8], [0, 2]]), ALU.add, ALU.mult,
                     [MOD.b, self.SM.b], [MOD.b])
        if 'MOD' in self.dbg:
            dm = self.nc.dram_tensor('dbgMOD', [128, NL * 96], F32, kind="ExternalOutput").ap()
            self.st(dm, MOD, in_ap=fv(MOD.ap, [[1, NL * 96]]))
        S.barrier()
        A.reset(m0)

    def norm_tile(self, xt, h, jS, jB, c, sq, rstd, tmp):
        l = self.l
        MOD = self.MOD
        self.act(sq.ap, xt.ap, AF.Square, [xt.b], [sq.b])
        ps = self.psn()
        for kt in range(8):
            self.mm(ps.ap, self.ones, sq.ap[:, kt, :], kt == 0, kt == 7, [sq.b, self.CB.b], [ps.b], inc=(kt == 7))
        self.act(rstd.ap, ps.ap, AF.Sqrt, [ps.b, self.eps.b], [rstd.b], bias=self.eps.ap, scale=1.0 / D)
        self.recip(rstd.ap, rstd.ap, [rstd.b], [rstd.b])
        self.tt(tmp.ap, xt.ap, fv(rstd.ap, [[0, 8], [1, 512]]), ALU.mult, [xt.b, rstd.b], [tmp.b])
        for kt in range(8):
            self.act(h.ap[:, kt, :], tmp.ap[:, kt, :], AF.Identity, [tmp.b, MOD.b], [h.b],
                     bias=MOD.ap[:, l, jB, kt, c:c + 1], scale=MOD.ap[:, l, jS, kt, c:c + 1])

    def headnorm(self, ps, onesmat, inv_n, gain_ap, out_f32, sq, rstd):
        self.act(sq.ap, ps.ap, AF.Square, [ps.b], [sq.b])
        ps2 = self.psn()
        self.mm(ps2.ap, onesmat, sq.ap, True, True, [sq.b, self.CB.b], [ps2.b])
        self.act(rstd.ap, ps2.ap, AF.Sqrt, [ps2.b, self.eps.b], [rstd.b], bias=self.eps.ap, scale=inv_n)
        self.recip(rstd.ap, rstd.ap, [rstd.b], [rstd.b])
        self.stt(out_f32.ap, ps.ap, gain_ap, rstd.ap, ALU.mult, ALU.mult, [ps.b, rstd.b, self.SM.b], [out_f32.b])

    def rope(self, qn, rotmat, cos_ap, sin_ap, out_bf, qnb, t1, t2, rows=128):
        r = slice(0, rows)
        self.cp(qnb.ap[r], qn.ap[r], [qn.b], [qnb.b], eng="act")
        ps = self.psn()
        self.mm(ps.ap[r], rotmat[r, 0:rows], qnb.ap[r], True, True, [qnb.b, self.CB.b], [ps.b])
        self.tt(t1.ap[r], qn.ap[r], cos_ap[r], ALU.mult, [qn.b, self.RT.b], [t1.b], eng="pool")
        self.tt(t2.ap[r], ps.ap[r], sin_ap[r], ALU.mult, [ps.b, self.RT.b], [t2.b])
        self.tt(out_bf.ap[r], t1.ap[r], t2.ap[r], ALU.add, [t1.b, t2.b], [out_bf.b])

    def phaseP(self, xcur):
        A, S, d = self.A, self.S, self.d
        l, LS, NS, NT = self.l, self.LS, self.NS, self.NT
        xv = xcur.rearrange("(kt p) n -> p kt n", p=128)
        WIN = A.alloc([8, INW], BF16, "WIN")
        wv = d['w_in'][l].rearrange("(kt p) n -> p kt n", p=128)
        for i in range(3):
            for s, hh in enumerate((i, i + 3)):
                self.ld(WIN, wv[:, :, hh * 64:(hh + 1) * 64], q="pool", out_ap=WIN.ap[:, :, i * 128 + s * 64:i * 128 + (s + 1) * 64])
        self.ld(WIN, wv[:, :, 384:INW], q="pool", out_ap=WIN.ap[:, :, 384:INW])
        WUQN = A.alloc([2, 384], BF16, "WUQN")
        WUQR = A.alloc([2, 192], BF16, "WUQR")
        uqv = d['w_uq'][l].rearrange("(i p) (h e) -> p i h e", p=128, e=96)
        for i in range(2):
            self.ld(WUQN, uqv[:, i, :, 0:64], q="pool", out_ap=fv(WUQN.ap[:, i, :], [[64, 6], [1, 64]]))
            self.ld(WUQR, uqv[:, i, :, 64:96], q="pool", out_ap=fv(WUQR.ap[:, i, :], [[32, 6], [1, 32]]))
        WUK = A.alloc([384], BF16, "WUK")
        WUV = A.alloc([384], BF16, "WUV")
        self.ld(WUK, d['w_uk'][l], q="pool")
        self.ld(WUV, d['w_uv'][l], q="pool")
        XT = [A.alloc([8, 512], F32, "xt%d" % i) for i in range(2)]
        H = [A.alloc([8, 512], BF16, "h%d" % i) for i in range(2)]
        sq8 = A.alloc([8, 512], BF16, "sq8")
        tmp8 = A.alloc([8, 512], F32, "tmp8")
        rstd = A.alloc([512], F32, "rstd")
        self.RT = A.alloc([2, 2, 512], F32, "ropetab")
        sqh = [A.alloc([512], BF16, "sqh%d" % i) for i in range(2)]
        rsh = [A.alloc([512], F32, "rsh%d" % i) for i in range(2)]
        qn = [A.alloc([512], F32, "qn%d" % i) for i in range(2)]
        qnb = [A.alloc([512], BF16, "qnb%d" % i) for i in range(2)]
        t1 = [A.alloc([512], F32, "t1%d" % i) for i in range(2)]
        t2 = [A.alloc([512], F32, "t2%d" % i) for i in range(2)]
        ob = [A.alloc([512], BF16, "ob%d" % i) for i in range(4)]
        vt = A.alloc([4, 128], BF16, "vt")
        vtf = A.alloc([4, 128], F32, "vtf")
        cqn = A.alloc([2, 512], BF16, "cqn")
        sq2 = A.alloc([2, 512], BF16, "sq2")
        ckvb = A.alloc([512], BF16, "ckvb")
        vbt = A.alloc([4, 384], BF16, "vbt")
        utm = A.alloc([16, 8, 16], BF16, "utm")
        self.rr = 0

        def nxt(lst):
            self.rr += 1
            return lst[self.rr % len(lst)]

        for t in range(self.NTILE):
            prm = (t == NS)
            c = 1 if prm else 0
            tok = slice(t * 512, (t + 1) * 512)
            kcol = slice(t * 512, (t + 1) * 512) if not prm else slice(LS + PAST, LS + PAST + 512)
            xt = XT[t % 2]
            h = H[t % 2]
            self.ld(xt, xv[:, :, tok])
            if not prm:
                self.ld(self.RT, d['ropeG'][:, :, tok], out_ap=self.RT.ap[:, 0])
                self.ld(self.RT, d['ropeM'][:, :, tok], out_ap=self.RT.ap[:, 1])
            self.norm_tile(xt, h, 1, 0, c, sq8, rstd, tmp8)

            def proj(col0, ncol, rows=128):
                ps = self.psn()
                for kt in range(8):
                    self.mm(ps.ap[0:rows], WIN.ap[:, kt, col0:col0 + ncol], h.ap[:, kt, :], kt == 0, kt == 7,
                            [WIN.b, h.b], [ps.b], inc=(kt == 7))
                return ps

            for i in range(4):
                isk = (i == 3)
                ps = proj(i * 128, 128)
                q_n = nxt(qn)
                gain = self.sm('kn' if isk else 'qn')[:, l:l + 1]
                self.headnorm(ps, self.bd2, 1.0 / 64, gain, q_n, nxt(sqh), nxt(rsh))
                o = nxt(ob)
                if prm:
                    self.cp(o.ap, q_n.ap, [q_n.b], [o.b], eng="act")
                    if isk:
                        self.st(d['o_k'][l], q_n, W=[self.dbuf('o_k')])
                else:
                    self.rope(q_n, self.rotG, self.RT.ap[:, 0, 0], self.RT.ap[:, 0, 1], o, nxt(qnb), nxt(t1), nxt(t2))
                if isk:
                    self.st(d['ka'][:, kcol], o, W=[self.dbuf('ka')])
                else:
                    self.st(d['qa'][i, :, tok], o, W=[self.dbuf('qa')])
            ps = self.psn()
            for s in range(4):
                for kt in range(8):
                    self.mm(ps.ap[:, s * 128:(s + 1) * 128], h.ap[:, kt, s * 128:(s + 1) * 128], WIN.ap[:, kt, 512:640],
                            kt == 0, kt == 7, [WIN.b, h.b], [ps.b], inc=(s == 3 and kt == 7))
            self.cp(fv(vt.ap, [[1, 512]]), ps.ap, [ps.b], [vt.b], eng="act")
            self.st(d['va'][tok, :].rearrange("(s p) c -> p s c", p=128), vt, W=[self.dbuf('va')])
            if prm:
                self.cp(fv(vtf.ap, [[1, 512]]), ps.ap, [ps.b], [vtf.b])
                self.st(d['o_v'][l].rearrange("(s p) c -> p s c", p=128), vtf, W=[self.dbuf('o_v')])
            psc = [proj(640, 128), proj(768, 128)]
            for i in range(2):
                self.act(sq2.ap[:, i, :], psc[i].ap, AF.Square, [psc[i].b], [sq2.b])
            ps2 = self.psn()
            for i in range(2):
                self.mm(ps2.ap, self.ones, sq2.ap[:, i, :], i == 0, i == 1, [sq2.b, self.CB.b], [ps2.b], inc=(i == 1))
            rs = nxt(rsh)
            self.act(rs.ap, ps2.ap, AF.Sqrt, [ps2.b, self.eps.b], [rs.b], bias=self.eps.ap, scale=1.0 / 256)
            self.recip(rs.ap, rs.ap, [rs.b], [rs.b])
            mqn = self.sm('mqn')
            for i in range(2):
                self.stt(cqn.ap[:, i, :], psc[i].ap, mqn[:, l, i:i + 1], rs.ap, ALU.mult, ALU.mult,
                         [psc[i].b, rs.b, self.SM.b], [cqn.b])
            for pr in range(3):
                ps = self.psn()
                for i in range(2):
                    self.mm(ps.ap, WUQN.ap[:, i, pr * 128:(pr + 1) * 128], cqn.ap[:, i, :], i == 0, i == 1,
                            [WUQN.b, cqn.b], [ps.b], inc=(i == 1))
                o = nxt(ob)
                self.cp(o.ap, ps.ap, [ps.b], [o.b], eng="act")
                for s in range(2):
                    self.st(d['qb'][2 * pr + s, 0:64, tok], o, W=[self.dbuf('qb')], in_ap=o.ap[s * 64:(s + 1) * 64])
            for (c0, nh) in ((0, 4), (128, 2)):
                rows = nh * 32
                ps = self.psn()
                for i in range(2):
                    self.mm(ps.ap[0:rows], WUQR.ap[:, i, c0:c0 + rows], cqn.ap[:, i, :], i == 0, i == 1,
                            [WUQR.b, cqn.b], [ps.b], inc=(i == 1))
                o = nxt(ob)
                if prm:
                    self.cp(o.ap[0:rows], ps.ap[0:rows], [ps.b], [o.b], eng="act")
                else:
                    q_n = nxt(qn)
                    self.cp(q_n.ap[0:rows], ps.ap[0:rows], [ps.b], [q_n.b])
                    self.rope(q_n, self.rotM, self.RT.ap[:, 1, 0], self.RT.ap[:, 1, 1], o, nxt(qnb), nxt(t1), nxt(t2), rows=rows)
                for s in range(nh):
                    hh = c0 // 32 + s
                    self.st(d['qb'][hh, 64:96, tok], o, W=[self.dbuf('qb')], in_ap=o.ap[s * 32:(s + 1) * 32])
            ps = proj(896, 128)
            ck = nxt(qn)
            self.headnorm(ps, self.ones, 1.0 / 128, self.sm('mkvn')[:, l:l + 1], ck, nxt(sqh), nxt(rsh))
            if prm:
                self.st(d['o_ckv'][l], ck, W=[self.dbuf('o_ckv')])
            self.cp(ckvb.ap, ck.ap, [ck.b], [ckvb.b], eng="act")
            self.mla_kv(ckvb, kcol, WUK, WUV, ob, vbt, nxt)
            ps = proj(1024, 32, rows=32)
            o = nxt(ob)
            if prm:
                q_n = nxt(qn)
                self.cp(q_n.ap[0:32], ps.ap[0:32], [ps.b], [q_n.b])
                self.st(d['o_kr'][l], q_n, W=[self.dbuf('o_kr')], in_ap=q_n.ap[0:32])
                self.cp(o.ap[0:32], q_n.ap[0:32], [q_n.b], [o.b], eng="act")
            else:
                q_n = nxt(qn)
                self.cp(q_n.ap[0:32], ps.ap[0:32], [ps.b], [q_n.b])
                self.rope(q_n, self.rotM, self.RT.ap[:, 1, 0], self.RT.ap[:, 1, 1], o, nxt(qnb), nxt(t1), nxt(t2), rows=32)
            for hh in range(6):
                self.st(d['kb'][hh, 64:96, kcol], o, W=[self.dbuf('kb')], in_ap=o.ap[0:32])
            for half in range(2):
                psu = [self.psn() for _ in range(2)]
                for jj in range(4):
                    j = half * 4 + jj
                    pst = psu[jj // 2]
                    for kt in range(8):
                        self.mm(pst.ap[0:64, (jj % 2) * 256:(jj % 2) * 256 + 256], fv(h.ap[:, kt, :], [[8, 64]], off=j),
                                WIN.ap[:, kt, 1056:1312], kt == 0, kt == 7, [WIN.b, h.b], [pst.b], inc=(kt == 7))
                    self.cp(utm.ap[0:64, :, j, :], fv(pst.ap[0:64], [[16, 16], [1, 16]], off=(jj % 2) * 256), [pst.b], [utm.b],
                            eng=("act" if jj % 2 else "dve"))
            pT = self.psn()
            pTb = pT.ap.bitcast(BF16)
            for g in range(16):
                self.tr(pTb[:, g * 64:(g + 1) * 64], fv(utm.ap[0:64, g], [[1, 128]]), self.ident[0:64, 0:64],
                        [utm.b, self.CB.b], [pT.b], inc=(g == 15))
            self.cp(self.UT.ap[:, :, t * 64:(t + 1) * 64], fv(pTb, [[64, 16], [1, 64]]), [pT.b], [self.UT.b])
        self.ld(ckvb, d['cCKV'][l], q="pool")
        kcol = slice(LS, LS + PAST)
        self.mla_kv(ckvb, kcol, WUK, WUV, ob, vbt, nxt)
        o = nxt(ob)
        self.ld(o, d['cKR'][l], q="pool", out_ap=o.ap[0:32])
        for hh in range(6):
            self.st(d['kb'][hh, 64:96, kcol], o, W=[self.dbuf('kb')], in_ap=o.ap[0:32])

    def mla_kv(self, ckvb, kcol, WUK, WUV, ob, vbt, nxt):
        d = self.d
        for pr in range(3):
            ps = self.psn()
            self.mm(ps.ap, WUK.ap[:, pr * 128:(pr + 1) * 128], ckvb.ap, True, True, [WUK.b, ckvb.b], [ps.b])
            o = nxt(ob)
            self.cp(o.ap, ps.ap, [ps.b], [o.b], eng="act")
            for s in range(2):
                self.st(d['kb'][2 * pr + s, 0:64, kcol], o, W=[self.dbuf('kb')], in_ap=o.ap[s * 64:(s + 1) * 64])
        for s in range(4):
            ps = self.psn()
            self.mm(ps.ap[:, 0:384], ckvb.ap[:, s * 128:(s + 1) * 128], WUV.ap, True, True, [WUV.b, ckvb.b], [ps.b])
            self.cp(vbt.ap[:, s, :], ps.ap[:, 0:384], [ps.b], [vbt.b], eng=("act" if s % 2 else "dve"))
        self.st(d['vb'][kcol, :].rearrange("(s p) c -> p s c", p=128), vbt, W=[self.dbuf('vb')])

    def phaseS(self):
        A, S, d = self.A, self.S, self.d
        l, LS, NS, NT, NB, NBS = self.l, self.LS, self.NS, self.NT, self.NB, self.NBS
        SM = self.SM
        UT = self.UT
        pb = Buf("s5prm")

        def sc(name, n=16):
            t = A.alloc([n], F32, name)
            t.b = pb
            return t
        lre = self.sm('lre')[:, l, :]
        lim = self.sm('lim')[:, l, :]
        ldt = self.sm('ldt')[:, l, :]
        Rp = [pb, SM.b]
        Wp = [pb]
        dt = sc("dt"); th = sc("th"); er = sc("er"); cc = sc("cc"); ss = sc("ss"); cs = sc("cs")
        c_ = sc("c"); s_ = sc("s"); dec = sc("dec"); are = sc("are"); aim = sc("aim")
        den = sc("den"); nre = sc("nre"); cre = sc("cre"); cim = sc("cim"); tA = sc("tA"); tB = sc("tB")
        rho = sc("rho"); rrho = sc("rrho")
        PWr = sc("PWr", 9 * 16); PWi = sc("PWi", 9 * 16)
        WKr = sc("WKr", 9 * 16); WKi = sc("WKi", 9 * 16)
        pw = lambda t, k: t.ap[:, k * 16:(k + 1) * 16]
        self.act(dt.ap, ldt, AF.Exp, Rp, Wp)
        self.tt(th.ap, lim, dt.ap, ALU.mult, Rp, Wp)
        self.tt(er.ap, lre, dt.ap, ALU.mult, Rp, Wp)
        self.act(s_.ap, th.ap, AF.Sin, Rp, Wp, scale=0.125)
        self.act(tA.ap, th.ap, AF.Sin, Rp, Wp, scale=0.0625)
        self.tt(tA.ap, tA.ap, tA.ap, ALU.mult, Rp, Wp)
        self.ts(c_.ap, tA.ap, -2.0, 1.0, ALU.mult, ALU.add, Rp, Wp)
        for _ in range(3):
            self.tt(cc.ap, c_.ap, c_.ap, ALU.mult, Rp, Wp)
            self.tt(ss.ap, s_.ap, s_.ap, ALU.mult, Rp, Wp)
            self.tt(cs.ap, c_.ap, s_.ap, ALU.mult, Rp, Wp)
            self.tt(c_.ap, cc.ap, ss.ap, ALU.subtract, Rp, Wp)
            self.ts(s_.ap, cs.ap, 2.0, None, ALU.mult, None, Rp, Wp)
        self.act(dec.ap, er.ap, AF.Exp, Rp, Wp)
        self.act(rho.ap, er.ap, AF.Exp, Rp, Wp, scale=8.0)
        self.act(rrho.ap, er.ap, AF.Exp, Rp, Wp, scale=-8.0)
        self.tt(are.ap, dec.ap, c_.ap, ALU.mult, Rp, Wp)
        self.tt(aim.ap, dec.ap, s_.ap, ALU.mult, Rp, Wp)
        self.tt(tA.ap, lre, lre, ALU.mult, Rp, Wp)
        self.tt(tB.ap, lim, lim, ALU.mult, Rp, Wp)
        self.tt(den.ap, tA.ap, tB.ap, ALU.add, Rp, Wp)
        self.recip(den.ap, den.ap, Rp, Wp)
        self.ts(nre.ap, are.ap, -1.0, None, ALU.add, None, Rp, Wp)
        self.tt(tA.ap, nre.ap, lre, ALU.mult, Rp, Wp)
        self.tt(tB.ap, aim.ap, lim, ALU.mult, Rp, Wp)
        self.tt(tA.ap, tA.ap, tB.ap, ALU.add, Rp, Wp)
        self.tt(cre.ap, tA.ap, den.ap, ALU.mult, Rp, Wp)
        self.tt(tA.ap, aim.ap, lre, ALU.mult, Rp, Wp)
        self.tt(tB.ap, nre.ap, lim, ALU.mult, Rp, Wp)
        self.tt(tA.ap, tA.ap, tB.ap, ALU.subtract, Rp, Wp)
        self.tt(cim.ap, tA.ap, den.ap, ALU.mult, Rp, Wp)

        def cmul(or_, oi_, ar, ai, br, bi):
            self.tt(tA.ap, ar, br, ALU.mult, Rp, Wp)
            self.tt(tB.ap, ai, bi, ALU.mult, Rp, Wp)
            self.tt(cc.ap, ar, bi, ALU.mult, Rp, Wp)
            self.tt(ss.ap, ai, br, ALU.mult, Rp, Wp)
            self.tt(or_, tA.ap, tB.ap, ALU.subtract, Rp, Wp)
            self.tt(oi_, cc.ap, ss.ap, ALU.add, Rp, Wp)
        self.memset(pw(PWr, 0), 1.0, Wp)
        self.memset(pw(PWi, 0), 0.0, Wp)
        for k in range(1, 9):
            cmul(pw(PWr, k), pw(PWi, k), pw(PWr, k - 1), pw(PWi, k - 1), are.ap, aim.ap)
        self.tt(pw(WKr, 0), pw(PWr, 8), rrho.ap, ALU.mult, Rp, Wp)
        self.tt(pw(WKi, 0), pw(PWi, 8), rrho.ap, ALU.mult, Rp, Wp)
        for k in range(1, 9):
            cmul(pw(WKr, k), pw(WKi, k), pw(WKr, k - 1), pw(WKi, k - 1), pw(WKr, k - 1), pw(WKi, k - 1))
        SB = A.alloc([4, 256], F32, "ssmbc")
        self.ld(SB, d['ssmbc'][l])
        wb = Buf("s5w")
        Rw = [pb, wb, SB.b]
        Ww = [wb]

        def wt(shape, dt_, name):
            t = A.alloc(shape, dt_, name)
            t.b = wb
            return t
        BBr = wt([16, 16], F32, "BBr"); BBi = wt([16, 16], F32, "BBi")
        u1 = wt([16, 16], F32, "u1"); u2 = wt([16, 16], F32, "u2")
        bc16 = lambda ap: fv(ap, [[1, 16], [0, 16]])
        b_re = SB.ap[:, 0].rearrange("p (q h) -> p q h", h=16)
        b_im = SB.ap[:, 1].rearrange("p (q h) -> p q h", h=16)
        c_re = SB.ap[:, 2].rearrange("p (q h) -> p q h", h=16)
        c_im = SB.ap[:, 3].rearrange("p (q h) -> p q h", h=16)
        self.tt(u1.ap, b_re, bc16(cre.ap), ALU.mult, Rw, Ww)
        self.tt(u2.ap, b_im, bc16(cim.ap), ALU.mult, Rw, Ww)
        self.tt(BBr.ap, u1.ap, u2.ap, ALU.subtract, Rw, Ww)
        self.tt(u1.ap, b_im, bc16(cre.ap), ALU.mult, Rw, Ww)
        self.tt(u2.ap, b_re, bc16(cim.ap), ALU.mult, Rw, Ww)
        self.tt(BBi.ap, u1.ap, u2.ap, ALU.add, Rw, Ww)
        XEr = wt([16, 15, 16], BF16, "XEr"); XEi = wt([16, 15, 16], BF16, "XEi")
        CAr = wt([16, 9, 16], BF16, "CAr"); CAi = wt([16, 9, 16], BF16, "CAi")
        Crb = wt([16, 16], BF16, "Crb"); Cib = wt([16, 16], BF16, "Cib")
        self.memset(fv(XEr.ap, [[1, 16 * 15 * 16]]), 0.0, Ww)
        self.memset(fv(XEi.ap, [[1, 16 * 15 * 16]]), 0.0, Ww)
        self.cp(Crb.ap, c_re, Rw, Ww)
        self.ts(Cib.ap, c_im, -1.0, None, ALU.mult, None, Rw, Ww)
        u3 = wt([16, 16], F32, "u3"); u4 = wt([16, 16], F32, "u4")
        for k in range(8):
            pr_b = bc16(pw(PWr, k)); pi_b = bc16(pw(PWi, k))
            self.tt(u1.ap, BBr.ap, pr_b, ALU.mult, Rw, Ww)
            self.tt(u2.ap, BBi.ap, pi_b, ALU.mult, Rw, Ww)
            self.tt(u3.ap, BBi.ap, pr_b, ALU.mult, Rw, Ww)
            self.tt(u4.ap, BBr.ap, pi_b, ALU.mult, Rw, Ww)
            for dr in range(2):
                m = 7 - k if dr == 0 else 7 + k
                qs = slice(dr * 8, dr * 8 + 8)
                self.tt(XEr.ap[:, qs, m, :], u1.ap[:, qs, :], u2.ap[:, qs, :], ALU.subtract, Rw, Ww)
                self.tt(XEi.ap[:, qs, m, :], u3.ap[:, qs, :], u4.ap[:, qs, :], ALU.add, Rw, Ww)
        for m in range(9):
            pr_b = bc16(pw(PWr, m)); pi_b = bc16(pw(PWi, m))
            self.tt(u1.ap, c_re, pr_b, ALU.mult, Rw, Ww)
            self.tt(u2.ap, c_im, pi_b, ALU.mult, Rw, Ww)
            self.tt(u3.ap, c_im, pr_b, ALU.mult, Rw, Ww)
            self.tt(u4.ap, c_re, pi_b, ALU.mult, Rw, Ww)
            for dr in range(2):
                qs = slice(dr * 8, dr * 8 + 8)
                mp = m if dr == 0 else 8 - m
                self.tt(CAr.ap[:, qs, mp, :], u1.ap[:, qs, :], u2.ap[:, qs, :], ALU.subtract, Rw, Ww)
                self.stt(CAi.ap[:, qs, mp, :], u3.ap[:, qs, :], -1.0, u4.ap[:, qs, :], ALU.mult, ALU.subtract, Rw, Ww)
        WEr = wt([16, 128], BF16, "WEr"); WEi = wt([16, 128], BF16, "WEi")
        for (XE, WE) in ((XEr, WEr), (XEi, WEi)):
            for hb in range(2):
                ps = self.psn()
                psb = ps.ap.bitcast(BF16)
                for qq in range(8):
                    q = hb * 8 + qq
                    m0 = 0 if q < 8 else 7
                    self.tr(psb[:, qq * 128:(qq + 1) * 128], fv(XE.ap[:, q, m0, :], [[1, 128]]), self.ident,
                            [wb, self.CB.b], [ps.b], inc=(qq == 7))
                self.cp(fv(WE.ap[:, hb * 8, :], [[1, 1024]]), psb, [ps.b], Ww)
        WL = wt([32, 128], BF16, "WLOC")
        drep = self.sm('drep')
        for dr in range(2):
            for kb in range(4):
                ps = self.psn()
                g2 = kb % 2
                rows = slice(g2 * 64, g2 * 64 + 64)
                for gi in range(4):
                    gp = 4 * (kb // 2) + gi
                    q = dr * 8 + gp
                    for j in range(8):
                        o_ = ps.ap[:, gi * 128 + j * 16: gi * 128 + (j + 1) * 16]
                        self.mm(o_, fv(XEr.ap[rows, q, 7 - j, :], [[1, 128]]), Crb.ap[rows, q, :], True, False, [wb], [ps.b], inc=False)
                        self.mm(o_, fv(XEi.ap[rows, q, 7 - j, :], [[1, 128]]), Cib.ap[rows, q, :], False, True, [wb], [ps.b],
                                inc=(gi == 3 and j == 7))
                g0 = 2 * (4 * (kb // 2)) + g2
                if dr == 0:
                    for gi in range(4):
                        g = g0 + 2 * gi
                        self.stt(WL.ap[:, g, :], self.ident, drep[:, l, g:g + 1], ps.ap[:, gi * 128:(gi + 1) * 128], ALU.mult, ALU.add,
                                 [ps.b, self.CB.b, SM.b], Ww)
                else:
                    self.cp(fv(WL.ap[:, 16 + g0, :], [[256, 4], [1, 128]]), fv(ps.ap, [[128, 4], [1, 128]]), [ps.b], Ww, eng="act")
        if 'stopS1' in self.dbg:
            return
        SNr = A.alloc([16, NB], BF16, "SINr"); SNi = A.alloc([16, NB], BF16, "SINi")
        FS = A.alloc([2, 2, 16], F32, "FS")
        self.memset(fv(SNr.ap, [[1, 16 * NB]]), 0.0, [SNr.b])
        self.memset(fv(SNi.ap, [[1, 16 * NB]]), 0.0, [SNi.b], eng="pool")
        Tc = A.alloc([512], F32, "Tc"); Ts = A.alloc([512], F32, "Ts"); Tt = A.alloc([256], F32, "Tt")
        TcL = A.alloc([NB], F32, "TcL"); TsL = A.alloc([NB], F32, "TsL")
        Zr = A.alloc([NB], F32, "Zr"); Zi = A.alloc([NB], F32, "Zi")
        Yr = A.alloc([NB], F32, "Yr"); Yi = A.alloc([NB], F32, "Yi")
        m1 = A.alloc([NB], F32, "m1"); m2 = A.alloc([NB], F32, "m2")
        m3 = A.alloc([NB], F32, "m3"); m4 = A.alloc([NB], F32, "m4")
        segs = [(0, NBS), (NBS, NBS + 32), (NBS + 32, NBS + 64)]
        s0r = self.sm('s0re')
        s0i = self.sm('s0im')
        for q in range(16):
            dr, gp = q // 8, q % 8
            self.cp(Tc.ap[:, 0:1], pw(WKr, 0)[:, q:q + 1], [pb], [Tc.b])
            self.cp(Ts.ap[:, 0:1], pw(WKi, 0)[:, q:q + 1], [pb], [Ts.b])
            for k in range(9):
                n = 1 << k
                wr = pw(WKr, k)[:, q:q + 1]
                wi = pw(WKi, k)[:, q:q + 1]
                self.ts(Tt.ap[:, 0:n], Ts.ap[:, 0:n], wi, None, ALU.mult, None, [Ts.b, pb], [Tt.b])
                self.stt(Tc.ap[:, n:2 * n], Tc.ap[:, 0:n], wr, Tt.ap[:, 0:n], ALU.mult, ALU.subtract, [Tc.b, Tt.b, pb], [Tc.b])
                self.ts(Tt.ap[:, 0:n], Tc.ap[:, 0:n], wi, None, ALU.mult, None, [Tc.b, pb], [Tt.b])
                self.stt(Ts.ap[:, n:2 * n], Ts.ap[:, 0:n], wr, Tt.ap[:, 0:n], ALU.mult, ALU.add, [Ts.b, Tt.b, pb], [Ts.b])
            for (a, b_) in segs:
                n = b_ - a
                for (Tx, TxL) in ((Tc, TcL), (Ts, TsL)):
                    if dr == 0:
                        self.cp(TxL.ap[:, a:b_], Tx.ap[:, 0:n], [Tx.b], [TxL.b])
                    else:
                        self.cp(TxL.ap[:, a:b_], fv(Tx.ap, [[-1, n]], off=n - 1), [Tx.b], [TxL.b])
            pi_ = (q % 2) * 2
            Er = self.PST[pi_][:, 0:NB]
            Ei = self.PST[pi_ + 1][:, 0:NB]
            bre = [self.PS[2 * pi_].b, self.PS[2 * pi_ + 1].b]
            bim = [self.PS[2 * pi_ + 2].b, self.PS[2 * pi_ + 3].b]
            for (E_, WE, bb) in ((Er, WEr, bre), (Ei, WEi, bim)):
                for g2 in range(2):
                    rows = slice(g2 * 64, g2 * 64 + 64)
                    chunks = [(c0, min(c0 + 512, NB)) for c0 in range(0, NB, 512)]
                    for ci, (c0, c1) in enumerate(chunks):
                        self.mm(E_[rows, c0:c1], WE.ap[:, q, g2 * 64:(g2 + 1) * 64], UT.ap[:, 2 * gp + g2, c0:c1], True, True,
                                [wb, UT.b], bb, inc=(g2 == 1 and ci == len(chunks) - 1))
            self.tt(m1.ap, Er, TcL.ap, ALU.mult, bre + [TcL.b], [m1.b])
            self.tt(m2.ap, Ei, TsL.ap, ALU.mult, bim + [TsL.b], [m2.b])
            self.tt(m3.ap, Ei, TcL.ap, ALU.mult, bim + [TcL.b], [m3.b])
            self.tt(m4.ap, Er, TsL.ap, ALU.mult, bre + [TsL.b], [m4.b])
            self.tt(Zr.ap, m1.ap, m2.ap, ALU.add, [m1.b, m2.b], [Zr.b], eng="pool")
            self.tt(Zi.ap, m3.ap, m4.ap, ALU.subtract, [m3.b, m4.b], [Zi.b], eng="pool")
            for si, (a, b_) in enumerate(segs):
                n = b_ - a
                for (Z, Y, s0) in ((Zr, Yr, s0r), (Zi, Yi, s0i)):
                    init = s0[:, l, q:q + 1] if si == 0 else 0.0
                    rb = rho.ap[:, q:q + 1].to_broadcast([128, n])
                    if dr == 0:
                        o_, i_ = Y.ap[:, a:b_], Z.ap[:, a:b_]
                    else:
                        o_, i_ = fv(Y.ap, [[-1, n]], off=b_ - 1), fv(Z.ap, [[-1, n]], off=b_ - 1)
                    S.op("dve", lambda e, o_=o_, rb=rb, i_=i_, init=init: e.tensor_tensor_scan(o_, rb, i_, init, ALU.mult, ALU.add),
                         [Z.b, pb, SM.b], [Y.b])
            self.tt(m1.ap, Yr.ap, TcL.ap, ALU.mult, [Yr.b, TcL.b], [m1.b])
            self.tt(m2.ap, Yi.ap, TsL.ap, ALU.mult, [Yi.b, TsL.b], [m2.b])
            self.tt(m3.ap, Yr.ap, TsL.ap, ALU.mult, [Yr.b, TsL.b], [m3.b], eng="pool")
            self.tt(m4.ap, Yi.ap, TcL.ap, ALU.mult, [Yi.b, TcL.b], [m4.b], eng="pool")
            for si, (a, b_) in enumerate(segs):
                if dr == 0:
                    osl, isl = slice(a + 1, b_), slice(a, b_ - 1)
                    s0col, fcol = a, b_ - 1
                else:
                    osl, isl = slice(a, b_ - 1), slice(a + 1, b_)
                    s0col, fcol = b_ - 1, a
                self.tt(SNr.ap[:, q, osl], m1.ap[:, isl], m2.ap[:, isl], ALU.subtract, [m1.b, m2.b], [SNr.b])
                self.tt(SNi.ap[:, q, osl], m3.ap[:, isl], m4.ap[:, isl], ALU.add, [m3.b, m4.b], [SNi.b])
                if si == 0:
                    self.cp(SNr.ap[:, q, s0col:s0col + 1], s0r[:, l, q:q + 1], [SM.b], [SNr.b])
                    self.cp(SNi.ap[:, q, s0col:s0col + 1], s0i[:, l, q:q + 1], [SM.b], [SNi.b])
                else:
                    pr = si - 1
                    self.tt(FS.ap[:, 0, pr, q:q + 1], m1.ap[:, fcol:fcol + 1], m2.ap[:, fcol:fcol + 1], ALU.subtract, [m1.b, m2.b], [FS.b])
                    self.tt(FS.ap[:, 1, pr, q:q + 1], m3.ap[:, fcol:fcol + 1], m4.ap[:, fcol:fcol + 1], ALU.add, [m3.b, m4.b], [FS.b])
        self.st(d['o_ss'][l], FS, W=[self.dbuf('o_ss')])
        if 'stopS2' in self.dbg:
            return
        WG = A.alloc([2, 256], BF16, "WGLU")
        self.ld(WG, d['w_glu'][l].rearrange("(i p) n -> p i n", p=128), q="pool")
        TM = A.alloc([8, 256], BF16, "TM")
        GF = [A.alloc([2, 512], BF16, "GF%d" % i) for i in range(2)]
        x2 = [A.alloc([512], F32, "gx2%d" % i) for i in range(2)]
        ux = [A.alloc([512], F32, "gux%d" % i) for i in range(2)]
        sg = [A.alloc([512], F32, "gsg%d" % i) for i in range(2)]
        oc = [A.alloc([512], BF16, "oc%d" % i) for i in range(2)]
        bglu = self.sm('bglu')
        for t in range(self.NTILE):
            tok = slice(t * 512, (t + 1) * 512)
            bc = slice(t * 64, (t + 1) * 64)
            for kb in range(4):
                ps = self.psn()
                g2 = kb % 2
                rows = slice(g2 * 64, g2 * 64 + 64)
                for gi in range(4):
                    gp = 4 * (kb // 2) + gi
                    g = 2 * gp + g2
                    o_ = ps.ap[0:64, gi * 128:(gi + 1) * 128]
                    n_mm = 0
                    for dr in range(2):
                        q = dr * 8 + gp
                        if dr == 0:
                            rr_ = fv(CAr.ap[rows, q, 1, :], [[1, 128]])
                            ri_ = fv(CAi.ap[rows, q, 1, :], [[1, 128]])
                        else:
                            rr_ = fv(CAr.ap[rows, q, 0, :], [[1, 128]])
                            ri_ = fv(CAi.ap[rows, q, 0, :], [[1, 128]])
                        for (lh, rh, Rb) in ((SNr.ap[rows, q, bc], rr_, [SNr.b, wb]), (SNi.ap[rows, q, bc], ri_, [SNi.b, wb]),
                                             (UT.ap[:, g, bc], WL.ap[:, dr * 16 + g, :], [UT.b, wb])):
                            self.mm(o_, lh, rh, n_mm == 0, n_mm == 5, Rb, [ps.b], inc=(gi == 3 and n_mm == 5))
                            n_mm += 1
                i2 = kb % 2
                self.act(x2[i2].ap[0:64], ps.ap[0:64], AF.Square, [ps.b], [x2[i2].b])
                self.ts(x2[i2].ap[0:64], x2[i2].ap[0:64], 0.044715, 1.0, ALU.mult, ALU.add, [x2[i2].b], [x2[i2].b])
                self.tt(ux[i2].ap[0:64], x2[i2].ap[0:64], ps.ap[0:64], ALU.mult, [x2[i2].b, ps.b], [ux[i2].b])
                self.act(sg[i2].ap[0:64], ux[i2].ap[0:64], AF.Sigmoid, [ux[i2].b], [sg[i2].b], scale=1.5957691216057308)
                ch0 = (8 * (kb // 2) + g2) * 16
                self.tt(fv(TM.ap[0:64], [[32, 4], [256, 8], [1, 16]], off=ch0),
                        fv(ps.ap[0:64], [[128, 4], [16, 8], [1, 16]]),
                        fv(sg[i2].ap[0:64], [[128, 4], [16, 8], [1, 16]]), ALU.mult, [ps.b, sg[i2].b], [TM.b])
            pT = self.psn()
            pTb = pT.ap.bitcast(BF16)
            for j in range(8):
                for ct in range(2):
                    self.tr(pTb[:, (j * 2 + ct) * 64:(j * 2 + ct + 1) * 64], TM.ap[0:64, j, ct * 128:(ct + 1) * 128], self.ident[0:64, 0:64],
                            [TM.b, self.CB.b], [pT.b], inc=(j == 7 and ct == 1))
            gf = GF[t % 2]
            for ct in range(2):
                self.cp(fv(gf.ap[:, ct, :], [[1, 8], [8, 64]]), fv(pTb, [[128, 8], [1, 64]], off=ct * 64), [pT.b], [gf.b],
                        eng=("act" if ct else "dve"))
            for mt in range(2):
                ps = self.psn()
                for ct in range(2):
                    self.mm(ps.ap, WG.ap[:, ct, mt * 128:(mt + 1) * 128], gf.ap[:, ct, :], ct == 0, ct == 1, [WG.b, gf.b], [ps.b], inc=(ct == 1))
                self.act(sg[mt].ap, ps.ap, AF.Sigmoid, [ps.b, SM.b], [sg[mt].b], bias=bglu[:, l, mt:mt + 1])
                self.tt(oc[mt].ap, gf.ap[:, mt, :], sg[mt].ap, ALU.mult, [gf.b, sg[mt].b], [oc[mt].b])
                self.st(d['mix'][6 + mt, :, tok], oc[mt], W=[self.dbuf('mix')])

    def attn_bufs(self):
        A = self.A
        self.attn_sbanks = [self.PS[0], self.PS[1], self.PS[2], self.PS[3]]
        self.sbi = 0
        self.pbi = 0
        self.rci = 0
        self.obi = 0
        self.fin_pending = None
        Pb = [A.alloc([2, 512], BF16, "P%d" % i) for i in range(5)]
        rcs = [(A.alloc([512], F32, "rcf%d" % i), A.alloc([512], BF16, "rch%d" % i), A.alloc([512], BF16, "rcl%d" % i)) for i in range(4)]
        onf = [(A.alloc([512], F32, "bcs%d" % i), A.alloc([512], BF16, "on%d" % i)) for i in range(4)]
        return Pb, rcs, onf

    def ffn_weight_cast(self):
        d, l = self.d, self.l
        w1v = d['w_f1'][l].rearrange("(kt p) n -> p kt n", p=128)
        w2v = d['w_f2'][l].rearrange("(f p) n -> p f n", p=128)
        for fc in range(FT // 2):
            for gu in range(2):
                c0 = gu * DFF + fc * 256
                self.S.dma("pool", d['w1s'][fc, :, :, gu, :], w1v[:, :, c0:c0 + 256])
                yield
        for m in range(8):
            self.S.dma("pool", d['w2s'][m], w2v[:, :, m * 128:(m + 1) * 128])
            yield

    def cast_pump(self, n):
        for _ in range(n):
            if self.castgen is None:
                return
            try:
                next(self.castgen)
            except StopIteration:
                self.castgen = None

    def phaseGA(self):
        A, S, d = self.A, self.S, self.d
        l, LS, NS, NT, NKEY = self.l, self.LS, self.NS, self.NT, self.NKEY
        self.castgen = self.ffn_weight_cast()
        NKT = NKEY // 128
        nlat = LS // 128
        Kt = A.alloc([NKEY], BF16, "Kt")
        self.ld(Kt, d['ka'][:, 0:LS], out_ap=Kt.ap[:, 0:LS], R=[self.dbuf('ka')])
        self.ld(Kt, d['ka'][:, LS + PAST:NKEY], out_ap=Kt.ap[:, LS + PAST:NKEY], R=[self.dbuf('ka')])
        self.ld(Kt, d['cK'][l], out_ap=Kt.ap[:, LS:LS + PAST], q="pool")
        Vt = A.alloc([NKT, 2, 65], BF16, "Vt")
        for g in range(2):
            self.ld(Vt, d['va'][0:LS, g * 64:(g + 1) * 64].rearrange("(kt p) e -> p kt e", p=128), out_ap=Vt.ap[:, 0:nlat, g, 0:64],
                    R=[self.dbuf('va')])
            self.ld(Vt, d['va'][LS:NT, g * 64:(g + 1) * 64].rearrange("(kt p) e -> p kt e", p=128), out_ap=Vt.ap[:, nlat + 4:NKT, g, 0:64],
                    R=[self.dbuf('va')])
            self.ld(Vt, d['cV'][l][:, g * 64:(g + 1) * 64].rearrange("(kt p) e -> p kt e", p=128), out_ap=Vt.ap[:, nlat:nlat + 4, g, 0:64],
                    q="pool")
        self.memset(Vt.ap[:, :, :, 64:65], 1.0, [Vt.b])
        QT = [A.alloc([3, 512], BF16, "QT%d" % i) for i in range(2)]
        Pb, rcs, onf = self.attn_bufs()
        scale = 64 ** -0.5
        for t in range(self.NTILE):
            prm = (t == NS)
            tok0 = t * 512
            Q = QT[t % 2]
            self.ld(Q, d['qa'][:, :, tok0:tok0 + 512].rearrange("i p n -> p i n"), R=[self.dbuf('qa')])
            if not prm:
                jobs = [(0, 512, list(range(0, nlat + 4)))]
            else:
                jobs = [(p * 256, 256, [nlat + 4 + 2 * p, nlat + 4 + 2 * p + 1]) for p in range(2)]
            for (c0, N, kts) in jobs:
                for i in range(3):
                    heads = []
                    for s_, hh in enumerate((i, i + 3)):
                        heads.append((self.PS[6 + s_], d['mix'][hh // 2, (hh % 2) * 64:(hh % 2) * 64 + 64, tok0 + c0:tok0 + c0 + N]))
                    steps = []
                    for kt in kts:
                        st_ = []
                        for s_ in range(2):
                            rows = slice(s_ * 64, s_ * 64 + 64)
                            st_.append((Kt.ap[rows, kt * 128:(kt + 1) * 128], Q.ap[rows, i, c0:c0 + N], Vt.ap[:, kt, s_, :], s_, [Kt.b, Q.b], [Vt.b]))
                        steps.append(st_)
                    self.attn_core3(steps, heads, N, scale, Pb, rcs, onf)
                    self.cast_pump(2)
        self.attn_flush()
        self.cast_pump(1000)

    def attn_core3(self, steps, heads, N, scale, Pb, rcs, onf):
        nsteps = len(steps)
        started = [False] * len(heads)
        last_step_of = [max(si for si, st_ in enumerate(steps) for sl in st_ if sl[3] == hi) for hi in range(len(heads))]
        prev = None

        def pv(si, st_, P_):
            for j, sl in enumerate(st_):
                hi = sl[3]
                O = heads[hi][0]
                is_last = (si == last_step_of[hi]) and all(s2[3] != hi for s2 in st_[j + 1:])
                self.mm(O.ap[0:65, 0:N], sl[2], P_.ap[:, j, 0:N], not started[hi], is_last, [P_.b] + sl[5], [O.b], inc=is_last)
                started[hi] = True

        fin_prev = self.fin_pending
        self.fin_pending = None
        pend = []
        for si, st_ in enumerate(steps):
            if si == 2 and fin_prev is not None:
                fin_prev()
                fin_prev = None
            di = self.sbi % 3
            self.sbi += 1
            Sd = self.PST[di]
            sb = [self.PS[2 * di].b, self.PS[2 * di + 1].b]
            for j, sl in enumerate(st_):
                self.mm(Sd[:, j * 512:j * 512 + N], sl[0], sl[1], True, True, sl[4], sb, inc=(j == len(st_) - 1))
            if len(pend) >= 2:
                pv(*pend.pop(0))
            P_ = Pb[self.pbi % len(Pb)]
            self.pbi += 1
            nj = len(st_)
            self.act(fv(P_.ap[:, 0, :], [[512, nj], [1, N]]), fv(Sd, [[512, nj], [1, N]]), AF.Exp, sb, [P_.b], scale=scale)
            pend.append((si, st_, P_))
        if fin_prev is not None:
            fin_prev()
            fin_prev = None
        while pend:
            pv(*pend.pop(0))
        parts = []
        for hi, (O, dst) in enumerate(heads):
            rcf, rch, rcl = rcs[self.rci % len(rcs)]
            bcs, on = onf[self.rci % len(onf)]
            self.rci += 1
            self.recip(rcf.ap[64:65, 0:N], O.ap[64:65, 0:N], [O.b], [rcf.b])
            self.cp(rch.ap[64:65, 0:N], rcf.ap[64:65, 0:N], [rcf.b], [rch.b])
            self.tt(rcl.ap[64:65, 0:N], rcf.ap[64:65, 0:N], rch.ap[64:65, 0:N], ALU.subtract, [rcf.b, rch.b], [rcl.b])
            parts.append((O, dst, rch, rcl, bcs, on))

        def fin_b(parts=parts, N=N):
            for (O, dst, rch, rcl, bcs, on) in parts:
                bcp = self.PS[2 * (self.sbi % 3)]
                self.sbi += 1
                self.mm(bcp.ap[0:64, 0:N], self.ones[64:65, 0:64], rch.ap[64:65, 0:N], True, False, [rch.b, self.CB.b], [bcp.b], inc=False)
                self.mm(bcp.ap[0:64, 0:N], self.ones[64:65, 0:64], rcl.ap[64:65, 0:N], False, True, [rcl.b, self.CB.b], [bcp.b])
                self.cp(bcs.ap[0:64, 0:N], bcp.ap[0:64, 0:N], [bcp.b], [bcs.b])
                self.tt(on.ap[0:64, 0:N], O.ap[0:64, 0:N], bcs.ap[0:64, 0:N], ALU.mult, [O.b, bcs.b], [on.b])
                self.st(dst, on, in_ap=on.ap[0:64, 0:N])
        self.fin_pending = fin_b

    def attn_flush(self):
        if self.fin_pending is not None:
            self.fin_pending()
            self.fin_pending = None

    def phaseMA(self):
        A, S, d = self.A, self.S, self.d
        l, LS, NS, NT, NKEY = self.l, self.LS, self.NS, self.NT, self.NKEY
        NKT = NKEY // 128
        nlat = LS // 128
        KH = [A.alloc([NKEY], BF16, "KH%d" % i) for i in range(2)]
        VH = [A.alloc([NKT, 65], BF16, "VH%d" % i) for i in range(2)]
        for i in range(2):
            self.memset(VH[i].ap[:, :, 64:65], 1.0, [VH[i].b])
        QH = [A.alloc([512], BF16, "QH%d" % i) for i in range(3)]
        Pb, rcs, onf = self.attn_bufs()
        scale = 96 ** -0.5
        qi = 0
        for hh in range(6):
            Kh = KH[hh % 2]
            Vh = VH[hh % 2]
            self.ld(Kh, d['kb'][hh], out_ap=Kh.ap[0:96, :], R=[self.dbuf('kb')])
            self.ld(Vh, d['vb'][:, hh * 64:(hh + 1) * 64].rearrange("(kt p) e -> p kt e", p=128), out_ap=Vh.ap[:, :, 0:64], R=[self.dbuf('vb')])
            for t in range(self.NTILE):
                prm = (t == NS)
                tok0 = t * 512
                Q = QH[qi % 3]
                qi += 1
                self.ld(Q, d['qb'][hh, :, tok0:tok0 + 512], out_ap=Q.ap[0:96, :], R=[self.dbuf('qb')])
                if not prm:
                    jobs = [(0, 512, list(range(0, nlat + 4)))]
                else:
                    jobs = [(p * 256, 256, [nlat + 4 + 2 * p, nlat + 4 + 2 * p + 1]) for p in range(2)]
                for (c0, N, kts) in jobs:
                    Ob = self.PS[6 + (self.obi % 2)]
                    self.obi += 1
                    dst = d['mix'][3 + hh // 2, (hh % 2) * 64:(hh % 2) * 64 + 64, tok0 + c0:tok0 + c0 + N]
                    steps = []
                    for k2 in range(0, len(kts), 2):
                        st_ = []
                        for kt in kts[k2:k2 + 2]:
                            st_.append((Kh.ap[0:96, kt * 128:(kt + 1) * 128], Q.ap[0:96, c0:c0 + N], Vh.ap[:, kt, :], 0, [Kh.b, Q.b], [Vh.b]))
                        steps.append(st_)
                    self.attn_core3(steps, [(Ob, dst)], N, scale, Pb, rcs, onf)
        self.attn_flush()

    def phaseO(self, xcur):
        A, S, d = self.A, self.S, self.d
        l, NS = self.l, self.NS
        WO = A.alloc([8, D], BF16, "WO")
        self.ld(WO, d['w_out'][l].rearrange("(kt p) n -> p kt n", p=128), q="pool")
        MX = [A.alloc([8, 512], BF16, "mx%d" % i) for i in range(2)]
        XT = [A.alloc([8, 512], F32, "xo%d" % i) for i in range(2)]
        XM = [A.alloc([8, 512], F32, "xmo%d" % i) for i in range(2)]
        xv = xcur.rearrange("(kt p) n -> p kt n", p=128)
        xmv = d['xm'].rearrange("(kt p) n -> p kt n", p=128)
        MOD = self.MOD
        for t in range(self.NTILE):
            c = 1 if t == NS else 0
            tok = slice(t * 512, (t + 1) * 512)
            mx, xt, xm = MX[t % 2], XT[t % 2], XM[t % 2]
            self.ld(mx, d['mix'][:, :, tok].rearrange("k p n -> p k n"), R=[self.dbuf('mix')])
            self.ld(xt, xv[:, :, tok], R=[self.dbuf('xcur')])
            for m in range(8):
                ps = self.psn()
                for kt in range(8):
                    self.mm(ps.ap, WO.ap[:, kt, m * 128:(m + 1) * 128], mx.ap[:, kt, :], kt == 0, kt == 7, [WO.b, mx.b], [ps.b], inc=(kt == 7))
                self.stt(xm.ap[:, m, :], ps.ap, MOD.ap[:, l, 2, m, c:c + 1], xt.ap[:, m, :], ALU.mult, ALU.add, [ps.b, xt.b, MOD.b], [xm.b])
            self.st(xmv[:, :, tok], xm, W=[self.dbuf('xm')])

    def phaseF(self, xnext, last):
        A, S, d = self.A, self.S, self.d
        l, NS = self.l, self.NS
        MOD = self.MOD
        TBT = 3
        X = A.alloc([TBT, 8, 512], F32, "Xf")
        H2 = A.alloc([TBT, 8, 512], BF16, "H2")
        ACTV = A.alloc([FT, TBT * 512], BF16, "ACTV")
        W1 = [A.alloc([8, 2, 256], BF16, "w1c%d" % i) for i in range(2)]
        W2 = [A.alloc([FT, 128], BF16, "w2%d" % i) for i in range(2)]
        sq8 = A.alloc([8, 512], BF16, "sq8f")
        tmp8 = T(fv(ACTV.ap, [[1, 8192]]).bitcast(F32).rearrange("p (k n) -> p k n", k=8), ACTV.b)
        rstd = A.alloc([512], F32, "rstdf")
        sgb = [A.alloc([512], F32, "sgf%d" % i) for i in range(3)]
        xmv = d['xm'].rearrange("(kt p) n -> p kt n", p=128)
        xnv = xnext.rearrange("(kt p) n -> p kt n", p=128)
        w1v = d['w_f1'][l].rearrange("(kt p) n -> p kt n", p=128)
        w2v = d['w_f2'][l].rearrange("(f p) n -> p f n", p=128)
        fing = self.sm('fing')
        blocks = []
        t = 0
        while t < self.NTILE:
            blocks.append(list(range(t, min(t + TBT, self.NTILE))))
            t += TBT
        wi = 0
        w2i = 0
        sgi = 0
        for blk in blocks:
            nt = len(blk)
            for ti, t in enumerate(blk):
                c = 1 if t == NS else 0
                tok = slice(t * 512, (t + 1) * 512)
                xt = T(X.ap[:, ti], X.b)
                ht = T(H2.ap[:, ti], H2.b)
                self.ld(xt, xmv[:, :, tok], R=[self.dbuf('xm')])
                self.norm_tile(xt, ht, 4, 3, c, sq8, rstd, tmp8)
            for fc in range(FT // 2):
                w1c = W1[wi % 2]
                wi += 1
                self.ld(w1c, d['w1s'][fc])
                wg = T(w1c.ap[:, :, 0, :], w1c.b)
                wu = T(w1c.ap[:, :, 1, :], w1c.b)
                for fi in range(2):
                    f = fc * 2 + fi
                    for ti in range(nt):
                        gps = self.psn()
                        ups = self.psn()
                        for kt in range(8):
                            self.mm(gps.ap, wg.ap[:, kt, fi * 128:(fi + 1) * 128], H2.ap[:, ti, kt, :], kt == 0, kt == 7, [wg.b, H2.b], [gps.b], inc=(kt == 7))
                        for kt in range(8):
                            self.mm(ups.ap, wu.ap[:, kt, fi * 128:(fi + 1) * 128], H2.ap[:, ti, kt, :], kt == 0, kt == 7, [wu.b, H2.b], [ups.b], inc=(kt == 7))
                        sg = sgb[sgi % 3]
                        sgi += 1
                        self.act(sg.ap, gps.ap, AF.Silu, [gps.b], [sg.b])
                        self.tt(ACTV.ap[:, f, ti * 512:(ti + 1) * 512], sg.ap, ups.ap, ALU.mult, [sg.b, ups.b], [ACTV.b])
            for m in range(8):
                w2 = W2[w2i % 2]
                w2i += 1
                self.ld(w2, d['w2s'][m])
                for ti, t in enumerate(blk):
                    c = 1 if t == NS else 0
                    ps = self.psn()
                    for f in range(FT):
                        self.mm(ps.ap, w2.ap[:, f, :], ACTV.ap[:, f, ti * 512:(ti + 1) * 512], f == 0, f == FT - 1, [w2.b, ACTV.b], [ps.b], inc=(f == FT - 1))
                    self.stt(X.ap[:, ti, m, :], ps.ap, MOD.ap[:, l, 5, m, c:c + 1], X.ap[:, ti, m, :], ALU.mult, ALU.add, [ps.b, X.b, MOD.b], [X.b])
            for ti, t in enumerate(blk):
                tok = slice(t * 512, (t + 1) * 512)
                xt = T(X.ap[:, ti], X.b)
                if not last:
                    self.st(xnv[:, :, tok], xt, W=[self.dbuf('xcur')])
                else:
                    self.act(sq8.ap, xt.ap, AF.Square, [xt.b], [sq8.b])
                    ps = self.psn()
                    for kt in range(8):
                        self.mm(ps.ap, self.ones, sq8.ap[:, kt, :], kt == 0, kt == 7, [sq8.b, self.CB.b], [ps.b], inc=(kt == 7))
                    self.act(rstd.ap, ps.ap, AF.Sqrt, [ps.b, self.eps.b], [rstd.b], bias=self.eps.ap, scale=1.0 / D)
                    self.recip(rstd.ap, rstd.ap, [rstd.b], [rstd.b])
                    self.tt(tmp8.ap, xt.ap, fv(rstd.ap, [[0, 8], [1, 512]]), ALU.mult, [xt.b, rstd.b], [tmp8.b])
                    for kt in range(8):
                        self.act(tmp8.ap[:, kt, :], tmp8.ap[:, kt, :], AF.Identity, [tmp8.b, self.SM.b], [tmp8.b], scale=fing[:, kt:kt + 1])
                    self.st(xnv[:, :, tok], tmp8, W=[self.dbuf('yT')])


def _rope_tables(LS):
    t = np.arange(LS)
    row = (t // 64).astype(np.float32)
    col = (t % 64).astype(np.float32)

    def tab(rot_dim):
        quarter = rot_dim // 4
        inv = (np.float32(10000.0) ** (-np.arange(quarter, dtype=np.float32) / np.float32(quarter))).astype(np.float32)
        ang = np.concatenate([row[:, None] * inv, col[:, None] * inv], axis=-1).astype(np.float32)
        c = np.cos(ang.astype(np.float64)).astype(np.float32)
        s = np.sin(ang.astype(np.float64)).astype(np.float32)
        c = np.concatenate([c, c], axis=-1)
        s = np.concatenate([s, s], axis=-1)
        return c.T, s.T

    cg, sg = tab(64)
    cm, sm_ = tab(32)
    ropeG = np.stack([np.tile(cg, (2, 1)), np.tile(sg, (2, 1))], axis=1)
    ropeM = np.stack([np.tile(cm, (4, 1)), np.tile(sm_, (4, 1))], axis=1)
    return np.ascontiguousarray(ropeG, np.float32), np.ascontiguousarray(ropeM, np.float32)


def _consts():
    ident = np.eye(128, dtype=np.float32)
    ones = np.ones((128, 128), np.float32)
    bd2 = np.zeros((128, 128), np.float32)
    bd2[0:64, 0:64] = 1
    bd2[64:128, 64:128] = 1

    def rot(n_heads, hd):
        m = np.zeros((128, 128), np.float32)
        half = hd // 2
        for hh in range(n_heads):
            b = hh * hd
            for dd in range(half):
                m[b + dd + half, b + dd] = -1.0
                m[b + dd, b + dd + half] = 1.0
        return m

    cb = np.concatenate([ident, ones, bd2, rot(2, 64), rot(4, 32)], axis=1)
    return cb.astype(ml_dtypes.bfloat16)


def _pack_small(inp, core, NL, SL, NSP):
    a = np.zeros((128, NSP), np.float32)

    def put(name, arr):
        off, shape = SL[name]
        arr = np.asarray(arr, np.float32).reshape(128, prod(shape))
        a[:, off:off + prod(shape)] = arr

    def fm(v, nt):
        return np.asarray(v)[:NL].reshape(NL, nt, 128).transpose(2, 0, 1)

    put('n1g', fm(inp['norm1_g'], 8))
    put('n2g', fm(inp['norm2_g'], 8))
    put('bada', fm(inp['b_ada'], 48))
    put('qn', np.tile(np.asarray(inp['gqa_q_norm'])[:NL].T, (2, 1)))
    put('kn', np.tile(np.asarray(inp['gqa_k_norm'])[:NL].T, (2, 1)))
    put('mqn', fm(inp['mla_q_norm'], 2))
    put('mkvn', fm(inp['mla_kv_norm'], 1))
    put('bglu', fm(inp['ssm_b_glu'], 2))
    put('fing', np.asarray(inp['final_g']).reshape(8, 128).T)
    cond = np.stack([np.asarray(inp['c'])[core], np.asarray(inp['c_ctx'])], axis=-1)
    put('cond', cond.reshape(8, 128, 2).transpose(1, 0, 2))

    def sq(v):
        v = np.asarray(v)[:NL].reshape(NL, 2, 8, 2, 64)
        return v.transpose(3, 4, 0, 1, 2).reshape(128, NL, 16)

    put('lre', sq(inp['ssm_lam_re']))
    put('lim', sq(inp['ssm_lam_im']))
    ldt = np.broadcast_to(np.asarray(inp['ssm_log_dt'])[:NL, :, :, None], (NL, 2, 16, 64))
    put('ldt', sq(ldt))
    put('s0re', sq(np.asarray(inp['state_ssm_re'])[core]))
    put('s0im', sq(np.asarray(inp['state_ssm_im'])[core]))
    dd = np.asarray(inp['ssm_d'])[:NL].reshape(NL, 16, 16)
    drep = np.broadcast_to(dd.transpose(2, 0, 1)[None], (8, 16, NL, 16)).reshape(128, NL, 16)
    put('drep', drep)
    return a


def _pack_ssmbc(inp, NL):
    def bq(v):
        v = np.asarray(v)[:NL].reshape(NL, 2, 8, 2, 64, 16)
        return v.transpose(3, 4, 0, 1, 2, 5).reshape(128, NL, 256)

    def cq(v):
        v = np.asarray(v)[:NL].reshape(NL, 2, 8, 2, 16, 64)
        return v.transpose(3, 5, 0, 1, 2, 4).reshape(128, NL, 256)

    a = np.stack([bq(inp['ssm_b_re']), bq(inp['ssm_b_im']), cq(inp['ssm_c_re']), cq(inp['ssm_c_im'])], axis=2)
    return np.ascontiguousarray(a.transpose(1, 0, 2, 3), np.float32)


def make_in_maps(inp, LS, NL, n_cores=8):
    SL, NSP = small_layout(NL)
    ropeG, ropeM = _rope_tables(LS)
    cbf = _consts()
    ssmbc = _pack_ssmbc(inp, NL)
    f = lambda k: np.ascontiguousarray(np.asarray(inp[k])[:NL], np.float32)
    shared = {
        'ssmbc': ssmbc, 'cbf': cbf, 'ropeG': ropeG, 'ropeM': ropeM,
        'w_ada': f('w_ada'), 'w_in': f('w_in'), 'w_uq': f('mla_w_uq'), 'w_uk': f('mla_w_uk'), 'w_uv': f('mla_w_uv'),
        'w_glu': f('ssm_w_glu'), 'w_out': f('w_out'), 'w_f1': f('w_ffn_in'), 'w_f2': f('w_ffn_out'),
    }
    maps = []
    xs = np.asarray(inp['x_sample'])
    xp = np.asarray(inp['x_prompt'])
    for c in range(n_cores):
        m = dict(shared)
        x = np.concatenate([xs[c, :LS], xp[2 * c], xp[2 * c + 1]], axis=0)
        m['xT'] = np.ascontiguousarray(x.T, np.float32)
        m['smallp'] = _pack_small(inp, c, NL, SL, NSP)
        m['cK'] = np.ascontiguousarray(np.asarray(inp['cache_gqa_k'])[c, :NL].reshape(NL, PAST, 128).transpose(0, 2, 1), np.float32)
        m['cV'] = np.ascontiguousarray(np.asarray(inp['cache_gqa_v'])[c, :NL].reshape(NL, PAST, 128), np.float32)
        m['cCKV'] = np.ascontiguousarray(np.asarray(inp['cache_mla_ckv'])[c, :NL].transpose(0, 2, 1), np.float32)
        m['cKR'] = np.ascontiguousarray(np.asarray(inp['cache_mla_krope'])[c, :NL].transpose(0, 2, 1), np.float32)
        maps.append(m)
    return maps


def assemble(results, LS, NL, n_cores=8):
    B = 2 * n_cores
    y_prompt = np.zeros((B, LP, D), np.float32)
    y_sample = np.zeros((n_cores, LS, D), np.float32)
    nk = np.zeros((B, NL, LP, 2, 64), np.float32)
    nv = np.zeros((B, NL, LP, 2, 64), np.float32)
    nckv = np.zeros((B, NL, LP, 128), np.float32)
    nkr = np.zeros((B, NL, LP, 32), np.float32)
    sre = np.zeros((B, NL, 2, 16, 64), np.float32)
    sim = np.zeros((B, NL, 2, 16, 64), np.float32)
    for c in range(n_cores):
        r = results[c]
        y = np.asarray(r['yT']).T
        y_sample[c] = y[:LS]
        for p in range(2):
            b = 2 * c + p
            y_prompt[b] = y[LS + p * LP: LS + (p + 1) * LP]
            sl = slice(p * LP, (p + 1) * LP)
            nk[b] = np.asarray(r['o_k'])[:, :, sl].transpose(0, 2, 1).reshape(NL, LP, 2, 64)
            nv[b] = np.asarray(r['o_v'])[:, sl, :].reshape(NL, LP, 2, 64)
            nckv[b] = np.asarray(r['o_ckv'])[:, :, sl].transpose(0, 2, 1)
            nkr[b] = np.asarray(r['o_kr'])[:, :, sl].transpose(0, 2, 1)
            ss = np.asarray(r['o_ss'])
            for ri, dst in ((0, sre), (1, sim)):
                v = ss[:, :, ri, p, :].reshape(NL, 2, 64, 2, 8)
                dst[b] = v.transpose(0, 3, 4, 1, 2).reshape(NL, 2, 16, 64)
    return (y_prompt, y_sample, nk, nv, nckv, nkr, sre, sim)


_CACHE = {}


def kernel(**inputs):
    LS, NL = 4096, 4
    key = (LS, NL)
    if key not in _CACHE:
        _CACHE[key] = Prog(LS, NL).build()
    nc = _CACHE[key]
    maps = make_in_maps(inputs, LS, NL)
    res = run_bass_kernel_spmd(nc, maps, core_ids=list(range(8)))
    return assemble(res.results, LS, NL)
```

```python
import math
import os
import numpy as np
import ml_dtypes
import concourse.bass as bass
import concourse.mybir as mybir
from concourse.bass_utils import run_bass_kernel_spmd
from concourse.ap import AP

F32 = mybir.dt.float32
BF16 = mybir.dt.bfloat16
ALU = mybir.AluOpType
AF = mybir.ActivationFunctionType

D = 1024
KT = 8
LP = 256
PAST = 512
DFF = 2816
FT = 22
EPS = 1e-6
INW = 1312
NQ = 16


def prod(s):
    r = 1
    for v in s:
        r *= int(v)
    return r


def small_layout(NL):
    off = 0
    L = {}

    def add(n, *shape):
        nonlocal off
        L[n] = (off, shape)
        off += prod(shape)

    add('n1g', NL, 8)
    add('n2g', NL, 8)
    add('bada', NL, 48)
    add('qn', NL)
    add('kn', NL)
    add('mqn', NL, 2)
    add('mkvn', NL)
    add('bglu', NL, 2)
    add('fing', 8)
    add('cond', 8, 2)
    add('lre', NL, 16)
    add('lim', NL, 16)
    add('ldt', NL, 16)
    add('s0re', NL, 16)
    add('s0im', NL, 16)
    add('drep', NL, 16)
    return L, off


class Buf:
    __slots__ = ("name", "w", "r")

    def __init__(self, name):
        self.name = name
        self.w = None
        self.r = {}


class T:
    __slots__ = ("ap", "b")

    def __init__(self, ap, b):
        self.ap = ap
        self.b = b


ENGS = ("pe", "act", "dve", "pool", "sp")
NDS = 40


class Sched:
    def __init__(self, nc, stack):
        self.nc = nc
        self.sem = {}
        self.cnt = {}
        self.q = {}
        self.waited = {}
        for e in ENGS:
            self.sem[("e", e)] = stack.enter_context(nc.semaphore("sem_" + e))
            self.cnt[e] = 0
            self.q[e] = []
            self.waited[e] = {}
        self.dcnt = {}
        self.dnext = {}
        for qn in ("sp", "pool"):
            self.dnext[qn] = 0
            for i in range(NDS):
                self.sem[("d", qn, i)] = stack.enter_context(nc.semaphore("d%s%d" % (qn, i)))
                self.dcnt[(qn, i)] = 0
        self.ninstr = 0
        self.dry = False

    def _wait(self, eng, key, val):
        if self.waited[eng].get(key, 0) >= val:
            return
        if key == ("e", eng) and (val > self.cnt[eng] or eng == "pe"):
            return
        sem = self.sem[key]
        self.q[eng].append(lambda e, sem=sem, val=val: e.wait_ge(sem, val))
        self.waited[eng][key] = val
        self.ninstr += 1

    def _deps(self, eng, R, W):
        for b in R:
            if b.w is not None:
                self._wait(eng, b.w[0], b.w[1])
        for b in W:
            if b.w is not None:
                self._wait(eng, b.w[0], b.w[1])
            for k, v in b.r.items():
                self._wait(eng, k, v)

    def op(self, eng, fn, R=(), W=(), inc=True):
        if self.dry:
            return
        self._deps(eng, R, W)
        key = ("e", eng)
        if inc:
            self.cnt[eng] += 1
            val = self.cnt[eng]
        else:
            val = self.cnt[eng] + 1
        for b in R:
            if b.r.get(key, 0) < val:
                b.r[key] = val
        for b in W:
            b.w = (key, val)
            b.r = {}
        sem = self.sem[key]
        if inc:
            self.q[eng].append(lambda e, fn=fn, sem=sem: fn(e).then_inc(sem, 1))
        else:
            self.q[eng].append(lambda e, fn=fn: fn(e))
        self.ninstr += 1

    def dma(self, qn, out, in_, R=(), W=(), **kw):
        if self.dry:
            return
        i = self.dnext[qn]
        self.dnext[qn] = (i + 1) % NDS
        key = ("d", qn, i)
        prev = self.dcnt[(qn, i)]
        if prev > 0:
            self._wait(qn, key, prev)
        self._deps(qn, R, W)
        self.dcnt[(qn, i)] = prev + 16
        val = prev + 16
        for b in R:
            if b.r.get(key, 0) < val:
                b.r[key] = val
        for b in W:
            b.w = (key, val)
            b.r = {}
        sem = self.sem[key]
        self.q[qn].append(lambda e, out=out, in_=in_, sem=sem, kw=kw: e.dma_start(out=out, in_=in_, **kw).then_inc(sem, 16))
        self.ninstr += 1

    def barrier(self):
        for e in ENGS:
            if e != "sp":
                self._wait("sp", ("e", e), self.cnt[e])
        for (qn, i), v in self.dcnt.items():
            if v > 0:
                self._wait("sp", ("d", qn, i), v)
        self.cnt["sp"] += 1
        val = self.cnt["sp"]
        sem = self.sem[("e", "sp")]
        self.q["sp"].append(lambda e, sem=sem: e.nop().then_inc(sem, 1))
        for e in ENGS:
            if e != "sp":
                self._wait(e, ("e", "sp"), val)

    def emit(self, block):
        nc = self.nc

        def run(e, lst):
            for f in lst:
                f(e)

        block.tensor(lambda e: run(e, self.q["pe"]))
        block.scalar(lambda e: run(e, self.q["act"]))
        block.vector(lambda e: run(e, self.q["dve"]))
        block.gpsimd(lambda e: run(e, self.q["pool"]))
        block.sync(lambda e: run(e, self.q["sp"]))


class Arena:
    def __init__(self, nc, nbytes):
        self.t = nc.alloc_sbuf_tensor("arena", [128, nbytes // 2], BF16)
        self.off = 0
        self.cap = nbytes
        self.n = 0
        self.peak = 0
        self.record = None
        self.replay = None

    def alloc(self, free_shape, dtype, name=None):
        if self.replay:
            return self.replay.pop(0)
        esz = 4 if dtype == F32 else 2
        n = prod(free_shape)
        nbytes = n * esz
        self.off = (self.off + 63) // 64 * 64
        o = self.off
        self.off += nbytes
        self.peak = max(self.peak, self.off)
        assert self.off <= self.cap, "SBUF arena overflow: %d > %d (%s)" % (self.off, self.cap, name)
        ap = self.t[:, o // 2: o // 2 + nbytes // 2]
        if dtype == F32:
            ap = ap.bitcast(F32)
        if len(free_shape) > 1:
            names = ["d%d" % i for i in range(len(free_shape))]
            kw = {nm: int(s) for nm, s in zip(names, free_shape)}
            ap = ap.rearrange("p (%s) -> p %s" % (" ".join(names), " ".join(names)), **kw)
        self.n += 1
        t_ = T(ap, Buf(name or ("t%d" % self.n)))
        if self.record is not None:
            self.record.append(t_)
        return t_

    def mark(self):
        return self.off

    def reset(self, m):
        self.off = m


def fv(ap, pattern, off=0):
    return AP(tensor=ap.tensor, offset=ap.offset + off, ap=[list(ap.ap[0])] + [list(p) for p in pattern])


class Prog:
    def __init__(self, LS, NL, dbg=()):
        self.LS = LS
        self.NL = NL
        self.NT = LS + 2 * LP
        self.NTILE = self.NT // 512
        self.NS = LS // 512
        self.NKEY = LS + PAST + 2 * LP
        self.NB = self.NT // 8
        self.NBS = LS // 8
        self.dbg = set(dbg)
        self.SL, self.NSP = small_layout(NL)

    def declare(self, nc):
        NL, NT, LS, NKEY = self.NL, self.NT, self.LS, self.NKEY

        def inp(name, shape, dt=F32):
            return nc.dram_tensor(name, list(shape), dt, kind="ExternalInput").ap()

        def outp(name, shape, dt=F32):
            return nc.dram_tensor(name, list(shape), dt, kind="ExternalOutput").ap()

        def scr(name, shape, dt):
            kind = "ExternalOutput" if name in self.dbg else "Internal"
            return nc.dram_tensor(name, list(shape), dt, kind=kind).ap()

        d = {}
        d['xT'] = inp('xT', [D, NT])
        d['smallp'] = inp('smallp', [128, self.NSP])
        d['ssmbc'] = inp('ssmbc', [NL, 128, 4, 256])
        d['cbf'] = inp('cbf', [128, 640], BF16)
        d['ropeG'] = inp('ropeG', [128, 2, LS])
        d['ropeM'] = inp('ropeM', [128, 2, LS])
        d['cK'] = inp('cK', [NL, 128, PAST])
        d['cV'] = inp('cV', [NL, PAST, 128])
        d['cCKV'] = inp('cCKV', [NL, 128, PAST])
        d['cKR'] = inp('cKR', [NL, 32, PAST])
        d['w_ada'] = inp('w_ada', [NL, D, 6 * D])
        d['w_in'] = inp('w_in', [NL, D, INW])
        d['w_uq'] = inp('w_uq', [NL, 256, 576])
        d['w_uk'] = inp('w_uk', [NL, 128, 384])
        d['w_uv'] = inp('w_uv', [NL, 128, 384])
        d['w_glu'] = inp('w_glu', [NL, 256, 256])
        d['w_out'] = inp('w_out', [NL, D, D])
        d['w_f1'] = inp('w_f1', [NL, D, 2 * DFF])
        d['w_f2'] = inp('w_f2', [NL, DFF, D])
        d['yT'] = outp('yT', [D, NT])
        d['o_k'] = outp('o_k', [NL, 128, 512])
        d['o_v'] = outp('o_v', [NL, 512, 128])
        d['o_ckv'] = outp('o_ckv', [NL, 128, 512])
        d['o_kr'] = outp('o_kr', [NL, 32, 512])
        d['o_ss'] = outp('o_ss', [NL, 128, 2, 2, NQ])
        d['qa'] = scr('qa', [3, 128, NT], BF16)
        d['ka'] = scr('ka', [128, NKEY], BF16)
        d['va'] = scr('va', [NT, 128], BF16)
        d['qb'] = scr('qb', [6, 96, NT], BF16)
        d['kb'] = scr('kb', [6, 96, NKEY], BF16)
        d['vb'] = scr('vb', [NKEY, 384], BF16)
        d['mix'] = scr('mix', [8, 128, NT], BF16)
        d['xm'] = scr('xm', [D, NT], F32)
        d['xs0'] = scr('xs0', [D, NT], F32)
        d['xs1'] = scr('xs1', [D, NT], F32)
        d['w1s'] = scr('w1s', [FT // 2, 128, 8, 2, 256], BF16)
        d['w2s'] = scr('w2s', [8, 128, FT, 128], BF16)
        self.d = d

    def mm(self, out, lhsT, rhs, start, stop, R, W, inc=True):
        self.S.op("pe", lambda e: e.matmul(out, lhsT, rhs, start=start, stop=stop), R, W, inc)

    def tr(self, out, in_, ident, R, W, inc=True):
        self.S.op("pe", lambda e: e.transpose(out, in_, ident), R, W, inc)

    def act(self, out, in_, func, R, W, bias=None, scale=None):
        kw = {}
        if bias is not None:
            kw['bias'] = bias
        if scale is not None:
            kw['scale'] = scale
        self.S.op("act", lambda e: e.activation(out, in_, func, **kw), R, W)

    def tt(self, out, in0, in1, op, R, W, eng="dve"):
        self.S.op(eng, lambda e: e.tensor_tensor(out, in0, in1, op), R, W)

    def ts(self, out, in0, s1, s2, op0, op1, R, W, eng="dve"):
        if s2 is None:
            self.S.op(eng, lambda e: e.tensor_scalar(out, in0, s1, None, op0), R, W)
        else:
            self.S.op(eng, lambda e: e.tensor_scalar(out, in0, s1, s2, op0, op1), R, W)

    def stt(self, out, in0, scalar, in1, op0, op1, R, W):
        self.S.op("dve", lambda e: e.scalar_tensor_tensor(out, in0, scalar, in1, op0, op1), R, W)

    def cp(self, out, in_, R, W, eng="dve"):
        if eng == "act":
            self.S.op("act", lambda e: e.activation(out, in_, AF.Copy), R, W)
        else:
            self.S.op(eng, lambda e: e.tensor_copy(out, in_), R, W)

    def recip(self, out, in_, R, W):
        self.S.op("dve", lambda e: e.reciprocal(out, in_), R, W)

    def memset(self, out, val, W, eng="dve"):
        self.S.op(eng, lambda e: e.memset(out, val), (), W)

    def ld(self, out_t, in_ap, R=(), q="sp", out_ap=None, **kw):
        self.S.dma(q, out_t.ap if out_ap is None else out_ap, in_ap, R=R, W=[out_t.b], **kw)

    def st(self, out_ap, in_t, W=(), q="pool", in_ap=None, **kw):
        self.S.dma(q, out_ap, in_t.ap if in_ap is None else in_ap, R=[in_t.b], W=W, **kw)

    def psn(self):
        t = self.PS[self.psi % 8]
        self.psi += 1
        return t

    def sm(self, name):
        off, shape = self.SL[name]
        ap = self.SM.ap[:, off:off + prod(shape)]
        if len(shape) > 1:
            names = ["d%d" % i for i in range(len(shape))]
            kw = {nm: int(s) for nm, s in zip(names, shape)}
            ap = ap.rearrange("p (%s) -> p %s" % (" ".join(names), " ".join(names)), **kw)
        return ap

    def build(self):
        from contextlib import ExitStack
        nc = bass.Bass("TRN2", target_bir_lowering=False)
        self.nc = nc
        self.declare(nc)
        stack = ExitStack()
        with stack:
            self.S = Sched(nc, stack)
            self.A = Arena(nc, 211000)
            self.PS = []
            self.PST = []
            for i in range(4):
                pt = nc.alloc_psum_tensor("psum%d" % i, [128, 1024], F32)
                self.PST.append(pt[:, :])
                for j in range(2):
                    self.PS.append(T(pt[:, j * 512:(j + 1) * 512], Buf("ps%d" % (2 * i + j))))
            self.psi = 0
            self.dbufs = {}
            self.body()
            self.S.barrier()
            block = stack.enter_context(nc.Block())
            self.S.emit(block)
        return nc

    def dbuf(self, name):
        return Buf(name)

    def body(self):
        A, S, d = self.A, self.S, self.d
        NL = self.NL
        self.SM = A.alloc([self.NSP], F32, "SM")
        self.CB = A.alloc([640], BF16, "CB")
        self.MOD = A.alloc([NL, 6, 8, 2], F32, "MOD")
        self.ld(self.SM, d['smallp'])
        self.ld(self.CB, d['cbf'])
        cb = self.CB.ap
        self.ident = cb[:, 0:128]
        self.ones = cb[:, 128:256]
        self.bd2 = cb[:, 256:384]
        self.rotG = cb[:, 384:512]
        self.rotM = cb[:, 512:640]
        self.eps = A.alloc([1], F32, "eps")
        self.memset(self.eps.ap, EPS, [self.eps.b])
        self.phase0()
        S.barrier()
        xcur = d['xT']
        for l in range(NL):
            self.l = l
            last = (l == NL - 1)
            xnext = d['yT'] if last else d['xs%d' % (l % 2)]
            m0 = A.mark()
            self.UT = A.alloc([16, self.NB], BF16, "Utilde")
            S.dry = True
            psi0 = self.psi
            self.s5rec = []
            for _ in self.s5_build():
                pass
            S.dry = False
            self.psi = psi0
            self.s5gen = self.s5_build()
            self.pump_n = 1
            mU = A.mark()
            self.phaseP(xcur)
            self.pump(100000)
            assert not self.s5rec
            S.barrier()
            A.reset(mU)
            if 'stopP' in self.dbg:
                return
            self.phaseS()
            S.barrier()
            A.reset(m0)
            if 'stopS' in self.dbg:
                return
            self.phaseGA()
            S.barrier()
            A.reset(m0)
            self.phaseMA()
            S.barrier()
            A.reset(m0)
            if 'stopA' in self.dbg:
                return
            self.phaseO(xcur)
            S.barrier()
            A.reset(m0)
            self.phaseF(xnext, last)
            S.barrier()
            A.reset(m0)
            xcur = xnext

    def phase0(self):
        A, S, d = self.A, self.S, self.d
        NL = self.NL
        m0 = A.mark()
        silc = A.alloc([16], BF16, "silc")
        condf = self.sm('cond')
        self.act(silc.ap, fv(condf, [[1, 16]]), AF.Silu, [self.SM.b], [silc.b])
        WA = [A.alloc([8, 1024], BF16, "wa%d" % i) for i in range(2)]
        MOD = self.MOD
        bada = self.sm('bada')
        for l in range(NL):
            wv = d['w_ada'][l].rearrange("(kt p) n -> p kt n", p=128)
            for j in range(6):
                wb = WA[(l * 6 + j) % 2]
                self.ld(wb, wv[:, :, j * 1024:(j + 1) * 1024], q="pool")
                ps = self.psn()
                for mt in range(8):
                    for kt in range(8):
                        self.mm(ps.ap[:, mt * 2:mt * 2 + 2], wb.ap[:, kt, mt * 128:(mt + 1) * 128],
                                silc.ap[:, kt * 2:kt * 2 + 2], kt == 0, kt == 7,
                                [wb.b, silc.b], [ps.b], inc=(mt == 7 and kt == 7))
                bsl = bada[:, l, j * 8:(j + 1) * 8]
                self.tt(MOD.ap[:, l, j], fv(ps.ap, [[2, 8], [1, 2]]), fv(bsl, [[1, 8], [0, 2]]), ALU.add,
                        [ps.b, self.SM.b], [MOD.b])
        n1g = self.sm('n1g')
        n2g = self.sm('n2g')
        for l in range(NL):
            self.stt(MOD.ap[:, l, 1], MOD.ap[:, l, 1], 1.0, fv(n1g[:, l, :], [[1, 8], [0, 2]]), ALU.add, ALU.mult,
                     [MOD.b, self.SM.b], [MOD.b])
            self.stt(MOD.ap[:, l, 4], MOD.ap[:, l, 4], 1.0, fv(n2g[:, l, :], [[1, 8], [0, 2]]), ALU.add, ALU.mult,
                     [MOD.b, self.SM.b], [MOD.b])
        if 'MOD' in self.dbg:
            dm = self.nc.dram_tensor('dbgMOD', [128, NL * 96], F32, kind="ExternalOutput").ap()
            self.st(dm, MOD, in_ap=fv(MOD.ap, [[1, NL * 96]]))
        S.barrier()
        A.reset(m0)

    def norm_tile(self, xt, h, jS, jB, c, sq, rstd, tmp):
        l = self.l
        MOD = self.MOD
        self.act(sq.ap, xt.ap, AF.Square, [xt.b], [sq.b])
        ps = self.psn()
        for kt in range(8):
            self.mm(ps.ap, self.ones, sq.ap[:, kt, :], kt == 0, kt == 7, [sq.b, self.CB.b], [ps.b], inc=(kt == 7))
        self.act(rstd.ap, ps.ap, AF.Sqrt, [ps.b, self.eps.b], [rstd.b], bias=self.eps.ap, scale=1.0 / D)
        self.recip(rstd.ap, rstd.ap, [rstd.b], [rstd.b])
        self.tt(tmp.ap, xt.ap, fv(rstd.ap, [[0, 8], [1, 512]]), ALU.mult, [xt.b, rstd.b], [tmp.b])
        for kt in range(8):
            self.act(h.ap[:, kt, :], tmp.ap[:, kt, :], AF.Identity, [tmp.b, MOD.b], [h.b],
                     bias=MOD.ap[:, l, jB, kt, c:c + 1], scale=MOD.ap[:, l, jS, kt, c:c + 1])

    def headnorm(self, ps, onesmat, inv_n, gain_ap, out_f32, sq, rstd):
        self.act(sq.ap, ps.ap, AF.Square, [ps.b], [sq.b])
        ps2 = self.psn()
        self.mm(ps2.ap, onesmat, sq.ap, True, True, [sq.b, self.CB.b], [ps2.b])
        self.act(rstd.ap, ps2.ap, AF.Sqrt, [ps2.b, self.eps.b], [rstd.b], bias=self.eps.ap, scale=inv_n)
        self.recip(rstd.ap, rstd.ap, [rstd.b], [rstd.b])
        self.stt(out_f32.ap, ps.ap, gain_ap, rstd.ap, ALU.mult, ALU.mult, [ps.b, rstd.b, self.SM.b], [out_f32.b])

    def rope(self, qn, rotmat, cos_ap, sin_ap, out_bf, qnb, t1, t2, rows=128):
        r = slice(0, rows)
        self.cp(qnb.ap[r], qn.ap[r], [qn.b], [qnb.b], eng="act")
        ps = self.psn()
        self.mm(ps.ap[r], rotmat[r, 0:rows], qnb.ap[r], True, True, [qnb.b, self.CB.b], [ps.b])
        self.tt(t1.ap[r], qn.ap[r], cos_ap[r], ALU.mult, [qn.b, self.RT.b], [t1.b], eng="pool")
        self.tt(t2.ap[r], ps.ap[r], sin_ap[r], ALU.mult, [ps.b, self.RT.b], [t2.b])
        self.tt(out_bf.ap[r], t1.ap[r], t2.ap[r], ALU.add, [t1.b, t2.b], [out_bf.b])

    def phaseP(self, xcur):
        A, S, d = self.A, self.S, self.d
        l, LS, NS, NT = self.l, self.LS, self.NS, self.NT
        xv = xcur.rearrange("(kt p) n -> p kt n", p=128)
        WIN = A.alloc([8, INW], BF16, "WIN")
        wv = d['w_in'][l].rearrange("(kt p) n -> p kt n", p=128)
        for i in range(3):
            for s, hh in enumerate((i, i + 3)):
                self.ld(WIN, wv[:, :, hh * 64:(hh + 1) * 64], q="pool", out_ap=WIN.ap[:, :, i * 128 + s * 64:i * 128 + (s + 1) * 64])
        self.ld(WIN, wv[:, :, 384:INW], q="pool", out_ap=WIN.ap[:, :, 384:INW])
        WUQN = A.alloc([2, 384], BF16, "WUQN")
        WUQR = A.alloc([2, 192], BF16, "WUQR")
        uqv = d['w_uq'][l].rearrange("(i p) (h e) -> p i h e", p=128, e=96)
        for i in range(2):
            self.ld(WUQN, uqv[:, i, :, 0:64], q="pool", out_ap=fv(WUQN.ap[:, i, :], [[64, 6], [1, 64]]))
            self.ld(WUQR, uqv[:, i, :, 64:96], q="pool", out_ap=fv(WUQR.ap[:, i, :], [[32, 6], [1, 32]]))
        WUK = A.alloc([384], BF16, "WUK")
        WUV = A.alloc([384], BF16, "WUV")
        self.ld(WUK, d['w_uk'][l], q="pool")
        self.ld(WUV, d['w_uv'][l], q="pool")
        XT = [A.alloc([8, 512], F32, "xt%d" % i) for i in range(1)]
        H = [A.alloc([8, 512], BF16, "h%d" % i) for i in range(2)]
        sq8 = A.alloc([8, 512], BF16, "sq8")
        tmp8 = A.alloc([8, 512], F32, "tmp8")
        rstd = A.alloc([512], F32, "rstd")
        self.RT = A.alloc([2, 2, 512], F32, "ropetab")
        sqh = [A.alloc([512], BF16, "sqh%d" % i) for i in range(2)]
        rsh = [A.alloc([512], F32, "rsh%d" % i) for i in range(2)]
        qn = [A.alloc([512], F32, "qn%d" % i) for i in range(2)]
        qnb = [A.alloc([512], BF16, "qnb%d" % i) for i in range(2)]
        t1 = [A.alloc([512], F32, "t1%d" % i) for i in range(1)]
        t2 = [A.alloc([512], F32, "t2%d" % i) for i in range(1)]
        ob = [A.alloc([512], BF16, "ob%d" % i) for i in range(4)]
        vt = A.alloc([4, 128], BF16, "vt")
        vtf = A.alloc([4, 128], F32, "vtf")
        cqn = A.alloc([2, 512], BF16, "cqn")
        sq2 = A.alloc([2, 512], BF16, "sq2")
        ckvb = A.alloc([512], BF16, "ckvb")
        vbt = A.alloc([4, 384], BF16, "vbt")
        utm = A.alloc([16, 8, 16], BF16, "utm")
        self.rr = 0

        def nxt(lst):
            self.rr += 1
            return lst[self.rr % len(lst)]

        for t in range(self.NTILE):
            prm = (t == NS)
            c = 1 if prm else 0
            tok = slice(t * 512, (t + 1) * 512)
            kcol = slice(t * 512, (t + 1) * 512) if not prm else slice(LS + PAST, LS + PAST + 512)
            xt = XT[0]
            h = H[t % 2]
            self.ld(xt, xv[:, :, tok])
            if not prm:
                self.ld(self.RT, d['ropeG'][:, :, tok], out_ap=self.RT.ap[:, 0])
                self.ld(self.RT, d['ropeM'][:, :, tok], out_ap=self.RT.ap[:, 1])
            self.norm_tile(xt, h, 1, 0, c, sq8, rstd, tmp8)

            def proj(col0, ncol, rows=128):
                ps = self.psn()
                for kt in range(8):
                    self.mm(ps.ap[0:rows], WIN.ap[:, kt, col0:col0 + ncol], h.ap[:, kt, :], kt == 0, kt == 7,
                            [WIN.b, h.b], [ps.b], inc=(kt == 7))
                return ps

            def gqa_gen(i):
                isk = (i == 3)
                bi = i % 2
                ps = proj(i * 128, 128)
                yield
                q_n, sq_, rs_ = qn[bi], sqh[bi], rsh[bi]
                gain = self.sm('kn' if isk else 'qn')[:, l:l + 1]
                self.act(sq_.ap, ps.ap, AF.Square, [ps.b], [sq_.b])
                yield
                ps2 = self.psn()
                self.mm(ps2.ap, self.bd2, sq_.ap, True, True, [sq_.b, self.CB.b], [ps2.b])
                yield
                self.act(rs_.ap, ps2.ap, AF.Sqrt, [ps2.b, self.eps.b], [rs_.b], bias=self.eps.ap, scale=1.0 / 64)
                yield
                self.recip(rs_.ap, rs_.ap, [rs_.b], [rs_.b])
                yield
                self.stt(q_n.ap, ps.ap, gain, rs_.ap, ALU.mult, ALU.mult, [ps.b, rs_.b, self.SM.b], [q_n.b])
                yield
                o = ob[i]
                if prm:
                    self.cp(o.ap, q_n.ap, [q_n.b], [o.b], eng="act")
                    if isk:
                        self.st(d['o_k'][l], q_n, W=[self.dbuf('o_k')])
                else:
                    self.rope(q_n, self.rotG, self.RT.ap[:, 0, 0], self.RT.ap[:, 0, 1], o, qnb[bi], t1[0], t2[0])
                if isk:
                    self.st(d['ka'][:, kcol], o, W=[self.dbuf('ka')])
                else:
                    self.st(d['qa'][i, :, tok], o, W=[self.dbuf('qa')])

            for pair in ((0, 1), (2, 3)):
                gens = [gqa_gen(i) for i in pair]
                while gens:
                    for g_ in list(gens):
                        try:
                            next(g_)
                        except StopIteration:
                            gens.remove(g_)
                self.pump(2 * self.pump_n)
            ps = self.psn()
            for s in range(4):
                for kt in range(8):
                    self.mm(ps.ap[:, s * 128:(s + 1) * 128], h.ap[:, kt, s * 128:(s + 1) * 128], WIN.ap[:, kt, 512:640],
                            kt == 0, kt == 7, [WIN.b, h.b], [ps.b], inc=(s == 3 and kt == 7))
            self.cp(fv(vt.ap, [[1, 512]]), ps.ap, [ps.b], [vt.b], eng="act")
            self.st(d['va'][tok, :].rearrange("(s p) c -> p s c", p=128), vt, W=[self.dbuf('va')])
            if prm:
                self.cp(fv(vtf.ap, [[1, 512]]), ps.ap, [ps.b], [vtf.b])
                self.st(d['o_v'][l].rearrange("(s p) c -> p s c", p=128), vtf, W=[self.dbuf('o_v')])
            self.pump(self.pump_n)
            psc = [proj(640, 128), proj(768, 128)]
            for i in range(2):
                self.act(sq2.ap[:, i, :], psc[i].ap, AF.Square, [psc[i].b], [sq2.b])
            ps2 = self.psn()
            for i in range(2):
                self.mm(ps2.ap, self.ones, sq2.ap[:, i, :], i == 0, i == 1, [sq2.b, self.CB.b], [ps2.b], inc=(i == 1))
            rs = nxt(rsh)
            self.act(rs.ap, ps2.ap, AF.Sqrt, [ps2.b, self.eps.b], [rs.b], bias=self.eps.ap, scale=1.0 / 256)
            self.recip(rs.ap, rs.ap, [rs.b], [rs.b])
            mqn = self.sm('mqn')
            for i in range(2):
                self.stt(cqn.ap[:, i, :], psc[i].ap, mqn[:, l, i:i + 1], rs.ap, ALU.mult, ALU.mult,
                         [psc[i].b, rs.b, self.SM.b], [cqn.b])
            for pr in range(3):
                ps = self.psn()
                for i in range(2):
                    self.mm(ps.ap, WUQN.ap[:, i, pr * 128:(pr + 1) * 128], cqn.ap[:, i, :], i == 0, i == 1,
                            [WUQN.b, cqn.b], [ps.b], inc=(i == 1))
                o = nxt(ob)
                self.cp(o.ap, ps.ap, [ps.b], [o.b], eng="act")
                for s in range(2):
                    self.st(d['qb'][2 * pr + s, 0:64, tok], o, W=[self.dbuf('qb')], in_ap=o.ap[s * 64:(s + 1) * 64])
            for (c0, nh) in ((0, 4), (128, 2)):
                rows = nh * 32
                ps = self.psn()
                for i in range(2):
                    self.mm(ps.ap[0:rows], WUQR.ap[:, i, c0:c0 + rows], cqn.ap[:, i, :], i == 0, i == 1,
                            [WUQR.b, cqn.b], [ps.b], inc=(i == 1))
                o = nxt(ob)
                if prm:
                    self.cp(o.ap[0:rows], ps.ap[0:rows], [ps.b], [o.b], eng="act")
                else:
                    q_n = nxt(qn)
                    self.cp(q_n.ap[0:rows], ps.ap[0:rows], [ps.b], [q_n.b])
                    self.rope(q_n, self.rotM, self.RT.ap[:, 1, 0], self.RT.ap[:, 1, 1], o, nxt(qnb), nxt(t1), nxt(t2), rows=rows)
                for s in range(nh):
                    hh = c0 // 32 + s
                    self.st(d['qb'][hh, 64:96, tok], o, W=[self.dbuf('qb')], in_ap=o.ap[s * 32:(s + 1) * 32])
                self.pump(self.pump_n)
            ps = proj(896, 128)
            ck = nxt(qn)
            self.headnorm(ps, self.ones, 1.0 / 128, self.sm('mkvn')[:, l:l + 1], ck, nxt(sqh), nxt(rsh))
            if prm:
                self.st(d['o_ckv'][l], ck, W=[self.dbuf('o_ckv')])
            self.cp(ckvb.ap, ck.ap, [ck.b], [ckvb.b], eng="act")
            self.mla_kv(ckvb, kcol, WUK, WUV, ob, vbt, nxt)
            self.pump(self.pump_n)
            ps = proj(1024, 32, rows=32)
            o = nxt(ob)
            if prm:
                q_n = nxt(qn)
                self.cp(q_n.ap[0:32], ps.ap[0:32], [ps.b], [q_n.b])
                self.st(d['o_kr'][l], q_n, W=[self.dbuf('o_kr')], in_ap=q_n.ap[0:32])
                self.cp(o.ap[0:32], q_n.ap[0:32], [q_n.b], [o.b], eng="act")
            else:
                q_n = nxt(qn)
                self.cp(q_n.ap[0:32], ps.ap[0:32], [ps.b], [q_n.b])
                self.rope(q_n, self.rotM, self.RT.ap[:, 1, 0], self.RT.ap[:, 1, 1], o, nxt(qnb), nxt(t1), nxt(t2), rows=32)
            for hh in range(6):
                self.st(d['kb'][hh, 64:96, kcol], o, W=[self.dbuf('kb')], in_ap=o.ap[0:32])
            for half in range(2):
                psu = [self.psn() for _ in range(2)]
                for jj in range(4):
                    j = half * 4 + jj
                    pst = psu[jj // 2]
                    for kt in range(8):
                        self.mm(pst.ap[0:64, (jj % 2) * 256:(jj % 2) * 256 + 256], fv(h.ap[:, kt, :], [[8, 64]], off=j),
                                WIN.ap[:, kt, 1056:1312], kt == 0, kt == 7, [WIN.b, h.b], [pst.b], inc=(kt == 7))
                    self.cp(utm.ap[0:64, :, j, :], fv(pst.ap[0:64], [[16, 16], [1, 16]], off=(jj % 2) * 256), [pst.b], [utm.b],
                            eng=("act" if jj % 2 else "dve"))
            pT = self.psn()
            pTb = pT.ap.bitcast(BF16)
            for g in range(16):
                self.tr(pTb[:, g * 64:(g + 1) * 64], fv(utm.ap[0:64, g], [[1, 128]]), self.ident[0:64, 0:64],
                        [utm.b, self.CB.b], [pT.b], inc=(g == 15))
            self.cp(self.UT.ap[:, :, t * 64:(t + 1) * 64], fv(pTb, [[64, 16], [1, 64]]), [pT.b], [self.UT.b])
            self.pump(self.pump_n)
        self.ld(ckvb, d['cCKV'][l], q="pool")
        kcol = slice(LS, LS + PAST)
        self.mla_kv(ckvb, kcol, WUK, WUV, ob, vbt, nxt)
        o = nxt(ob)
        self.ld(o, d['cKR'][l], q="pool", out_ap=o.ap[0:32])
        for hh in range(6):
            self.st(d['kb'][hh, 64:96, kcol], o, W=[self.dbuf('kb')], in_ap=o.ap[0:32])

    def mla_kv(self, ckvb, kcol, WUK, WUV, ob, vbt, nxt):
        d = self.d
        for pr in range(3):
            ps = self.psn()
            self.mm(ps.ap, WUK.ap[:, pr * 128:(pr + 1) * 128], ckvb.ap, True, True, [WUK.b, ckvb.b], [ps.b])
            o = nxt(ob)
            self.cp(o.ap, ps.ap, [ps.b], [o.b], eng="act")
            for s in range(2):
                self.st(d['kb'][2 * pr + s, 0:64, kcol], o, W=[self.dbuf('kb')], in_ap=o.ap[s * 64:(s + 1) * 64])
        for s in range(4):
            ps = self.psn()
            self.mm(ps.ap[:, 0:384], ckvb.ap[:, s * 128:(s + 1) * 128], WUV.ap, True, True, [WUV.b, ckvb.b], [ps.b])
            self.cp(vbt.ap[:, s, :], ps.ap[:, 0:384], [ps.b], [vbt.b], eng=("act" if s % 2 else "dve"))
        self.st(d['vb'][kcol, :].rearrange("(s p) c -> p s c", p=128), vbt, W=[self.dbuf('vb')])

    def s5_build(self):
        A, S, d = self.A, self.S, self.d
        l, LS, NS, NT, NB, NBS = self.l, self.LS, self.NS, self.NT, self.NB, self.NBS
        SM = self.SM
        UT = self.UT
        pb = Buf("s5prm")

        def sc(name, n=16):
            t = self.s5a([n], F32, name)
            t.b = pb
            return t
        lre = self.sm('lre')[:, l, :]
        lim = self.sm('lim')[:, l, :]
        ldt = self.sm('ldt')[:, l, :]
        Rp = [pb, SM.b]
        Wp = [pb]
        dt = sc("dt"); th = sc("th"); er = sc("er"); cc = sc("cc"); ss = sc("ss"); cs = sc("cs")
        c_ = sc("c"); s_ = sc("s"); dec = sc("dec"); are = sc("are"); aim = sc("aim")
        den = sc("den"); nre = sc("nre"); cre = sc("cre"); cim = sc("cim"); tA = sc("tA"); tB = sc("tB")
        rho = sc("rho"); rrho = sc("rrho")
        PWr = sc("PWr", 9 * 16); PWi = sc("PWi", 9 * 16)
        WKr = sc("WKr", 9 * 16); WKi = sc("WKi", 9 * 16)
        pw = lambda t, k: t.ap[:, k * 16:(k + 1) * 16]
        self.act(dt.ap, ldt, AF.Exp, Rp, Wp)
        self.tt(th.ap, lim, dt.ap, ALU.mult, Rp, Wp)
        self.tt(er.ap, lre, dt.ap, ALU.mult, Rp, Wp)
        self.act(s_.ap, th.ap, AF.Sin, Rp, Wp, scale=0.125)
        yield
        self.act(tA.ap, th.ap, AF.Sin, Rp, Wp, scale=0.0625)
        self.tt(tA.ap, tA.ap, tA.ap, ALU.mult, Rp, Wp)
        self.ts(c_.ap, tA.ap, -2.0, 1.0, ALU.mult, ALU.add, Rp, Wp)
        for _ in range(3):
            self.tt(cc.ap, c_.ap, c_.ap, ALU.mult, Rp, Wp)
            yield
            self.tt(ss.ap, s_.ap, s_.ap, ALU.mult, Rp, Wp)
            self.tt(cs.ap, c_.ap, s_.ap, ALU.mult, Rp, Wp)
            self.tt(c_.ap, cc.ap, ss.ap, ALU.subtract, Rp, Wp)
            self.ts(s_.ap, cs.ap, 2.0, None, ALU.mult, None, Rp, Wp)
            yield
        self.act(dec.ap, er.ap, AF.Exp, Rp, Wp)
        self.act(rho.ap, er.ap, AF.Exp, Rp, Wp, scale=8.0)
        self.act(rrho.ap, er.ap, AF.Exp, Rp, Wp, scale=-8.0)
        self.tt(are.ap, dec.ap, c_.ap, ALU.mult, Rp, Wp)
        yield
        self.tt(aim.ap, dec.ap, s_.ap, ALU.mult, Rp, Wp)
        self.tt(tA.ap, lre, lre, ALU.mult, Rp, Wp)
        self.tt(tB.ap, lim, lim, ALU.mult, Rp, Wp)
        self.tt(den.ap, tA.ap, tB.ap, ALU.add, Rp, Wp)
        yield
        self.recip(den.ap, den.ap, Rp, Wp)
        self.ts(nre.ap, are.ap, -1.0, None, ALU.add, None, Rp, Wp)
        self.tt(tA.ap, nre.ap, lre, ALU.mult, Rp, Wp)
        self.tt(tB.ap, aim.ap, lim, ALU.mult, Rp, Wp)
        yield
        self.tt(tA.ap, tA.ap, tB.ap, ALU.add, Rp, Wp)
        self.tt(cre.ap, tA.ap, den.ap, ALU.mult, Rp, Wp)
        self.tt(tA.ap, aim.ap, lre, ALU.mult, Rp, Wp)
        self.tt(tB.ap, nre.ap, lim, ALU.mult, Rp, Wp)
        yield
        self.tt(tA.ap, tA.ap, tB.ap, ALU.subtract, Rp, Wp)
        self.tt(cim.ap, tA.ap, den.ap, ALU.mult, Rp, Wp)

        def cmul(or_, oi_, ar, ai, br, bi):
            self.tt(tA.ap, ar, br, ALU.mult, Rp, Wp)
            self.tt(tB.ap, ai, bi, ALU.mult, Rp, Wp)
            self.tt(cc.ap, ar, bi, ALU.mult, Rp, Wp)
            self.tt(ss.ap, ai, br, ALU.mult, Rp, Wp)
            self.tt(or_, tA.ap, tB.ap, ALU.subtract, Rp, Wp)
            self.tt(oi_, cc.ap, ss.ap, ALU.add, Rp, Wp)
        self.memset(pw(PWr, 0), 1.0, Wp)
        self.memset(pw(PWi, 0), 0.0, Wp)
        for k in range(1, 9):
            cmul(pw(PWr, k), pw(PWi, k), pw(PWr, k - 1), pw(PWi, k - 1), are.ap, aim.ap)
        self.tt(pw(WKr, 0), pw(PWr, 8), rrho.ap, ALU.mult, Rp, Wp)
        self.tt(pw(WKi, 0), pw(PWi, 8), rrho.ap, ALU.mult, Rp, Wp)
        yield
        for k in range(1, 9):
            cmul(pw(WKr, k), pw(WKi, k), pw(WKr, k - 1), pw(WKi, k - 1), pw(WKr, k - 1), pw(WKi, k - 1))
        SB = self.s5a([4, 256], F32, "ssmbc")
        self.ld(SB, d['ssmbc'][l])
        wb = Buf("s5w")
        Rw = [pb, wb, SB.b]
        Ww = [wb]

        def wt(shape, dt_, name):
            t = self.s5a(shape, dt_, name)
            t.b = wb
            return t
        BBr = wt([16, 16], F32, "BBr"); BBi = wt([16, 16], F32, "BBi")
        u1 = wt([16, 16], F32, "u1"); u2 = wt([16, 16], F32, "u2")
        bc16 = lambda ap: fv(ap, [[1, 16], [0, 16]])
        b_re = SB.ap[:, 0].rearrange("p (q h) -> p q h", h=16)
        b_im = SB.ap[:, 1].rearrange("p (q h) -> p q h", h=16)
        c_re = SB.ap[:, 2].rearrange("p (q h) -> p q h", h=16)
        c_im = SB.ap[:, 3].rearrange("p (q h) -> p q h", h=16)
        self.tt(u1.ap, b_re, bc16(cre.ap), ALU.mult, Rw, Ww)
        self.tt(u2.ap, b_im, bc16(cim.ap), ALU.mult, Rw, Ww)
        self.tt(BBr.ap, u1.ap, u2.ap, ALU.subtract, Rw, Ww)
        yield
        self.tt(u1.ap, b_im, bc16(cre.ap), ALU.mult, Rw, Ww)
        self.tt(u2.ap, b_re, bc16(cim.ap), ALU.mult, Rw, Ww)
        self.tt(BBi.ap, u1.ap, u2.ap, ALU.add, Rw, Ww)
        XEr = wt([16, 15, 16], BF16, "XEr"); XEi = wt([16, 15, 16], BF16, "XEi")
        CAr = wt([16, 9, 16], BF16, "CAr"); CAi = wt([16, 9, 16], BF16, "CAi")
        Crb = wt([16, 16], BF16, "Crb"); Cib = wt([16, 16], BF16, "Cib")
        self.memset(fv(XEr.ap, [[1, 16 * 15 * 16]]), 0.0, Ww)
        yield
        self.memset(fv(XEi.ap, [[1, 16 * 15 * 16]]), 0.0, Ww)
        self.cp(Crb.ap, c_re, Rw, Ww)
        self.ts(Cib.ap, c_im, -1.0, None, ALU.mult, None, Rw, Ww)
        u3 = wt([16, 16], F32, "u3"); u4 = wt([16, 16], F32, "u4")
        for k in range(8):
            pr_b = bc16(pw(PWr, k)); pi_b = bc16(pw(PWi, k))
            self.tt(u1.ap, BBr.ap, pr_b, ALU.mult, Rw, Ww)
            yield
            self.tt(u2.ap, BBi.ap, pi_b, ALU.mult, Rw, Ww)
            self.tt(u3.ap, BBi.ap, pr_b, ALU.mult, Rw, Ww)
            self.tt(u4.ap, BBr.ap, pi_b, ALU.mult, Rw, Ww)
            for dr in range(2):
                m = 7 - k if dr == 0 else 7 + k
                qs = slice(dr * 8, dr * 8 + 8)
                self.tt(XEr.ap[:, qs, m, :], u1.ap[:, qs, :], u2.ap[:, qs, :], ALU.subtract, Rw, Ww)
                yield
                self.tt(XEi.ap[:, qs, m, :], u3.ap[:, qs, :], u4.ap[:, qs, :], ALU.add, Rw, Ww)
        for m in range(9):
            pr_b = bc16(pw(PWr, m)); pi_b = bc16(pw(PWi, m))
            self.tt(u1.ap, c_re, pr_b, ALU.mult, Rw, Ww)
            self.tt(u2.ap, c_im, pi_b, ALU.mult, Rw, Ww)
            self.tt(u3.ap, c_im, pr_b, ALU.mult, Rw, Ww)
            yield
            self.tt(u4.ap, c_re, pi_b, ALU.mult, Rw, Ww)
            for dr in range(2):
                qs = slice(dr * 8, dr * 8 + 8)
                mp = m if dr == 0 else 8 - m
                self.tt(CAr.ap[:, qs, mp, :], u1.ap[:, qs, :], u2.ap[:, qs, :], ALU.subtract, Rw, Ww)
                self.stt(CAi.ap[:, qs, mp, :], u3.ap[:, qs, :], -1.0, u4.ap[:, qs, :], ALU.mult, ALU.subtract, Rw, Ww)
        WEr = wt([16, 128], BF16, "WEr"); WEi = wt([16, 128], BF16, "WEi")
        for (XE, WE) in ((XEr, WEr), (XEi, WEi)):
            for hb in range(2):
                ps = self.psn()
                psb = ps.ap.bitcast(BF16)
                for qq in range(8):
                    q = hb * 8 + qq
                    m0 = 0 if q < 8 else 7
                    self.tr(psb[:, qq * 128:(qq + 1) * 128], fv(XE.ap[:, q, m0, :], [[1, 128]]), self.ident,
                            [wb, self.CB.b], [ps.b], inc=(qq == 7))
                self.cp(fv(WE.ap[:, hb * 8, :], [[1, 1024]]), psb, [ps.b], Ww)
                yield
        WL = wt([32, 128], BF16, "WLOC")
        drep = self.sm('drep')
        for dr in range(2):
            for kb in range(4):
                ps = self.psn()
                g2 = kb % 2
                rows = slice(g2 * 64, g2 * 64 + 64)
                for gi in range(4):
                    gp = 4 * (kb // 2) + gi
                    q = dr * 8 + gp
                    for j in range(8):
                        o_ = ps.ap[:, gi * 128 + j * 16: gi * 128 + (j + 1) * 16]
                        self.mm(o_, fv(XEr.ap[rows, q, 7 - j, :], [[1, 128]]), Crb.ap[rows, q, :], True, False, [wb], [ps.b], inc=False)
                        self.mm(o_, fv(XEi.ap[rows, q, 7 - j, :], [[1, 128]]), Cib.ap[rows, q, :], False, True, [wb], [ps.b],
                                inc=(gi == 3 and j == 7))
                g0 = 2 * (4 * (kb // 2)) + g2
                if dr == 0:
                    for gi in range(4):
                        g = g0 + 2 * gi
                        self.stt(WL.ap[:, g, :], self.ident, drep[:, l, g:g + 1], ps.ap[:, gi * 128:(gi + 1) * 128], ALU.mult, ALU.add,
                                 [ps.b, self.CB.b, SM.b], Ww)
                else:
                    self.cp(fv(WL.ap[:, 16 + g0, :], [[256, 4], [1, 128]]), fv(ps.ap, [[128, 4], [1, 128]]), [ps.b], Ww, eng="act")
                yield
        self.s5 = dict(pb=pb, wb=wb, WKr=WKr, WKi=WKi, rho=rho, WEr=WEr, WEi=WEi, CAr=CAr, CAi=CAi, WL=WL)
        yield

    def s5a(self, shape, dtype, name=None):
        if self.S.dry:
            t = self.A.alloc(shape, dtype, name)
            self.s5rec.append(t)
            return t
        return self.s5rec.pop(0)

    def pump(self, n):
        for _ in range(n):
            if self.s5gen is None:
                return
            try:
                next(self.s5gen)
            except StopIteration:
                self.s5gen = None


    def phaseS(self):
        A, S, d = self.A, self.S, self.d
        l, LS, NS, NT, NB, NBS = self.l, self.LS, self.NS, self.NT, self.NB, self.NBS
        SM = self.SM
        UT = self.UT
        s5 = self.s5
        pb, wb, WKr, WKi, rho = s5['pb'], s5['wb'], s5['WKr'], s5['WKi'], s5['rho']
        WEr, WEi, CAr, CAi, WL = s5['WEr'], s5['WEi'], s5['CAr'], s5['CAi'], s5['WL']
        pw = lambda t, k: t.ap[:, k * 16:(k + 1) * 16]
        drep = self.sm('drep')

        SNr = A.alloc([16, NB], BF16, "SINr"); SNi = A.alloc([16, NB], BF16, "SINi")
        FS = A.alloc([2, 2, 16], F32, "FS")
        self.memset(fv(SNr.ap, [[1, 16 * NB]]), 0.0, [SNr.b])
        self.memset(fv(SNi.ap, [[1, 16 * NB]]), 0.0, [SNi.b], eng="pool")
        Tc = A.alloc([512], F32, "Tc"); Ts = A.alloc([512], F32, "Ts"); Tt = A.alloc([256], F32, "Tt")
        TcL = A.alloc([NB], F32, "TcL"); TsL = A.alloc([NB], F32, "TsL")
        Zr = A.alloc([NB], F32, "Zr"); Zi = A.alloc([NB], F32, "Zi")
        Yr = A.alloc([NB], F32, "Yr"); Yi = A.alloc([NB], F32, "Yi")
        m1 = A.alloc([NB], F32, "m1"); m2 = A.alloc([NB], F32, "m2")
        m3 = A.alloc([NB], F32, "m3"); m4 = A.alloc([NB], F32, "m4")
        segs = [(0, NBS), (NBS, NBS + 32), (NBS + 32, NBS + 64)]
        s0r = self.sm('s0re')
        s0i = self.sm('s0im')
        for q in range(16):
            dr, gp = q // 8, q % 8
            self.cp(Tc.ap[:, 0:1], pw(WKr, 0)[:, q:q + 1], [pb], [Tc.b])
            self.cp(Ts.ap[:, 0:1], pw(WKi, 0)[:, q:q + 1], [pb], [Ts.b])
            for k in range(9):
                n = 1 << k
                wr = pw(WKr, k)[:, q:q + 1]
                wi = pw(WKi, k)[:, q:q + 1]
                self.ts(Tt.ap[:, 0:n], Ts.ap[:, 0:n], wi, None, ALU.mult, None, [Ts.b, pb], [Tt.b])
                self.stt(Tc.ap[:, n:2 * n], Tc.ap[:, 0:n], wr, Tt.ap[:, 0:n], ALU.mult, ALU.subtract, [Tc.b, Tt.b, pb], [Tc.b])
                self.ts(Tt.ap[:, 0:n], Tc.ap[:, 0:n], wi, None, ALU.mult, None, [Tc.b, pb], [Tt.b])
                self.stt(Ts.ap[:, n:2 * n], Ts.ap[:, 0:n], wr, Tt.ap[:, 0:n], ALU.mult, ALU.add, [Ts.b, Tt.b, pb], [Ts.b])
            for (a, b_) in segs:
                n = b_ - a
                for (Tx, TxL) in ((Tc, TcL), (Ts, TsL)):
                    if dr == 0:
                        self.cp(TxL.ap[:, a:b_], Tx.ap[:, 0:n], [Tx.b], [TxL.b])
                    else:
                        self.cp(TxL.ap[:, a:b_], fv(Tx.ap, [[-1, n]], off=n - 1), [Tx.b], [TxL.b])
            pi_ = (q % 2) * 2
            Er = self.PST[pi_][:, 0:NB]
            Ei = self.PST[pi_ + 1][:, 0:NB]
            bre = [self.PS[2 * pi_].b, self.PS[2 * pi_ + 1].b]
            bim = [self.PS[2 * pi_ + 2].b, self.PS[2 * pi_ + 3].b]
            for (E_, WE, bb) in ((Er, WEr, bre), (Ei, WEi, bim)):
                for g2 in range(2):
                    rows = slice(g2 * 64, g2 * 64 + 64)
                    chunks = [(c0, min(c0 + 512, NB)) for c0 in range(0, NB, 512)]
                    for ci, (c0, c1) in enumerate(chunks):
                        self.mm(E_[rows, c0:c1], WE.ap[:, q, g2 * 64:(g2 + 1) * 64], UT.ap[:, 2 * gp + g2, c0:c1], True, True,
                                [wb, UT.b], bb, inc=(g2 == 1 and ci == len(chunks) - 1))
            self.tt(m1.ap, Er, TcL.ap, ALU.mult, bre + [TcL.b], [m1.b])
            self.tt(m2.ap, Ei, TsL.ap, ALU.mult, bim + [TsL.b], [m2.b])
            self.tt(m3.ap, Ei, TcL.ap, ALU.mult, bim + [TcL.b], [m3.b])
            self.tt(m4.ap, Er, TsL.ap, ALU.mult, bre + [TsL.b], [m4.b])
            self.tt(Zr.ap, m1.ap, m2.ap, ALU.add, [m1.b, m2.b], [Zr.b], eng="pool")
            self.tt(Zi.ap, m3.ap, m4.ap, ALU.subtract, [m3.b, m4.b], [Zi.b], eng="pool")
            for si, (a, b_) in enumerate(segs):
                n = b_ - a
                for (Z, Y, s0) in ((Zr, Yr, s0r), (Zi, Yi, s0i)):
                    init = s0[:, l, q:q + 1] if si == 0 else 0.0
                    rb = rho.ap[:, q:q + 1].to_broadcast([128, n])
                    if dr == 0:
                        o_, i_ = Y.ap[:, a:b_], Z.ap[:, a:b_]
                    else:
                        o_, i_ = fv(Y.ap, [[-1, n]], off=b_ - 1), fv(Z.ap, [[-1, n]], off=b_ - 1)
                    S.op("dve", lambda e, o_=o_, rb=rb, i_=i_, init=init: e.tensor_tensor_scan(o_, rb, i_, init, ALU.mult, ALU.add),
                         [Z.b, pb, SM.b], [Y.b])
            self.tt(m1.ap, Yr.ap, TcL.ap, ALU.mult, [Yr.b, TcL.b], [m1.b])
            self.tt(m2.ap, Yi.ap, TsL.ap, ALU.mult, [Yi.b, TsL.b], [m2.b])
            self.tt(m3.ap, Yr.ap, TsL.ap, ALU.mult, [Yr.b, TsL.b], [m3.b], eng="pool")
            self.tt(m4.ap, Yi.ap, TcL.ap, ALU.mult, [Yi.b, TcL.b], [m4.b], eng="pool")
            for si, (a, b_) in enumerate(segs):
                if dr == 0:
                    osl, isl = slice(a + 1, b_), slice(a, b_ - 1)
                    s0col, fcol = a, b_ - 1
                else:
                    osl, isl = slice(a, b_ - 1), slice(a + 1, b_)
                    s0col, fcol = b_ - 1, a
                self.tt(SNr.ap[:, q, osl], m1.ap[:, isl], m2.ap[:, isl], ALU.subtract, [m1.b, m2.b], [SNr.b])
                self.tt(SNi.ap[:, q, osl], m3.ap[:, isl], m4.ap[:, isl], ALU.add, [m3.b, m4.b], [SNi.b])
                if si == 0:
                    self.cp(SNr.ap[:, q, s0col:s0col + 1], s0r[:, l, q:q + 1], [SM.b], [SNr.b])
                    self.cp(SNi.ap[:, q, s0col:s0col + 1], s0i[:, l, q:q + 1], [SM.b], [SNi.b])
                else:
                    pr = si - 1
                    self.tt(FS.ap[:, 0, pr, q:q + 1], m1.ap[:, fcol:fcol + 1], m2.ap[:, fcol:fcol + 1], ALU.subtract, [m1.b, m2.b], [FS.b])
                    self.tt(FS.ap[:, 1, pr, q:q + 1], m3.ap[:, fcol:fcol + 1], m4.ap[:, fcol:fcol + 1], ALU.add, [m3.b, m4.b], [FS.b])
        self.st(d['o_ss'][l], FS, W=[self.dbuf('o_ss')])
        if 'stopS2' in self.dbg:
            return
        WG = A.alloc([2, 256], BF16, "WGLU")
        self.ld(WG, d['w_glu'][l].rearrange("(i p) n -> p i n", p=128), q="pool")
        TM = A.alloc([8, 256], BF16, "TM")
        GF = [A.alloc([2, 512], BF16, "GF%d" % i) for i in range(2)]
        x2 = [A.alloc([512], F32, "gx2%d" % i) for i in range(2)]
        ux = [A.alloc([512], F32, "gux%d" % i) for i in range(2)]
        sg = [A.alloc([512], F32, "gsg%d" % i) for i in range(2)]
        oc = [A.alloc([512], BF16, "oc%d" % i) for i in range(2)]
        bglu = self.sm('bglu')
        for t in range(self.NTILE):
            tok = slice(t * 512, (t + 1) * 512)
            bc = slice(t * 64, (t + 1) * 64)
            for kb in range(4):
                ps = self.psn()
                g2 = kb % 2
                rows = slice(g2 * 64, g2 * 64 + 64)
                for gi in range(4):
                    gp = 4 * (kb // 2) + gi
                    g = 2 * gp + g2
                    o_ = ps.ap[0:64, gi * 128:(gi + 1) * 128]
                    n_mm = 0
                    for dr in range(2):
                        q = dr * 8 + gp
                        if dr == 0:
                            rr_ = fv(CAr.ap[rows, q, 1, :], [[1, 128]])
                            ri_ = fv(CAi.ap[rows, q, 1, :], [[1, 128]])
                        else:
                            rr_ = fv(CAr.ap[rows, q, 0, :], [[1, 128]])
                            ri_ = fv(CAi.ap[rows, q, 0, :], [[1, 128]])
                        for (lh, rh, Rb) in ((SNr.ap[rows, q, bc], rr_, [SNr.b, wb]), (SNi.ap[rows, q, bc], ri_, [SNi.b, wb]),
                                             (UT.ap[:, g, bc], WL.ap[:, dr * 16 + g, :], [UT.b, wb])):
                            self.mm(o_, lh, rh, n_mm == 0, n_mm == 5, Rb, [ps.b], inc=(gi == 3 and n_mm == 5))
                            n_mm += 1
                i2 = kb % 2
                self.act(x2[i2].ap[0:64], ps.ap[0:64], AF.Square, [ps.b], [x2[i2].b])
                self.ts(x2[i2].ap[0:64], x2[i2].ap[0:64], 0.044715, 1.0, ALU.mult, ALU.add, [x2[i2].b], [x2[i2].b])
                self.tt(ux[i2].ap[0:64], x2[i2].ap[0:64], ps.ap[0:64], ALU.mult, [x2[i2].b, ps.b], [ux[i2].b])
                self.act(sg[i2].ap[0:64], ux[i2].ap[0:64], AF.Sigmoid, [ux[i2].b], [sg[i2].b], scale=1.5957691216057308)
                ch0 = (8 * (kb // 2) + g2) * 16
                self.tt(fv(TM.ap[0:64], [[32, 4], [256, 8], [1, 16]], off=ch0),
                        fv(ps.ap[0:64], [[128, 4], [16, 8], [1, 16]]),
                        fv(sg[i2].ap[0:64], [[128, 4], [16, 8], [1, 16]]), ALU.mult, [ps.b, sg[i2].b], [TM.b])
            pT = self.psn()
            pTb = pT.ap.bitcast(BF16)
            for j in range(8):
                for ct in range(2):
                    self.tr(pTb[:, (j * 2 + ct) * 64:(j * 2 + ct + 1) * 64], TM.ap[0:64, j, ct * 128:(ct + 1) * 128], self.ident[0:64, 0:64],
                            [TM.b, self.CB.b], [pT.b], inc=(j == 7 and ct == 1))
            gf = GF[t % 2]
            for ct in range(2):
                self.cp(fv(gf.ap[:, ct, :], [[1, 8], [8, 64]]), fv(pTb, [[128, 8], [1, 64]], off=ct * 64), [pT.b], [gf.b],
                        eng=("act" if ct else "dve"))
            for mt in range(2):
                ps = self.psn()
                for ct in range(2):
                    self.mm(ps.ap, WG.ap[:, ct, mt * 128:(mt + 1) * 128], gf.ap[:, ct, :], ct == 0, ct == 1, [WG.b, gf.b], [ps.b], inc=(ct == 1))
                self.act(sg[mt].ap, ps.ap, AF.Sigmoid, [ps.b, SM.b], [sg[mt].b], bias=bglu[:, l, mt:mt + 1])
                self.tt(oc[mt].ap, gf.ap[:, mt, :], sg[mt].ap, ALU.mult, [gf.b, sg[mt].b], [oc[mt].b])
                self.st(d['mix'][6 + mt, :, tok], oc[mt], W=[self.dbuf('mix')])

    def attn_bufs(self):
        A = self.A
        self.attn_sbanks = [self.PS[0], self.PS[1], self.PS[2], self.PS[3]]
        self.sbi = 0
        self.pbi = 0
        self.rci = 0
        self.obi = 0
        self.fin_pending = None
        Pb = [A.alloc([2, 512], BF16, "P%d" % i) for i in range(5)]
        rcs = [(A.alloc([512], F32, "rcf%d" % i), A.alloc([512], BF16, "rch%d" % i), A.alloc([512], BF16, "rcl%d" % i)) for i in range(4)]
        onf = [(A.alloc([512], F32, "bcs%d" % i), A.alloc([512], BF16, "on%d" % i)) for i in range(4)]
        return Pb, rcs, onf

    def ffn_weight_cast(self):
        d, l = self.d, self.l
        w1v = d['w_f1'][l].rearrange("(kt p) n -> p kt n", p=128)
        w2v = d['w_f2'][l].rearrange("(f p) n -> p f n", p=128)
        for fc in range(FT // 2):
            for gu in range(2):
                c0 = gu * DFF + fc * 256
                self.S.dma("pool", d['w1s'][fc, :, :, gu, :], w1v[:, :, c0:c0 + 256])
                yield
        for m in range(8):
            self.S.dma("pool", d['w2s'][m], w2v[:, :, m * 128:(m + 1) * 128])
            yield

    def cast_pump(self, n):
        for _ in range(n):
            if self.castgen is None:
                return
            try:
                next(self.castgen)
            except StopIteration:
                self.castgen = None

    def phaseGA(self):
        A, S, d = self.A, self.S, self.d
        l, LS, NS, NT, NKEY = self.l, self.LS, self.NS, self.NT, self.NKEY
        self.castgen = self.ffn_weight_cast()
        NKT = NKEY // 128
        nlat = LS // 128
        Kt = A.alloc([NKEY], BF16, "Kt")
        self.ld(Kt, d['ka'][:, 0:LS], out_ap=Kt.ap[:, 0:LS], R=[self.dbuf('ka')])
        self.ld(Kt, d['ka'][:, LS + PAST:NKEY], out_ap=Kt.ap[:, LS + PAST:NKEY], R=[self.dbuf('ka')])
        self.ld(Kt, d['cK'][l], out_ap=Kt.ap[:, LS:LS + PAST], q="pool")
        Vt = A.alloc([NKT, 2, 65], BF16, "Vt")
        for g in range(2):
            self.ld(Vt, d['va'][0:LS, g * 64:(g + 1) * 64].rearrange("(kt p) e -> p kt e", p=128), out_ap=Vt.ap[:, 0:nlat, g, 0:64],
                    R=[self.dbuf('va')])
            self.ld(Vt, d['va'][LS:NT, g * 64:(g + 1) * 64].rearrange("(kt p) e -> p kt e", p=128), out_ap=Vt.ap[:, nlat + 4:NKT, g, 0:64],
                    R=[self.dbuf('va')])
            self.ld(Vt, d['cV'][l][:, g * 64:(g + 1) * 64].rearrange("(kt p) e -> p kt e", p=128), out_ap=Vt.ap[:, nlat:nlat + 4, g, 0:64],
                    q="pool")
        self.memset(Vt.ap[:, :, :, 64:65], 1.0, [Vt.b])
        QT = [A.alloc([3, 512], BF16, "QT%d" % i) for i in range(2)]
        Pb, rcs, onf = self.attn_bufs()
        scale = 64 ** -0.5
        for t in range(self.NTILE):
            prm = (t == NS)
            tok0 = t * 512
            Q = QT[t % 2]
            self.ld(Q, d['qa'][:, :, tok0:tok0 + 512].rearrange("i p n -> p i n"), R=[self.dbuf('qa')])
            if not prm:
                jobs = [(0, 512, list(range(0, nlat + 4)))]
            else:
                jobs = [(p * 256, 256, [nlat + 4 + 2 * p, nlat + 4 + 2 * p + 1]) for p in range(2)]
            for (c0, N, kts) in jobs:
                for i in range(3):
                    heads = []
                    for s_, hh in enumerate((i, i + 3)):
                        heads.append((self.PS[6 + s_], d['mix'][hh // 2, (hh % 2) * 64:(hh % 2) * 64 + 64, tok0 + c0:tok0 + c0 + N]))
                    steps = []
                    for kt in kts:
                        st_ = []
                        for s_ in range(2):
                            rows = slice(s_ * 64, s_ * 64 + 64)
                            st_.append((Kt.ap[rows, kt * 128:(kt + 1) * 128], Q.ap[rows, i, c0:c0 + N], Vt.ap[:, kt, s_, :], s_, [Kt.b, Q.b], [Vt.b]))
                        steps.append(st_)
                    self.attn_core3(steps, heads, N, scale, Pb, rcs, onf)
                    self.cast_pump(2)
        self.attn_flush()
        self.cast_pump(1000)

    def attn_core3(self, steps, heads, N, scale, Pb, rcs, onf):
        nsteps = len(steps)
        started = [False] * len(heads)
        last_step_of = [max(si for si, st_ in enumerate(steps) for sl in st_ if sl[3] == hi) for hi in range(len(heads))]
        prev = None

        def pv(si, st_, P_):
            for j, sl in enumerate(st_):
                hi = sl[3]
                O = heads[hi][0]
                is_last = (si == last_step_of[hi]) and all(s2[3] != hi for s2 in st_[j + 1:])
                self.mm(O.ap[0:65, 0:N], sl[2], P_.ap[:, j, 0:N], not started[hi], is_last, [P_.b] + sl[5], [O.b], inc=is_last)
                started[hi] = True

        fin_prev = self.fin_pending
        self.fin_pending = None
        pend = []
        for si, st_ in enumerate(steps):
            if si == 2 and fin_prev is not None:
                fin_prev()
                fin_prev = None
            di = self.sbi % 3
            self.sbi += 1
            Sd = self.PST[di]
            sb = [self.PS[2 * di].b, self.PS[2 * di + 1].b]
            for j, sl in enumerate(st_):
                self.mm(Sd[:, j * 512:j * 512 + N], sl[0], sl[1], True, True, sl[4], sb, inc=(j == len(st_) - 1))
            if len(pend) >= 2:
                pv(*pend.pop(0))
            P_ = Pb[self.pbi % len(Pb)]
            self.pbi += 1
            nj = len(st_)
            self.act(fv(P_.ap[:, 0, :], [[512, nj], [1, N]]), fv(Sd, [[512, nj], [1, N]]), AF.Exp, sb, [P_.b], scale=scale)
            pend.append((si, st_, P_))
        if fin_prev is not None:
            fin_prev()
            fin_prev = None
        while pend:
            pv(*pend.pop(0))
        parts = []
        for hi, (O, dst) in enumerate(heads):
            rcf, rch, rcl = rcs[self.rci % len(rcs)]
            bcs, on = onf[self.rci % len(onf)]
            self.rci += 1
            self.recip(rcf.ap[64:65, 0:N], O.ap[64:65, 0:N], [O.b], [rcf.b])
            self.cp(rch.ap[64:65, 0:N], rcf.ap[64:65, 0:N], [rcf.b], [rch.b])
            self.tt(rcl.ap[64:65, 0:N], rcf.ap[64:65, 0:N], rch.ap[64:65, 0:N], ALU.subtract, [rcf.b, rch.b], [rcl.b])
            parts.append((O, dst, rch, rcl, bcs, on))

        def fin_b(parts=parts, N=N):
            for (O, dst, rch, rcl, bcs, on) in parts:
                bcp = self.PS[2 * (self.sbi % 3)]
                self.sbi += 1
                self.mm(bcp.ap[0:64, 0:N], self.ones[64:65, 0:64], rch.ap[64:65, 0:N], True, False, [rch.b, self.CB.b], [bcp.b], inc=False)
                self.mm(bcp.ap[0:64, 0:N], self.ones[64:65, 0:64], rcl.ap[64:65, 0:N], False, True, [rcl.b, self.CB.b], [bcp.b])
                self.cp(bcs.ap[0:64, 0:N], bcp.ap[0:64, 0:N], [bcp.b], [bcs.b])
                self.tt(on.ap[0:64, 0:N], O.ap[0:64, 0:N], bcs.ap[0:64, 0:N], ALU.mult, [O.b, bcs.b], [on.b])
                self.st(dst, on, in_ap=on.ap[0:64, 0:N])
        self.fin_pending = fin_b

    def attn_flush(self):
        if self.fin_pending is not None:
            self.fin_pending()
            self.fin_pending = None

    def phaseMA(self):
        A, S, d = self.A, self.S, self.d
        l, LS, NS, NT, NKEY = self.l, self.LS, self.NS, self.NT, self.NKEY
        NKT = NKEY // 128
        nlat = LS // 128
        KH = [A.alloc([NKEY], BF16, "KH%d" % i) for i in range(2)]
        VH = [A.alloc([NKT, 65], BF16, "VH%d" % i) for i in range(2)]
        for i in range(2):
            self.memset(VH[i].ap[:, :, 64:65], 1.0, [VH[i].b])
        QH = [A.alloc([512], BF16, "QH%d" % i) for i in range(3)]
        Pb, rcs, onf = self.attn_bufs()
        scale = 96 ** -0.5
        qi = 0
        for hh in range(6):
            Kh = KH[hh % 2]
            Vh = VH[hh % 2]
            self.ld(Kh, d['kb'][hh], out_ap=Kh.ap[0:96, :], R=[self.dbuf('kb')])
            self.ld(Vh, d['vb'][:, hh * 64:(hh + 1) * 64].rearrange("(kt p) e -> p kt e", p=128), out_ap=Vh.ap[:, :, 0:64], R=[self.dbuf('vb')])
            for t in range(self.NTILE):
                prm = (t == NS)
                tok0 = t * 512
                Q = QH[qi % 3]
                qi += 1
                self.ld(Q, d['qb'][hh, :, tok0:tok0 + 512], out_ap=Q.ap[0:96, :], R=[self.dbuf('qb')])
                if not prm:
                    jobs = [(0, 512, list(range(0, nlat + 4)))]
                else:
                    jobs = [(p * 256, 256, [nlat + 4 + 2 * p, nlat + 4 + 2 * p + 1]) for p in range(2)]
                for (c0, N, kts) in jobs:
                    Ob = self.PS[6 + (self.obi % 2)]
                    self.obi += 1
                    dst = d['mix'][3 + hh // 2, (hh % 2) * 64:(hh % 2) * 64 + 64, tok0 + c0:tok0 + c0 + N]
                    steps = []
                    for k2 in range(0, len(kts), 2):
                        st_ = []
                        for kt in kts[k2:k2 + 2]:
                            st_.append((Kh.ap[0:96, kt * 128:(kt + 1) * 128], Q.ap[0:96, c0:c0 + N], Vh.ap[:, kt, :], 0, [Kh.b, Q.b], [Vh.b]))
                        steps.append(st_)
                    self.attn_core3(steps, [(Ob, dst)], N, scale, Pb, rcs, onf)
        self.attn_flush()

    def phaseO(self, xcur):
        A, S, d = self.A, self.S, self.d
        l, NS = self.l, self.NS
        WO = A.alloc([8, D], BF16, "WO")
        self.ld(WO, d['w_out'][l].rearrange("(kt p) n -> p kt n", p=128), q="pool")
        MX = [A.alloc([8, 512], BF16, "mx%d" % i) for i in range(2)]
        XT = [A.alloc([8, 512], F32, "xo%d" % i) for i in range(2)]
        XM = [A.alloc([8, 512], F32, "xmo%d" % i) for i in range(2)]
        xv = xcur.rearrange("(kt p) n -> p kt n", p=128)
        xmv = d['xm'].rearrange("(kt p) n -> p kt n", p=128)
        MOD = self.MOD
        for t in range(self.NTILE):
            c = 1 if t == NS else 0
            tok = slice(t * 512, (t + 1) * 512)
            mx, xt, xm = MX[t % 2], XT[t % 2], XM[t % 2]
            self.ld(mx, d['mix'][:, :, tok].rearrange("k p n -> p k n"), R=[self.dbuf('mix')])
            self.ld(xt, xv[:, :, tok], R=[self.dbuf('xcur')])
            for m in range(8):
                ps = self.psn()
                for kt in range(8):
                    self.mm(ps.ap, WO.ap[:, kt, m * 128:(m + 1) * 128], mx.ap[:, kt, :], kt == 0, kt == 7, [WO.b, mx.b], [ps.b], inc=(kt == 7))
                self.stt(xm.ap[:, m, :], ps.ap, MOD.ap[:, l, 2, m, c:c + 1], xt.ap[:, m, :], ALU.mult, ALU.add, [ps.b, xt.b, MOD.b], [xm.b])
            self.st(xmv[:, :, tok], xm, W=[self.dbuf('xm')])

    def phaseF(self, xnext, last):
        A, S, d = self.A, self.S, self.d
        l, NS = self.l, self.NS
        MOD = self.MOD
        TBT = 3
        X = A.alloc([TBT, 8, 512], F32, "Xf")
        H2 = A.alloc([TBT, 8, 512], BF16, "H2")
        ACTV = A.alloc([FT, TBT * 512], BF16, "ACTV")
        W1 = [A.alloc([8, 2, 256], BF16, "w1c%d" % i) for i in range(2)]
        W2 = [A.alloc([FT, 128], BF16, "w2%d" % i) for i in range(2)]
        sq8 = A.alloc([8, 512], BF16, "sq8f")
        tmp8 = T(fv(ACTV.ap, [[1, 8192]]).bitcast(F32).rearrange("p (k n) -> p k n", k=8), ACTV.b)
        rstd = A.alloc([512], F32, "rstdf")
        sgb = [A.alloc([512], F32, "sgf%d" % i) for i in range(3)]
        xmv = d['xm'].rearrange("(kt p) n -> p kt n", p=128)
        xnv = xnext.rearrange("(kt p) n -> p kt n", p=128)
        w1v = d['w_f1'][l].rearrange("(kt p) n -> p kt n", p=128)
        w2v = d['w_f2'][l].rearrange("(f p) n -> p f n", p=128)
        fing = self.sm('fing')
        blocks = []
        t = 0
        while t < self.NTILE:
            blocks.append(list(range(t, min(t + TBT, self.NTILE))))
            t += TBT
        wi = 0
        w2i = 0
        sgi = 0
        for blk in blocks:
            nt = len(blk)
            for ti, t in enumerate(blk):
                c = 1 if t == NS else 0
                tok = slice(t * 512, (t + 1) * 512)
                xt = T(X.ap[:, ti], X.b)
                ht = T(H2.ap[:, ti], H2.b)
                self.ld(xt, xmv[:, :, tok], R=[self.dbuf('xm')])
                self.norm_tile(xt, ht, 4, 3, c, sq8, rstd, tmp8)
            for fc in range(FT // 2):
                w1c = W1[wi % 2]
                wi += 1
                self.ld(w1c, d['w1s'][fc])
                wg = T(w1c.ap[:, :, 0, :], w1c.b)
                wu = T(w1c.ap[:, :, 1, :], w1c.b)
                for fi in range(2):
                    f = fc * 2 + fi
                    for ti in range(nt):
                        gps = self.psn()
                        ups = self.psn()
                        for kt in range(8):
                            self.mm(gps.ap, wg.ap[:, kt, fi * 128:(fi + 1) * 128], H2.ap[:, ti, kt, :], kt == 0, kt == 7, [wg.b, H2.b], [gps.b], inc=(kt == 7))
                        for kt in range(8):
                            self.mm(ups.ap, wu.ap[:, kt, fi * 128:(fi + 1) * 128], H2.ap[:, ti, kt, :], kt == 0, kt == 7, [wu.b, H2.b], [ups.b], inc=(kt == 7))
                        sg = sgb[sgi % 3]
                        sgi += 1
                        self.act(sg.ap, gps.ap, AF.Silu, [gps.b], [sg.b])
                        self.tt(ACTV.ap[:, f, ti * 512:(ti + 1) * 512], sg.ap, ups.ap, ALU.mult, [sg.b, ups.b], [ACTV.b])
            for m in range(8):
                w2 = W2[w2i % 2]
                w2i += 1
                self.ld(w2, d['w2s'][m])
                for ti, t in enumerate(blk):
                    c = 1 if t == NS else 0
                    ps = self.psn()
                    for f in range(FT):
                        self.mm(ps.ap, w2.ap[:, f, :], ACTV.ap[:, f, ti * 512:(ti + 1) * 512], f == 0, f == FT - 1, [w2.b, ACTV.b], [ps.b], inc=(f == FT - 1))
                    self.stt(X.ap[:, ti, m, :], ps.ap, MOD.ap[:, l, 5, m, c:c + 1], X.ap[:, ti, m, :], ALU.mult, ALU.add, [ps.b, X.b, MOD.b], [X.b])
            for ti, t in enumerate(blk):
                tok = slice(t * 512, (t + 1) * 512)
                xt = T(X.ap[:, ti], X.b)
                if not last:
                    self.st(xnv[:, :, tok], xt, W=[self.dbuf('xcur')])
                else:
                    self.act(sq8.ap, xt.ap, AF.Square, [xt.b], [sq8.b])
                    ps = self.psn()
                    for kt in range(8):
                        self.mm(ps.ap, self.ones, sq8.ap[:, kt, :], kt == 0, kt == 7, [sq8.b, self.CB.b], [ps.b], inc=(kt == 7))
                    self.act(rstd.ap, ps.ap, AF.Sqrt, [ps.b, self.eps.b], [rstd.b], bias=self.eps.ap, scale=1.0 / D)
                    self.recip(rstd.ap, rstd.ap, [rstd.b], [rstd.b])
                    self.tt(tmp8.ap, xt.ap, fv(rstd.ap, [[0, 8], [1, 512]]), ALU.mult, [xt.b, rstd.b], [tmp8.b])
                    for kt in range(8):
                        self.act(tmp8.ap[:, kt, :], tmp8.ap[:, kt, :], AF.Identity, [tmp8.b, self.SM.b], [tmp8.b], scale=fing[:, kt:kt + 1])
                    self.st(xnv[:, :, tok], tmp8, W=[self.dbuf('yT')])


def _rope_tables(LS):
    t = np.arange(LS)
    row = (t // 64).astype(np.float32)
    col = (t % 64).astype(np.float32)

    def tab(rot_dim):
        quarter = rot_dim // 4
        inv = (np.float32(10000.0) ** (-np.arange(quarter, dtype=np.float32) / np.float32(quarter))).astype(np.float32)
        ang = np.concatenate([row[:, None] * inv, col[:, None] * inv], axis=-1).astype(np.float32)
        c = np.cos(ang.astype(np.float64)).astype(np.float32)
        s = np.sin(ang.astype(np.float64)).astype(np.float32)
        c = np.concatenate([c, c], axis=-1)
        s = np.concatenate([s, s], axis=-1)
        return c.T, s.T

    cg, sg = tab(64)
    cm, sm_ = tab(32)
    ropeG = np.stack([np.tile(cg, (2, 1)), np.tile(sg, (2, 1))], axis=1)
    ropeM = np.stack([np.tile(cm, (4, 1)), np.tile(sm_, (4, 1))], axis=1)
    return np.ascontiguousarray(ropeG, np.float32), np.ascontiguousarray(ropeM, np.float32)


def _consts():
    ident = np.eye(128, dtype=np.float32)
    ones = np.ones((128, 128), np.float32)
    bd2 = np.zeros((128, 128), np.float32)
    bd2[0:64, 0:64] = 1
    bd2[64:128, 64:128] = 1

    def rot(n_heads, hd):
        m = np.zeros((128, 128), np.float32)
        half = hd // 2
        for hh in range(n_heads):
            b = hh * hd
            for dd in range(half):
                m[b + dd + half, b + dd] = -1.0
                m[b + dd, b + dd + half] = 1.0
        return m

    cb = np.concatenate([ident, ones, bd2, rot(2, 64), rot(4, 32)], axis=1)
    return cb.astype(ml_dtypes.bfloat16)


def _pack_small(inp, core, NL, SL, NSP):
    a = np.zeros((128, NSP), np.float32)

    def put(name, arr):
        off, shape = SL[name]
        arr = np.asarray(arr, np.float32).reshape(128, prod(shape))
        a[:, off:off + prod(shape)] = arr

    def fm(v, nt):
        return np.asarray(v)[:NL].reshape(NL, nt, 128).transpose(2, 0, 1)

    put('n1g', fm(inp['norm1_g'], 8))
    put('n2g', fm(inp['norm2_g'], 8))
    put('bada', fm(inp['b_ada'], 48))
    put('qn', np.tile(np.asarray(inp['gqa_q_norm'])[:NL].T, (2, 1)))
    put('kn', np.tile(np.asarray(inp['gqa_k_norm'])[:NL].T, (2, 1)))
    put('mqn', fm(inp['mla_q_norm'], 2))
    put('mkvn', fm(inp['mla_kv_norm'], 1))
    put('bglu', fm(inp['ssm_b_glu'], 2))
    put('fing', np.asarray(inp['final_g']).reshape(8, 128).T)
    cond = np.stack([np.asarray(inp['c'])[core], np.asarray(inp['c_ctx'])], axis=-1)
    put('cond', cond.reshape(8, 128, 2).transpose(1, 0, 2))

    def sq(v):
        v = np.asarray(v)[:NL].reshape(NL, 2, 8, 2, 64)
        return v.transpose(3, 4, 0, 1, 2).reshape(128, NL, 16)

    put('lre', sq(inp['ssm_lam_re']))
    put('lim', sq(inp['ssm_lam_im']))
    ldt = np.broadcast_to(np.asarray(inp['ssm_log_dt'])[:NL, :, :, None], (NL, 2, 16, 64))
    put('ldt', sq(ldt))
    put('s0re', sq(np.asarray(inp['state_ssm_re'])[core]))
    put('s0im', sq(np.asarray(inp['state_ssm_im'])[core]))
    dd = np.asarray(inp['ssm_d'])[:NL].reshape(NL, 16, 16)
    drep = np.broadcast_to(dd.transpose(2, 0, 1)[None], (8, 16, NL, 16)).reshape(128, NL, 16)
    put('drep', drep)
    return a


def _pack_ssmbc(inp, NL):
    def bq(v):
        v = np.asarray(v)[:NL].reshape(NL, 2, 8, 2, 64, 16)
        return v.transpose(3, 4, 0, 1, 2, 5).reshape(128, NL, 256)

    def cq(v):
        v = np.asarray(v)[:NL].reshape(NL, 2, 8, 2, 16, 64)
        return v.transpose(3, 5, 0, 1, 2, 4).reshape(128, NL, 256)

    a = np.stack([bq(inp['ssm_b_re']), bq(inp['ssm_b_im']), cq(inp['ssm_c_re']), cq(inp['ssm_c_im'])], axis=2)
    return np.ascontiguousarray(a.transpose(1, 0, 2, 3), np.float32)


def make_in_maps(inp, LS, NL, n_cores=8):
    SL, NSP = small_layout(NL)
    ropeG, ropeM = _rope_tables(LS)
    cbf = _consts()
    ssmbc = _pack_ssmbc(inp, NL)
    f = lambda k: np.ascontiguousarray(np.asarray(inp[k])[:NL], np.float32)
    shared = {
        'ssmbc': ssmbc, 'cbf': cbf, 'ropeG': ropeG, 'ropeM': ropeM,
        'w_ada': f('w_ada'), 'w_in': f('w_in'), 'w_uq': f('mla_w_uq'), 'w_uk': f('mla_w_uk'), 'w_uv': f('mla_w_uv'),
        'w_glu': f('ssm_w_glu'), 'w_out': f('w_out'), 'w_f1': f('w_ffn_in'), 'w_f2': f('w_ffn_out'),
    }
    maps = []
    xs = np.asarray(inp['x_sample'])
    xp = np.asarray(inp['x_prompt'])
    for c in range(n_cores):
        m = dict(shared)
        x = np.concatenate([xs[c, :LS], xp[2 * c], xp[2 * c + 1]], axis=0)
        m['xT'] = np.ascontiguousarray(x.T, np.float32)
        m['smallp'] = _pack_small(inp, c, NL, SL, NSP)
        m['cK'] = np.ascontiguousarray(np.asarray(inp['cache_gqa_k'])[c, :NL].reshape(NL, PAST, 128).transpose(0, 2, 1), np.float32)
        m['cV'] = np.ascontiguousarray(np.asarray(inp['cache_gqa_v'])[c, :NL].reshape(NL, PAST, 128), np.float32)
        m['cCKV'] = np.ascontiguousarray(np.asarray(inp['cache_mla_ckv'])[c, :NL].transpose(0, 2, 1), np.float32)
        m['cKR'] = np.ascontiguousarray(np.asarray(inp['cache_mla_krope'])[c, :NL].transpose(0, 2, 1), np.float32)
        maps.append(m)
    return maps


def assemble(results, LS, NL, n_cores=8):
    B = 2 * n_cores
    y_prompt = np.zeros((B, LP, D), np.float32)
    y_sample = np.zeros((n_cores, LS, D), np.float32)
    nk = np.zeros((B, NL, LP, 2, 64), np.float32)
    nv = np.zeros((B, NL, LP, 2, 64), np.float32)
    nckv = np.zeros((B, NL, LP, 128), np.float32)
    nkr = np.zeros((B, NL, LP, 32), np.float32)
    sre = np.zeros((B, NL, 2, 16, 64), np.float32)
    sim = np.zeros((B, NL, 2, 16, 64), np.float32)
    for c in range(n_cores):
        r = results[c]
        y = np.asarray(r['yT']).T
        y_sample[c] = y[:LS]
        for p in range(2):
            b = 2 * c + p
            y_prompt[b] = y[LS + p * LP: LS + (p + 1) * LP]
            sl = slice(p * LP, (p + 1) * LP)
            nk[b] = np.asarray(r['o_k'])[:, :, sl].transpose(0, 2, 1).reshape(NL, LP, 2, 64)
            nv[b] = np.asarray(r['o_v'])[:, sl, :].reshape(NL, LP, 2, 64)
            nckv[b] = np.asarray(r['o_ckv'])[:, :, sl].transpose(0, 2, 1)
            nkr[b] = np.asarray(r['o_kr'])[:, :, sl].transpose(0, 2, 1)
            ss = np.asarray(r['o_ss'])
            for ri, dst in ((0, sre), (1, sim)):
                v = ss[:, :, ri, p, :].reshape(NL, 2, 64, 2, 8)
                dst[b] = v.transpose(0, 3, 4, 1, 2).reshape(NL, 2, 16, 64)
    return (y_prompt, y_sample, nk, nv, nckv, nkr, sre, sim)


_CACHE = {}


def kernel(**inputs):
    LS, NL = 4096, 4
    key = (LS, NL)
    if key not in _CACHE:
        _CACHE[key] = Prog(LS, NL).build()
    nc = _CACHE[key]
    maps = make_in_maps(inputs, LS, NL)
    res = run_bass_kernel_spmd(nc, maps, core_ids=list(range(8)))
    return assemble(res.results, LS, NL)
```

```python
import math
import os
import numpy as np
import ml_dtypes
import concourse.bass as bass
import concourse.mybir as mybir
from concourse.bass_utils import run_bass_kernel_spmd
from concourse.ap import AP

F32 = mybir.dt.float32
BF16 = mybir.dt.bfloat16
ALU = mybir.AluOpType
AF = mybir.ActivationFunctionType

D = 1024
KT = 8
LP = 256
PAST = 512
DFF = 2816
FT = 22
EPS = 1e-6
INW = 1312
NQ = 16


def prod(s):
    r = 1
    for v in s:
        r *= int(v)
    return r


def small_layout(NL):
    off = 0
    L = {}

    def add(n, *shape):
        nonlocal off
        L[n] = (off, shape)
        off += prod(shape)

    add('n1g', NL, 8)
    add('n2g', NL, 8)
    add('bada', NL, 48)
    add('qn', NL)
    add('kn', NL)
    add('mqn', NL, 2)
    add('mkvn', NL)
    add('bglu', NL, 2)
    add('fing', 8)
    add('cond', 8, 2)
    add('lre', NL, 16)
    add('lim', NL, 16)
    add('ldt', NL, 16)
    add('s0re', NL, 16)
    add('s0im', NL, 16)
    add('drep', NL, 16)
    return L, off


class Buf:
    __slots__ = ("name", "w", "r")

    def __init__(self, name):
        self.name = name
        self.w = None
        self.r = {}


class T:
    __slots__ = ("ap", "b")

    def __init__(self, ap, b):
        self.ap = ap
        self.b = b


ENGS = ("pe", "act", "dve", "pool", "sp")
NDS = 40


class Sched:
    def __init__(self, nc, stack):
        self.nc = nc
        self.sem = {}
        self.cnt = {}
        self.q = {}
        self.waited = {}
        for e in ENGS:
            self.sem[("e", e)] = stack.enter_context(nc.semaphore("sem_" + e))
            self.cnt[e] = 0
            self.q[e] = []
            self.waited[e] = {}
        self.dcnt = {}
        self.dnext = {}
        for qn in ("sp", "pool"):
            self.dnext[qn] = 0
            for i in range(NDS):
                self.sem[("d", qn, i)] = stack.enter_context(nc.semaphore("d%s%d" % (qn, i)))
                self.dcnt[(qn, i)] = 0
        self.ninstr = 0
        self.dry = False

    def _wait(self, eng, key, val):
        if self.waited[eng].get(key, 0) >= val:
            return
        if key == ("e", eng) and (val > self.cnt[eng] or eng == "pe"):
            return
        sem = self.sem[key]
        self.q[eng].append(lambda e, sem=sem, val=val: e.wait_ge(sem, val))
        self.waited[eng][key] = val
        self.ninstr += 1

    def _deps(self, eng, R, W):
        for b in R:
            if b.w is not None:
                self._wait(eng, b.w[0], b.w[1])
        for b in W:
            if b.w is not None:
                self._wait(eng, b.w[0], b.w[1])
            for k, v in b.r.items():
                self._wait(eng, k, v)

    def op(self, eng, fn, R=(), W=(), inc=True):
        if self.dry:
            return
        self._deps(eng, R, W)
        key = ("e", eng)
        if inc:
            self.cnt[eng] += 1
            val = self.cnt[eng]
        else:
            val = self.cnt[eng] + 1
        for b in R:
            if b.r.get(key, 0) < val:
                b.r[key] = val
        for b in W:
            b.w = (key, val)
            b.r = {}
        sem = self.sem[key]
        if inc:
            self.q[eng].append(lambda e, fn=fn, sem=sem: fn(e).then_inc(sem, 1))
        else:
            self.q[eng].append(lambda e, fn=fn: fn(e))
        self.ninstr += 1

    def dma(self, qn, out, in_, R=(), W=(), **kw):
        if self.dry:
            return
        i = self.dnext[qn]
        self.dnext[qn] = (i + 1) % NDS
        key = ("d", qn, i)
        prev = self.dcnt[(qn, i)]
        if prev > 0:
            self._wait(qn, key, prev)
        self._deps(qn, R, W)
        self.dcnt[(qn, i)] = prev + 16
        val = prev + 16
        for b in R:
            if b.r.get(key, 0) < val:
                b.r[key] = val
        for b in W:
            b.w = (key, val)
            b.r = {}
        sem = self.sem[key]
        self.q[qn].append(lambda e, out=out, in_=in_, sem=sem, kw=kw: e.dma_start(out=out, in_=in_, **kw).then_inc(sem, 16))
        self.ninstr += 1

    def barrier(self):
        for e in ENGS:
            if e != "sp":
                self._wait("sp", ("e", e), self.cnt[e])
        for (qn, i), v in self.dcnt.items():
            if v > 0:
                self._wait("sp", ("d", qn, i), v)
        self.cnt["sp"] += 1
        val = self.cnt["sp"]
        sem = self.sem[("e", "sp")]
        self.q["sp"].append(lambda e, sem=sem: e.nop().then_inc(sem, 1))
        for e in ENGS:
            if e != "sp":
                self._wait(e, ("e", "sp"), val)

    def emit(self, block):
        nc = self.nc

        def run(e, lst):
            for f in lst:
                f(e)

        block.tensor(lambda e: run(e, self.q["pe"]))
        block.scalar(lambda e: run(e, self.q["act"]))
        block.vector(lambda e: run(e, self.q["dve"]))
        block.gpsimd(lambda e: run(e, self.q["pool"]))
        block.sync(lambda e: run(e, self.q["sp"]))


class Arena:
    def __init__(self, nc, nbytes):
        self.t = nc.alloc_sbuf_tensor("arena", [128, nbytes // 2], BF16)
        self.off = 0
        self.cap = nbytes
        self.n = 0
        self.peak = 0
        self.record = None
        self.replay = None

    def alloc(self, free_shape, dtype, name=None):
        if self.replay:
            return self.replay.pop(0)
        esz = 4 if dtype == F32 else 2
        n = prod(free_shape)
        nbytes = n * esz
        self.off = (self.off + 63) // 64 * 64
        o = self.off
        self.off += nbytes
        self.peak = max(self.peak, self.off)
        assert self.off <= self.cap, "SBUF arena overflow: %d > %d (%s)" % (self.off, self.cap, name)
        ap = self.t[:, o // 2: o // 2 + nbytes // 2]
        if dtype == F32:
            ap = ap.bitcast(F32)
        if len(free_shape) > 1:
            names = ["d%d" % i for i in range(len(free_shape))]
            kw = {nm: int(s) for nm, s in zip(names, free_shape)}
            ap = ap.rearrange("p (%s) -> p %s" % (" ".join(names), " ".join(names)), **kw)
        self.n += 1
        t_ = T(ap, Buf(name or ("t%d" % self.n)))
        if self.record is not None:
            self.record.append(t_)
        return t_

    def mark(self):
        return self.off

    def reset(self, m):
        self.off = m


def fv(ap, pattern, off=0):
    return AP(tensor=ap.tensor, offset=ap.offset + off, ap=[list(ap.ap[0])] + [list(p) for p in pattern])


class Prog:
    def __init__(self, LS, NL, dbg=()):
        self.LS = LS
        self.NL = NL
        self.NT = LS + 2 * LP
        self.NTILE = self.NT // 512
        self.NS = LS // 512
        self.NKEY = LS + PAST + 2 * LP
        self.NB = self.NT // 8
        self.NBS = LS // 8
        self.dbg = set(dbg)
        self.SL, self.NSP = small_layout(NL)

    def declare(self, nc):
        NL, NT, LS, NKEY = self.NL, self.NT, self.LS, self.NKEY

        def inp(name, shape, dt=F32):
            return nc.dram_tensor(name, list(shape), dt, kind="ExternalInput").ap()

        def outp(name, shape, dt=F32):
            return nc.dram_tensor(name, list(shape), dt, kind="ExternalOutput").ap()

        def scr(name, shape, dt):
            kind = "ExternalOutput" if name in self.dbg else "Internal"
            return nc.dram_tensor(name, list(shape), dt, kind=kind).ap()

        d = {}
        d['xT'] = inp('xT', [D, NT])
        d['smallp'] = inp('smallp', [128, self.NSP])
        d['ssmbc'] = inp('ssmbc', [NL, 128, 4, 256])
        d['cbf'] = inp('cbf', [128, 640], BF16)
        d['ropeG'] = inp('ropeG', [128, 2, LS])
        d['ropeM'] = inp('ropeM', [128, 2, LS])
        d['cK'] = inp('cK', [NL, 128, PAST])
        d['cV'] = inp('cV', [NL, PAST, 128])
        d['cCKV'] = inp('cCKV', [NL, 128, PAST])
        d['cKR'] = inp('cKR', [NL, 32, PAST])
        d['w_ada'] = inp('w_ada', [NL, D, 6 * D])
        d['w_in'] = inp('w_in', [NL, D, INW])
        d['w_uq'] = inp('w_uq', [NL, 256, 576])
        d['w_uk'] = inp('w_uk', [NL, 128, 384])
        d['w_uv'] = inp('w_uv', [NL, 128, 384])
        d['w_glu'] = inp('w_glu', [NL, 256, 256])
        d['w_out'] = inp('w_out', [NL, D, D])
        d['w_f1'] = inp('w_f1', [NL, D, 2 * DFF])
        d['w_f2'] = inp('w_f2', [NL, DFF, D])
        d['yT'] = outp('yT', [D, NT])
        d['o_k'] = outp('o_k', [NL, 128, 512])
        d['o_v'] = outp('o_v', [NL, 512, 128])
        d['o_ckv'] = outp('o_ckv', [NL, 128, 512])
        d['o_kr'] = outp('o_kr', [NL, 32, 512])
        d['o_ss'] = outp('o_ss', [NL, 128, 2, 2, NQ])
        d['qa'] = scr('qa', [3, 128, NT], BF16)
        d['ka'] = scr('ka', [128, NKEY], BF16)
        d['va'] = scr('va', [NT, 128], BF16)
        d['qb'] = scr('qb', [6, 96, NT], BF16)
        d['kb'] = scr('kb', [6, 96, NKEY], BF16)
        d['vb'] = scr('vb', [NKEY, 384], BF16)
        d['mix'] = scr('mix', [8, 128, NT], BF16)
        d['xm'] = scr('xm', [D, NT], F32)
        d['xs0'] = scr('xs0', [D, NT], F32)
        d['xs1'] = scr('xs1', [D, NT], F32)
        d['w1s'] = scr('w1s', [FT // 2, 128, 8, 2, 256], BF16)
        d['w2s'] = scr('w2s', [8, 128, FT, 128], BF16)
        self.d = d

    def mm(self, out, lhsT, rhs, start, stop, R, W, inc=True):
        self.S.op("pe", lambda e: e.matmul(out, lhsT, rhs, start=start, stop=stop), R, W, inc)

    def tr(self, out, in_, ident, R, W, inc=True):
        self.S.op("pe", lambda e: e.transpose(out, in_, ident), R, W, inc)

    def act(self, out, in_, func, R, W, bias=None, scale=None):
        kw = {}
        if bias is not None:
            kw['bias'] = bias
        if scale is not None:
            kw['scale'] = scale
        self.S.op("act", lambda e: e.activation(out, in_, func, **kw), R, W)

    def tt(self, out, in0, in1, op, R, W, eng="dve"):
        self.S.op(eng, lambda e: e.tensor_tensor(out, in0, in1, op), R, W)

    def ts(self, out, in0, s1, s2, op0, op1, R, W, eng="dve"):
        if s2 is None:
            self.S.op(eng, lambda e: e.tensor_scalar(out, in0, s1, None, op0), R, W)
        else:
            self.S.op(eng, lambda e: e.tensor_scalar(out, in0, s1, s2, op0, op1), R, W)

    def stt(self, out, in0, scalar, in1, op0, op1, R, W):
        self.S.op("dve", lambda e: e.scalar_tensor_tensor(out, in0, scalar, in1, op0, op1), R, W)

    def cp(self, out, in_, R, W, eng="dve"):
        if eng == "act":
            self.S.op("act", lambda e: e.activation(out, in_, AF.Copy), R, W)
        else:
            self.S.op(eng, lambda e: e.tensor_copy(out, in_), R, W)

    def recip(self, out, in_, R, W):
        self.S.op("dve", lambda e: e.reciprocal(out, in_), R, W)

    def memset(self, out, val, W, eng="dve"):
        self.S.op(eng, lambda e: e.memset(out, val), (), W)

    def ld(self, out_t, in_ap, R=(), q="sp", out_ap=None, **kw):
        self.S.dma(q, out_t.ap if out_ap is None else out_ap, in_ap, R=R, W=[out_t.b], **kw)

    def st(self, out_ap, in_t, W=(), q="pool", in_ap=None, **kw):
        self.S.dma(q, out_ap, in_t.ap if in_ap is None else in_ap, R=[in_t.b], W=W, **kw)

    def psn(self):
        t = self.PS[self.psi % 8]
        self.psi += 1
        return t

    def sm(self, name):
        off, shape = self.SL[name]
        ap = self.SM.ap[:, off:off + prod(shape)]
        if len(shape) > 1:
            names = ["d%d" % i for i in range(len(shape))]
            kw = {nm: int(s) for nm, s in zip(names, shape)}
            ap = ap.rearrange("p (%s) -> p %s" % (" ".join(names), " ".join(names)), **kw)
        return ap

    def build(self):
        from contextlib import ExitStack
        nc = bass.Bass("TRN2", target_bir_lowering=False)
        self.nc = nc
        self.declare(nc)
        stack = ExitStack()
        with stack:
            self.S = Sched(nc, stack)
            self.A = Arena(nc, 211000)
            self.PS = []
            self.PST = []
            for i in range(4):
                pt = nc.alloc_psum_tensor("psum%d" % i, [128, 1024], F32)
                self.PST.append(pt[:, :])
                for j in range(2):
                    self.PS.append(T(pt[:, j * 512:(j + 1) * 512], Buf("ps%d" % (2 * i + j))))
            self.psi = 0
            self.dbufs = {}
            self.body()
            self.S.barrier()
            block = stack.enter_context(nc.Block())
            self.S.emit(block)
        return nc

    def dbuf(self, name):
        return Buf(name)

    def body(self):
        A, S, d = self.A, self.S, self.d
        NL = self.NL
        self.SM = A.alloc([self.NSP], F32, "SM")
        self.CB = A.alloc([640], BF16, "CB")
        self.MOD = A.alloc([NL, 6, 8, 2], F32, "MOD")
        self.ld(self.SM, d['smallp'])
        self.ld(self.CB, d['cbf'])
        cb = self.CB.ap
        self.ident = cb[:, 0:128]
        self.ones = cb[:, 128:256]
        self.bd2 = cb[:, 256:384]
        self.rotG = cb[:, 384:512]
        self.rotM = cb[:, 512:640]
        self.eps = A.alloc([1], F32, "eps")
        self.memset(self.eps.ap, EPS, [self.eps.b])
        self.phase0()
        S.barrier()
        xcur = d['xT']
        for l in range(NL):
            self.l = l
            last = (l == NL - 1)
            xnext = d['yT'] if last else d['xs%d' % (l % 2)]
            m0 = A.mark()
            self.UT = A.alloc([16, self.NB], BF16, "Utilde")
            S.dry = True
            psi0 = self.psi
            self.s5rec = []
            for _ in self.s5_build():
                pass
            S.dry = False
            self.psi = psi0
            self.s5gen = self.s5_build()
            self.pump_n = 1
            mU = A.mark()
            self.phaseP(xcur)
            self.pump(100000)
            assert not self.s5rec
            S.barrier()
            A.reset(mU)
            if 'stopP' in self.dbg:
                return
            self.phaseS()
            S.barrier()
            A.reset(m0)
            if 'stopS' in self.dbg:
                return
            self.phaseGA()
            S.barrier()
            A.reset(m0)
            self.phaseMA()
            S.barrier()
            A.reset(m0)
            if 'stopA' in self.dbg:
                return
            self.phaseO(xcur)
            S.barrier()
            A.reset(m0)
            self.phaseF(xnext, last)
            S.barrier()
            A.reset(m0)
            xcur = xnext

    def phase0(self):
        A, S, d = self.A, self.S, self.d
        NL = self.NL
        m0 = A.mark()
        silc = A.alloc([16], BF16, "silc")
        condf = self.sm('cond')
        self.act(silc.ap, fv(condf, [[1, 16]]), AF.Silu, [self.SM.b], [silc.b])
        WA = [A.alloc([8, 1024], BF16, "wa%d" % i) for i in range(2)]
        MOD = self.MOD
        bada = self.sm('bada')
        for l in range(NL):
            wv = d['w_ada'][l].rearrange("(kt p) n -> p kt n", p=128)
            for j in range(6):
                wb = WA[(l * 6 + j) % 2]
                self.ld(wb, wv[:, :, j * 1024:(j + 1) * 1024], q="pool")
                ps = self.psn()
                for mt in range(8):
                    for kt in range(8):
                        self.mm(ps.ap[:, mt * 2:mt * 2 + 2], wb.ap[:, kt, mt * 128:(mt + 1) * 128],
                                silc.ap[:, kt * 2:kt * 2 + 2], kt == 0, kt == 7,
                                [wb.b, silc.b], [ps.b], inc=(mt == 7 and kt == 7))
                bsl = bada[:, l, j * 8:(j + 1) * 8]
                self.tt(MOD.ap[:, l, j], fv(ps.ap, [[2, 8], [1, 2]]), fv(bsl, [[1, 8], [0, 2]]), ALU.add,
                        [ps.b, self.SM.b], [MOD.b])
        n1g = self.sm('n1g')
        n2g = self.sm('n2g')
        for l in range(NL):
            self.stt(MOD.ap[:, l, 1], MOD.ap[:, l, 1], 1.0, fv(n1g[:, l, :], [[1, 8], [0, 2]]), ALU.add, ALU.mult,
                     [MOD.b, self.SM.b], [MOD.b])
            self.stt(MOD.ap[:, l, 4], MOD.ap[:, l, 4], 1.0, fv(n2g[:, l, :], [[1, 8], [0, 2]]), ALU.add, ALU.mult,
                     [MOD.b, self.SM.b], [MOD.b])
        if 'MOD' in self.dbg:
            dm = self.nc.dram_tensor('dbgMOD', [128, NL * 96], F32, kind="ExternalOutput").ap()
            self.st(dm, MOD, in_ap=fv(MOD.ap, [[1, NL * 96]]))
        S.barrier()
        A.reset(m0)

    def norm_tile(self, xt, h, jS, jB, c, sq, rstd, tmp):
        l = self.l
        MOD = self.MOD
        self.act(sq.ap, xt.ap, AF.Square, [xt.b], [sq.b])
        ps = self.psn()
        for kt in range(8):
            self.mm(ps.ap, self.ones, sq.ap[:, kt, :], kt == 0, kt == 7, [sq.b, self.CB.b], [ps.b], inc=(kt == 7))
        self.act(rstd.ap, ps.ap, AF.Sqrt, [ps.b, self.eps.b], [rstd.b], bias=self.eps.ap, scale=1.0 / D)
        self.recip(rstd.ap, rstd.ap, [rstd.b], [rstd.b])
        self.tt(tmp.ap, xt.ap, fv(rstd.ap, [[0, 8], [1, 512]]), ALU.mult, [xt.b, rstd.b], [tmp.b])
        for kt in range(8):
            self.act(h.ap[:, kt, :], tmp.ap[:, kt, :], AF.Identity, [tmp.b, MOD.b], [h.b],
                     bias=MOD.ap[:, l, jB, kt, c:c + 1], scale=MOD.ap[:, l, jS, kt, c:c + 1])

    def headnorm(self, ps, onesmat, inv_n, gain_ap, out_f32, sq, rstd):
        self.act(sq.ap, ps.ap, AF.Square, [ps.b], [sq.b])
        ps2 = self.psn()
        self.mm(ps2.ap, onesmat, sq.ap, True, True, [sq.b, self.CB.b], [ps2.b])
        self.act(rstd.ap, ps2.ap, AF.Sqrt, [ps2.b, self.eps.b], [rstd.b], bias=self.eps.ap, scale=inv_n)
        self.recip(rstd.ap, rstd.ap, [rstd.b], [rstd.b])
        self.stt(out_f32.ap, ps.ap, gain_ap, rstd.ap, ALU.mult, ALU.mult, [ps.b, rstd.b, self.SM.b], [out_f32.b])

    def rope(self, qn, rotmat, cos_ap, sin_ap, out_bf, qnb, t1, t2, rows=128):
        r = slice(0, rows)
        self.cp(qnb.ap[r], qn.ap[r], [qn.b], [qnb.b], eng="act")
        ps = self.psn()
        self.mm(ps.ap[r], rotmat[r, 0:rows], qnb.ap[r], True, True, [qnb.b, self.CB.b], [ps.b])
        self.tt(t1.ap[r], qn.ap[r], cos_ap[r], ALU.mult, [qn.b, self.RT.b], [t1.b], eng="pool")
        self.tt(t2.ap[r], ps.ap[r], sin_ap[r], ALU.mult, [ps.b, self.RT.b], [t2.b])
        self.tt(out_bf.ap[r], t1.ap[r], t2.ap[r], ALU.add, [t1.b, t2.b], [out_bf.b])

    def phaseP(self, xcur):
        A, S, d = self.A, self.S, self.d
        l, LS, NS, NT = self.l, self.LS, self.NS, self.NT
        xv = xcur.rearrange("(kt p) n -> p kt n", p=128)
        WIN = A.alloc([8, INW], BF16, "WIN")
        wv = d['w_in'][l].rearrange("(kt p) n -> p kt n", p=128)
        for i in range(3):
            for s, hh in enumerate((i, i + 3)):
                self.ld(WIN, wv[:, :, hh * 64:(hh + 1) * 64], q="pool", out_ap=WIN.ap[:, :, i * 128 + s * 64:i * 128 + (s + 1) * 64])
        self.ld(WIN, wv[:, :, 384:INW], q="pool", out_ap=WIN.ap[:, :, 384:INW])
        WUQN = A.alloc([2, 384], BF16, "WUQN")
        WUQR = A.alloc([2, 192], BF16, "WUQR")
        uqv = d['w_uq'][l].rearrange("(i p) (h e) -> p i h e", p=128, e=96)
        for i in range(2):
            self.ld(WUQN, uqv[:, i, :, 0:64], q="pool", out_ap=fv(WUQN.ap[:, i, :], [[64, 6], [1, 64]]))
            self.ld(WUQR, uqv[:, i, :, 64:96], q="pool", out_ap=fv(WUQR.ap[:, i, :], [[32, 6], [1, 32]]))
        WUK = A.alloc([384], BF16, "WUK")
        WUV = A.alloc([384], BF16, "WUV")
        self.ld(WUK, d['w_uk'][l], q="pool")
        self.ld(WUV, d['w_uv'][l], q="pool")
        XT = [A.alloc([8, 512], F32, "xt%d" % i) for i in range(1)]
        H = [A.alloc([8, 512], BF16, "h%d" % i) for i in range(2)]
        sq8 = A.alloc([8, 512], BF16, "sq8")
        tmp8 = A.alloc([8, 512], F32, "tmp8")
        rstd = A.alloc([512], F32, "rstd")
        self.RT = A.alloc([2, 2, 512], F32, "ropetab")
        sqh = [A.alloc([512], BF16, "sqh%d" % i) for i in range(2)]
        rsh = [A.alloc([512], F32, "rsh%d" % i) for i in range(2)]
        qn = [A.alloc([512], F32, "qn%d" % i) for i in range(2)]
        qnb = [A.alloc([512], BF16, "qnb%d" % i) for i in range(2)]
        t1 = [A.alloc([512], F32, "t1%d" % i) for i in range(1)]
        t2 = [A.alloc([512], F32, "t2%d" % i) for i in range(1)]
        ob = [A.alloc([512], BF16, "ob%d" % i) for i in range(4)]
        vt = A.alloc([4, 128], BF16, "vt")
        vtf = A.alloc([4, 128], F32, "vtf")
        cqn = A.alloc([2, 512], BF16, "cqn")
        sq2 = A.alloc([2, 512], BF16, "sq2")
        ckvb = A.alloc([512], BF16, "ckvb")
        vbt = A.alloc([4, 384], BF16, "vbt")
        utm = A.alloc([16, 8, 16], BF16, "utm")
        self.rr = 0

        def nxt(lst):
            self.rr += 1
            return lst[self.rr % len(lst)]

        MOD = self.MOD

        def norm_gen(tn):
            cn = 1 if tn == NS else 0
            xt_ = XT[0]
            h_ = H[tn % 2]
            self.ld(xt_, xv[:, :, tn * 512:(tn + 1) * 512])
            yield
            self.act(sq8.ap, xt_.ap, AF.Square, [xt_.b], [sq8.b])
            yield
            ps_ = self.psn()
            for kt in range(8):
                self.mm(ps_.ap, self.ones, sq8.ap[:, kt, :], kt == 0, kt == 7, [sq8.b, self.CB.b], [ps_.b], inc=(kt == 7))
            self.act(rstd.ap, ps_.ap, AF.Sqrt, [ps_.b, self.eps.b], [rstd.b], bias=self.eps.ap, scale=1.0 / D)
            yield
            self.recip(rstd.ap, rstd.ap, [rstd.b], [rstd.b])
            yield
            self.tt(tmp8.ap, xt_.ap, fv(rstd.ap, [[0, 8], [1, 512]]), ALU.mult, [xt_.b, rstd.b], [tmp8.b])
            yield
            for kt in range(8):
                self.act(h_.ap[:, kt, :], tmp8.ap[:, kt, :], AF.Identity, [tmp8.b, MOD.b], [h_.b],
                         bias=MOD.ap[:, l, 0, kt, cn:cn + 1], scale=MOD.ap[:, l, 1, kt, cn:cn + 1])
                if kt % 2 == 1:
                    yield

        for t in range(self.NTILE):
            prm = (t == NS)
            c = 1 if prm else 0
            tok = slice(t * 512, (t + 1) * 512)
            kcol = slice(t * 512, (t + 1) * 512) if not prm else slice(LS + PAST, LS + PAST + 512)
            xt = XT[0]
            h = H[t % 2]
            if t == 0:
                self.ngen = norm_gen(0)
            self.npump(100000)
            if not prm:
                self.ld(self.RT, d['ropeG'][:, :, tok], out_ap=self.RT.ap[:, 0])
                self.ld(self.RT, d['ropeM'][:, :, tok], out_ap=self.RT.ap[:, 1])
            if t + 1 < self.NTILE:
                self.ngen = norm_gen(t + 1)

            def proj(col0, ncol, rows=128):
                ps = self.psn()
                for kt in range(8):
                    self.mm(ps.ap[0:rows], WIN.ap[:, kt, col0:col0 + ncol], h.ap[:, kt, :], kt == 0, kt == 7,
                            [WIN.b, h.b], [ps.b], inc=(kt == 7))
                return ps

            def gqa_gen(i):
                isk = (i == 3)
                bi = i % 2
                ps = proj(i * 128, 128)
                yield
                q_n, sq_, rs_ = qn[bi], sqh[bi], rsh[bi]
                gain = self.sm('kn' if isk else 'qn')[:, l:l + 1]
                self.act(sq_.ap, ps.ap, AF.Square, [ps.b], [sq_.b])
                yield
                ps2 = self.psn()
                self.mm(ps2.ap, self.bd2, sq_.ap, True, True, [sq_.b, self.CB.b], [ps2.b])
                yield
                self.act(rs_.ap, ps2.ap, AF.Sqrt, [ps2.b, self.eps.b], [rs_.b], bias=self.eps.ap, scale=1.0 / 64)
                yield
                self.recip(rs_.ap, rs_.ap, [rs_.b], [rs_.b])
                yield
                self.stt(q_n.ap, ps.ap, gain, rs_.ap, ALU.mult, ALU.mult, [ps.b, rs_.b, self.SM.b], [q_n.b])
                yield
                o = ob[i]
                if prm:
                    self.cp(o.ap, q_n.ap, [q_n.b], [o.b], eng="act")
                    if isk:
                        self.st(d['o_k'][l], q_n, W=[self.dbuf('o_k')])
                else:
                    self.rope(q_n, self.rotG, self.RT.ap[:, 0, 0], self.RT.ap[:, 0, 1], o, qnb[bi], t1[0], t2[0])
                if isk:
                    self.st(d['ka'][:, kcol], o, W=[self.dbuf('ka')])
                else:
                    self.st(d['qa'][i, :, tok], o, W=[self.dbuf('qa')])

            for pair in ((0, 1), (2, 3)):
                gens = [gqa_gen(i) for i in pair]
                while gens:
                    for g_ in list(gens):
                        try:
                            next(g_)
                        except StopIteration:
                            gens.remove(g_)
                self.pump(2 * self.pump_n)
            ps = self.psn()
            for s in range(4):
                for kt in range(8):
                    self.mm(ps.ap[:, s * 128:(s + 1) * 128], h.ap[:, kt, s * 128:(s + 1) * 128], WIN.ap[:, kt, 512:640],
                            kt == 0, kt == 7, [WIN.b, h.b], [ps.b], inc=(s == 3 and kt == 7))
            self.cp(fv(vt.ap, [[1, 512]]), ps.ap, [ps.b], [vt.b], eng="act")
            self.st(d['va'][tok, :].rearrange("(s p) c -> p s c", p=128), vt, W=[self.dbuf('va')])
            if prm:
                self.cp(fv(vtf.ap, [[1, 512]]), ps.ap, [ps.b], [vtf.b])
                self.st(d['o_v'][l].rearrange("(s p) c -> p s c", p=128), vtf, W=[self.dbuf('o_v')])
            self.pump(self.pump_n)
            self.npump(3)
            psc = [proj(640, 128), proj(768, 128)]
            for i in range(2):
                self.act(sq2.ap[:, i, :], psc[i].ap, AF.Square, [psc[i].b], [sq2.b])
            ps2 = self.psn()
            for i in range(2):
                self.mm(ps2.ap, self.ones, sq2.ap[:, i, :], i == 0, i == 1, [sq2.b, self.CB.b], [ps2.b], inc=(i == 1))
            rs = nxt(rsh)
            self.act(rs.ap, ps2.ap, AF.Sqrt, [ps2.b, self.eps.b], [rs.b], bias=self.eps.ap, scale=1.0 / 256)
            self.recip(rs.ap, rs.ap, [rs.b], [rs.b])
            mqn = self.sm('mqn')
            for i in range(2):
                self.stt(cqn.ap[:, i, :], psc[i].ap, mqn[:, l, i:i + 1], rs.ap, ALU.mult, ALU.mult,
                         [psc[i].b, rs.b, self.SM.b], [cqn.b])
            for pr in range(3):
                ps = self.psn()
                for i in range(2):
                    self.mm(ps.ap, WUQN.ap[:, i, pr * 128:(pr + 1) * 128], cqn.ap[:, i, :], i == 0, i == 1,
                            [WUQN.b, cqn.b], [ps.b], inc=(i == 1))
                o = nxt(ob)
                self.cp(o.ap, ps.ap, [ps.b], [o.b], eng="act")
                for s in range(2):
                    self.st(d['qb'][2 * pr + s, 0:64, tok], o, W=[self.dbuf('qb')], in_ap=o.ap[s * 64:(s + 1) * 64])
            for (c0, nh) in ((0, 4), (128, 2)):
                rows = nh * 32
                ps = self.psn()
                for i in range(2):
                    self.mm(ps.ap[0:rows], WUQR.ap[:, i, c0:c0 + rows], cqn.ap[:, i, :], i == 0, i == 1,
                            [WUQR.b, cqn.b], [ps.b], inc=(i == 1))
                o = nxt(ob)
                if prm:
                    self.cp(o.ap[0:rows], ps.ap[0:rows], [ps.b], [o.b], eng="act")
                else:
                    q_n = nxt(qn)
                    self.cp(q_n.ap[0:rows], ps.ap[0:rows], [ps.b], [q_n.b])
                    self.rope(q_n, self.rotM, self.RT.ap[:, 1, 0], self.RT.ap[:, 1, 1], o, nxt(qnb), nxt(t1), nxt(t2), rows=rows)
                for s in range(nh):
                    hh = c0 // 32 + s
                    self.st(d['qb'][hh, 64:96, tok], o, W=[self.dbuf('qb')], in_ap=o.ap[s * 32:(s + 1) * 32])
                self.pump(self.pump_n)
                self.npump(2)
            self.npump(3)
            ps = proj(896, 128)
            ck = nxt(qn)
            self.headnorm(ps, self.ones, 1.0 / 128, self.sm('mkvn')[:, l:l + 1], ck, nxt(sqh), nxt(rsh))
            if prm:
                self.st(d['o_ckv'][l], ck, W=[self.dbuf('o_ckv')])
            self.cp(ckvb.ap, ck.ap, [ck.b], [ckvb.b], eng="act")
            self.mla_kv(ckvb, kcol, WUK, WUV, ob, vbt, nxt)
            self.pump(self.pump_n)
            self.npump(3)
            ps = proj(1024, 32, rows=32)
            o = nxt(ob)
            if prm:
                q_n = nxt(qn)
                self.cp(q_n.ap[0:32], ps.ap[0:32], [ps.b], [q_n.b])
                self.st(d['o_kr'][l], q_n, W=[self.dbuf('o_kr')], in_ap=q_n.ap[0:32])
                self.cp(o.ap[0:32], q_n.ap[0:32], [q_n.b], [o.b], eng="act")
            else:
                q_n = nxt(qn)
                self.cp(q_n.ap[0:32], ps.ap[0:32], [ps.b], [q_n.b])
                self.rope(q_n, self.rotM, self.RT.ap[:, 1, 0], self.RT.ap[:, 1, 1], o, nxt(qnb), nxt(t1), nxt(t2), rows=32)
            for hh in range(6):
                self.st(d['kb'][hh, 64:96, kcol], o, W=[self.dbuf('kb')], in_ap=o.ap[0:32])
            for half in range(2):
                psu = [self.psn() for _ in range(2)]
                for jj in range(4):
                    j = half * 4 + jj
                    pst = psu[jj // 2]
                    for kt in range(8):
                        self.mm(pst.ap[0:64, (jj % 2) * 256:(jj % 2) * 256 + 256], fv(h.ap[:, kt, :], [[8, 64]], off=j),
                                WIN.ap[:, kt, 1056:1312], kt == 0, kt == 7, [WIN.b, h.b], [pst.b], inc=(kt == 7))
                    self.cp(utm.ap[0:64, :, j, :], fv(pst.ap[0:64], [[16, 16], [1, 16]], off=(jj % 2) * 256), [pst.b], [utm.b],
                            eng=("act" if jj % 2 else "dve"))
            pT = self.psn()
            pTb = pT.ap.bitcast(BF16)
            for g in range(16):
                self.tr(pTb[:, g * 64:(g + 1) * 64], fv(utm.ap[0:64, g], [[1, 128]]), self.ident[0:64, 0:64],
                        [utm.b, self.CB.b], [pT.b], inc=(g == 15))
            self.cp(self.UT.ap[:, :, t * 64:(t + 1) * 64], fv(pTb, [[64, 16], [1, 64]]), [pT.b], [self.UT.b])
            self.pump(self.pump_n)
            self.npump(3)
        self.ld(ckvb, d['cCKV'][l], q="pool")
        kcol = slice(LS, LS + PAST)
        self.mla_kv(ckvb, kcol, WUK, WUV, ob, vbt, nxt)
        o = nxt(ob)
        self.ld(o, d['cKR'][l], q="pool", out_ap=o.ap[0:32])
        for hh in range(6):
            self.st(d['kb'][hh, 64:96, kcol], o, W=[self.dbuf('kb')], in_ap=o.ap[0:32])

    def mla_kv(self, ckvb, kcol, WUK, WUV, ob, vbt, nxt):
        d = self.d
        for pr in range(3):
            ps = self.psn()
            self.mm(ps.ap, WUK.ap[:, pr * 128:(pr + 1) * 128], ckvb.ap, True, True, [WUK.b, ckvb.b], [ps.b])
            o = nxt(ob)
            self.cp(o.ap, ps.ap, [ps.b], [o.b], eng="act")
            for s in range(2):
                self.st(d['kb'][2 * pr + s, 0:64, kcol], o, W=[self.dbuf('kb')], in_ap=o.ap[s * 64:(s + 1) * 64])
        for s in range(4):
            ps = self.psn()
            self.mm(ps.ap[:, 0:384], ckvb.ap[:, s * 128:(s + 1) * 128], WUV.ap, True, True, [WUV.b, ckvb.b], [ps.b])
            self.cp(vbt.ap[:, s, :], ps.ap[:, 0:384], [ps.b], [vbt.b], eng=("act" if s % 2 else "dve"))
        self.st(d['vb'][kcol, :].rearrange("(s p) c -> p s c", p=128), vbt, W=[self.dbuf('vb')])

    def s5_build(self):
        A, S, d = self.A, self.S, self.d
        l, LS, NS, NT, NB, NBS = self.l, self.LS, self.NS, self.NT, self.NB, self.NBS
        SM = self.SM
        UT = self.UT
        pb = Buf("s5prm")

        def sc(name, n=16):
            t = self.s5a([n], F32, name)
            t.b = pb
            return t
        lre = self.sm('lre')[:, l, :]
        lim = self.sm('lim')[:, l, :]
        ldt = self.sm('ldt')[:, l, :]
        Rp = [pb, SM.b]
        Wp = [pb]
        dt = sc("dt"); th = sc("th"); er = sc("er"); cc = sc("cc"); ss = sc("ss"); cs = sc("cs")
        c_ = sc("c"); s_ = sc("s"); dec = sc("dec"); are = sc("are"); aim = sc("aim")
        den = sc("den"); nre = sc("nre"); cre = sc("cre"); cim = sc("cim"); tA = sc("tA"); tB = sc("tB")
        rho = sc("rho"); rrho = sc("rrho")
        PWr = sc("PWr", 9 * 16); PWi = sc("PWi", 9 * 16)
        WKr = sc("WKr", 9 * 16); WKi = sc("WKi", 9 * 16)
        pw = lambda t, k: t.ap[:, k * 16:(k + 1) * 16]
        self.act(dt.ap, ldt, AF.Exp, Rp, Wp)
        self.tt(th.ap, lim, dt.ap, ALU.mult, Rp, Wp)
        self.tt(er.ap, lre, dt.ap, ALU.mult, Rp, Wp)
        self.act(s_.ap, th.ap, AF.Sin, Rp, Wp, scale=0.125)
        yield
        self.act(tA.ap, th.ap, AF.Sin, Rp, Wp, scale=0.0625)
        self.tt(tA.ap, tA.ap, tA.ap, ALU.mult, Rp, Wp)
        self.ts(c_.ap, tA.ap, -2.0, 1.0, ALU.mult, ALU.add, Rp, Wp)
        for _ in range(3):
            self.tt(cc.ap, c_.ap, c_.ap, ALU.mult, Rp, Wp)
            yield
            self.tt(ss.ap, s_.ap, s_.ap, ALU.mult, Rp, Wp)
            self.tt(cs.ap, c_.ap, s_.ap, ALU.mult, Rp, Wp)
            self.tt(c_.ap, cc.ap, ss.ap, ALU.subtract, Rp, Wp)
            self.ts(s_.ap, cs.ap, 2.0, None, ALU.mult, None, Rp, Wp)
            yield
        self.act(dec.ap, er.ap, AF.Exp, Rp, Wp)
        self.act(rho.ap, er.ap, AF.Exp, Rp, Wp, scale=8.0)
        self.act(rrho.ap, er.ap, AF.Exp, Rp, Wp, scale=-8.0)
        self.tt(are.ap, dec.ap, c_.ap, ALU.mult, Rp, Wp)
        yield
        self.tt(aim.ap, dec.ap, s_.ap, ALU.mult, Rp, Wp)
        self.tt(tA.ap, lre, lre, ALU.mult, Rp, Wp)
        self.tt(tB.ap, lim, lim, ALU.mult, Rp, Wp)
        self.tt(den.ap, tA.ap, tB.ap, ALU.add, Rp, Wp)
        yield
        self.recip(den.ap, den.ap, Rp, Wp)
        self.ts(nre.ap, are.ap, -1.0, None, ALU.add, None, Rp, Wp)
        self.tt(tA.ap, nre.ap, lre, ALU.mult, Rp, Wp)
        self.tt(tB.ap, aim.ap, lim, ALU.mult, Rp, Wp)
        yield
        self.tt(tA.ap, tA.ap, tB.ap, ALU.add, Rp, Wp)
        self.tt(cre.ap, tA.ap, den.ap, ALU.mult, Rp, Wp)
        self.tt(tA.ap, aim.ap, lre, ALU.mult, Rp, Wp)
        self.tt(tB.ap, nre.ap, lim, ALU.mult, Rp, Wp)
        yield
        self.tt(tA.ap, tA.ap, tB.ap, ALU.subtract, Rp, Wp)
        self.tt(cim.ap, tA.ap, den.ap, ALU.mult, Rp, Wp)

        def cmul(or_, oi_, ar, ai, br, bi):
            self.tt(tA.ap, ar, br, ALU.mult, Rp, Wp)
            self.tt(tB.ap, ai, bi, ALU.mult, Rp, Wp)
            self.tt(cc.ap, ar, bi, ALU.mult, Rp, Wp)
            self.tt(ss.ap, ai, br, ALU.mult, Rp, Wp)
            self.tt(or_, tA.ap, tB.ap, ALU.subtract, Rp, Wp)
            self.tt(oi_, cc.ap, ss.ap, ALU.add, Rp, Wp)
        self.memset(pw(PWr, 0), 1.0, Wp)
        self.memset(pw(PWi, 0), 0.0, Wp)
        for k in range(1, 9):
            cmul(pw(PWr, k), pw(PWi, k), pw(PWr, k - 1), pw(PWi, k - 1), are.ap, aim.ap)
        self.tt(pw(WKr, 0), pw(PWr, 8), rrho.ap, ALU.mult, Rp, Wp)
        self.tt(pw(WKi, 0), pw(PWi, 8), rrho.ap, ALU.mult, Rp, Wp)
        yield
        for k in range(1, 9):
            cmul(pw(WKr, k), pw(WKi, k), pw(WKr, k - 1), pw(WKi, k - 1), pw(WKr, k - 1), pw(WKi, k - 1))
        SB = self.s5a([4, 256], F32, "ssmbc")
        self.ld(SB, d['ssmbc'][l])
        wb = Buf("s5w")
        Rw = [pb, wb, SB.b]
        Ww = [wb]

        def wt(shape, dt_, name):
            t = self.s5a(shape, dt_, name)
            t.b = wb
            return t
        BBr = wt([16, 16], F32, "BBr"); BBi = wt([16, 16], F32, "BBi")
        u1 = wt([16, 16], F32, "u1"); u2 = wt([16, 16], F32, "u2")
        bc16 = lambda ap: fv(ap, [[1, 16], [0, 16]])
        b_re = SB.ap[:, 0].rearrange("p (q h) -> p q h", h=16)
        b_im = SB.ap[:, 1].rearrange("p (q h) -> p q h", h=16)
        c_re = SB.ap[:, 2].rearrange("p (q h) -> p q h", h=16)
        c_im = SB.ap[:, 3].rearrange("p (q h) -> p q h", h=16)
        self.tt(u1.ap, b_re, bc16(cre.ap), ALU.mult, Rw, Ww)
        self.tt(u2.ap, b_im, bc16(cim.ap), ALU.mult, Rw, Ww)
        self.tt(BBr.ap, u1.ap, u2.ap, ALU.subtract, Rw, Ww)
        yield
        self.tt(u1.ap, b_im, bc16(cre.ap), ALU.mult, Rw, Ww)
        self.tt(u2.ap, b_re, bc16(cim.ap), ALU.mult, Rw, Ww)
        self.tt(BBi.ap, u1.ap, u2.ap, ALU.add, Rw, Ww)
        XEr = wt([16, 15, 16], BF16, "XEr"); XEi = wt([16, 15, 16], BF16, "XEi")
        CAr = wt([16, 9, 16], BF16, "CAr"); CAi = wt([16, 9, 16], BF16, "CAi")
        Crb = wt([16, 16], BF16, "Crb"); Cib = wt([16, 16], BF16, "Cib")
        self.memset(fv(XEr.ap, [[1, 16 * 15 * 16]]), 0.0, Ww)
        yield
        self.memset(fv(XEi.ap, [[1, 16 * 15 * 16]]), 0.0, Ww)
        self.cp(Crb.ap, c_re, Rw, Ww)
        self.ts(Cib.ap, c_im, -1.0, None, ALU.mult, None, Rw, Ww)
        u3 = wt([16, 16], F32, "u3"); u4 = wt([16, 16], F32, "u4")
        for k in range(8):
            pr_b = bc16(pw(PWr, k)); pi_b = bc16(pw(PWi, k))
            self.tt(u1.ap, BBr.ap, pr_b, ALU.mult, Rw, Ww)
            yield
            self.tt(u2.ap, BBi.ap, pi_b, ALU.mult, Rw, Ww)
            self.tt(u3.ap, BBi.ap, pr_b, ALU.mult, Rw, Ww)
            self.tt(u4.ap, BBr.ap, pi_b, ALU.mult, Rw, Ww)
            for dr in range(2):
                m = 7 - k if dr == 0 else 7 + k
                qs = slice(dr * 8, dr * 8 + 8)
                self.tt(XEr.ap[:, qs, m, :], u1.ap[:, qs, :], u2.ap[:, qs, :], ALU.subtract, Rw, Ww)
                yield
                self.tt(XEi.ap[:, qs, m, :], u3.ap[:, qs, :], u4.ap[:, qs, :], ALU.add, Rw, Ww)
        for m in range(9):
            pr_b = bc16(pw(PWr, m)); pi_b = bc16(pw(PWi, m))
            self.tt(u1.ap, c_re, pr_b, ALU.mult, Rw, Ww)
            self.tt(u2.ap, c_im, pi_b, ALU.mult, Rw, Ww)
            self.tt(u3.ap, c_im, pr_b, ALU.mult, Rw, Ww)
            yield
            self.tt(u4.ap, c_re, pi_b, ALU.mult, Rw, Ww)
            for dr in range(2):
                qs = slice(dr * 8, dr * 8 + 8)
                mp = m if dr == 0 else 8 - m
                self.tt(CAr.ap[:, qs, mp, :], u1.ap[:, qs, :], u2.ap[:, qs, :], ALU.subtract, Rw, Ww)
                self.stt(CAi.ap[:, qs, mp, :], u3.ap[:, qs, :], -1.0, u4.ap[:, qs, :], ALU.mult, ALU.subtract, Rw, Ww)
        WEr = wt([16, 128], BF16, "WEr"); WEi = wt([16, 128], BF16, "WEi")
        for (XE, WE) in ((XEr, WEr), (XEi, WEi)):
            for hb in range(2):
                ps = self.psn()
                psb = ps.ap.bitcast(BF16)
                for qq in range(8):
                    q = hb * 8 + qq
                    m0 = 0 if q < 8 else 7
                    self.tr(psb[:, qq * 128:(qq + 1) * 128], fv(XE.ap[:, q, m0, :], [[1, 128]]), self.ident,
                            [wb, self.CB.b], [ps.b], inc=(qq == 7))
                self.cp(fv(WE.ap[:, hb * 8, :], [[1, 1024]]), psb, [ps.b], Ww)
                yield
        WL = wt([32, 128], BF16, "WLOC")
        drep = self.sm('drep')
        for dr in range(2):
            for kb in range(4):
                ps = self.psn()
                g2 = kb % 2
                rows = slice(g2 * 64, g2 * 64 + 64)
                for gi in range(4):
                    gp = 4 * (kb // 2) + gi
                    q = dr * 8 + gp
                    for j in range(8):
                        o_ = ps.ap[:, gi * 128 + j * 16: gi * 128 + (j + 1) * 16]
                        self.mm(o_, fv(XEr.ap[rows, q, 7 - j, :], [[1, 128]]), Crb.ap[rows, q, :], True, False, [wb], [ps.b], inc=False)
                        self.mm(o_, fv(XEi.ap[rows, q, 7 - j, :], [[1, 128]]), Cib.ap[rows, q, :], False, True, [wb], [ps.b],
                                inc=(gi == 3 and j == 7))
                g0 = 2 * (4 * (kb // 2)) + g2
                if dr == 0:
                    for gi in range(4):
                        g = g0 + 2 * gi
                        self.stt(WL.ap[:, g, :], self.ident, drep[:, l, g:g + 1], ps.ap[:, gi * 128:(gi + 1) * 128], ALU.mult, ALU.add,
                                 [ps.b, self.CB.b, SM.b], Ww)
                else:
                    self.cp(fv(WL.ap[:, 16 + g0, :], [[256, 4], [1, 128]]), fv(ps.ap, [[128, 4], [1, 128]]), [ps.b], Ww, eng="act")
                yield
        self.s5 = dict(pb=pb, wb=wb, WKr=WKr, WKi=WKi, rho=rho, WEr=WEr, WEi=WEi, CAr=CAr, CAi=CAi, WL=WL)
        yield

    def s5a(self, shape, dtype, name=None):
        if self.S.dry:
            t = self.A.alloc(shape, dtype, name)
            self.s5rec.append(t)
            return t
        return self.s5rec.pop(0)

    def npump(self, n):
        for _ in range(n):
            if self.ngen is None:
                return
            try:
                next(self.ngen)
            except StopIteration:
                self.ngen = None

    def pump(self, n):
        for _ in range(n):
            if self.s5gen is None:
                return
            try:
                next(self.s5gen)
            except StopIteration:
                self.s5gen = None


    def phaseS(self):
        A, S, d = self.A, self.S, self.d
        l, LS, NS, NT, NB, NBS = self.l, self.LS, self.NS, self.NT, self.NB, self.NBS
        SM = self.SM
        UT = self.UT
        s5 = self.s5
        pb, wb, WKr, WKi, rho = s5['pb'], s5['wb'], s5['WKr'], s5['WKi'], s5['rho']
        WEr, WEi, CAr, CAi, WL = s5['WEr'], s5['WEi'], s5['CAr'], s5['CAi'], s5['WL']
        pw = lambda t, k: t.ap[:, k * 16:(k + 1) * 16]
        drep = self.sm('drep')

        SNr = A.alloc([16, NB], BF16, "SINr"); SNi = A.alloc([16, NB], BF16, "SINi")
        FS = A.alloc([2, 2, 16], F32, "FS")
        self.memset(fv(SNr.ap, [[1, 16 * NB]]), 0.0, [SNr.b])
        self.memset(fv(SNi.ap, [[1, 16 * NB]]), 0.0, [SNi.b], eng="pool")
        Tc = A.alloc([512], F32, "Tc"); Ts = A.alloc([512], F32, "Ts"); Tt = A.alloc([256], F32, "Tt")
        TcL = A.alloc([NB], F32, "TcL"); TsL = A.alloc([NB], F32, "TsL")
        Zr = A.alloc([NB], F32, "Zr"); Zi = A.alloc([NB], F32, "Zi")
        Yr = A.alloc([NB], F32, "Yr"); Yi = A.alloc([NB], F32, "Yi")
        m1 = A.alloc([NB], F32, "m1"); m2 = A.alloc([NB], F32, "m2")
        m3 = A.alloc([NB], F32, "m3"); m4 = A.alloc([NB], F32, "m4")
        segs = [(0, NBS), (NBS, NBS + 32), (NBS + 32, NBS + 64)]
        s0r = self.sm('s0re')
        s0i = self.sm('s0im')
        for q in range(16):
            dr, gp = q // 8, q % 8
            self.cp(Tc.ap[:, 0:1], pw(WKr, 0)[:, q:q + 1], [pb], [Tc.b])
            self.cp(Ts.ap[:, 0:1], pw(WKi, 0)[:, q:q + 1], [pb], [Ts.b])
            for k in range(9):
                n = 1 << k
                wr = pw(WKr, k)[:, q:q + 1]
                wi = pw(WKi, k)[:, q:q + 1]
                self.ts(Tt.ap[:, 0:n], Ts.ap[:, 0:n], wi, None, ALU.mult, None, [Ts.b, pb], [Tt.b])
                self.stt(Tc.ap[:, n:2 * n], Tc.ap[:, 0:n], wr, Tt.ap[:, 0:n], ALU.mult, ALU.subtract, [Tc.b, Tt.b, pb], [Tc.b])
                self.ts(Tt.ap[:, 0:n], Tc.ap[:, 0:n], wi, None, ALU.mult, None, [Tc.b, pb], [Tt.b])
                self.stt(Ts.ap[:, n:2 * n], Ts.ap[:, 0:n], wr, Tt.ap[:, 0:n], ALU.mult, ALU.add, [Ts.b, Tt.b, pb], [Ts.b])
            for (a, b_) in segs:
                n = b_ - a
                for (Tx, TxL) in ((Tc, TcL), (Ts, TsL)):
                    if dr == 0:
                        self.cp(TxL.ap[:, a:b_], Tx.ap[:, 0:n], [Tx.b], [TxL.b])
                    else:
                        self.cp(TxL.ap[:, a:b_], fv(Tx.ap, [[-1, n]], off=n - 1), [Tx.b], [TxL.b])
            pi_ = (q % 2) * 2
            Er = self.PST[pi_][:, 0:NB]
            Ei = self.PST[pi_ + 1][:, 0:NB]
            bre = [self.PS[2 * pi_].b, self.PS[2 * pi_ + 1].b]
            bim = [self.PS[2 * pi_ + 2].b, self.PS[2 * pi_ + 3].b]
            for (E_, WE, bb) in ((Er, WEr, bre), (Ei, WEi, bim)):
                for g2 in range(2):
                    rows = slice(g2 * 64, g2 * 64 + 64)
                    chunks = [(c0, min(c0 + 512, NB)) for c0 in range(0, NB, 512)]
                    for ci, (c0, c1) in enumerate(chunks):
                        self.mm(E_[rows, c0:c1], WE.ap[:, q, g2 * 64:(g2 + 1) * 64], UT.ap[:, 2 * gp + g2, c0:c1], True, True,
                                [wb, UT.b], bb, inc=(g2 == 1 and ci == len(chunks) - 1))
            self.tt(m1.ap, Er, TcL.ap, ALU.mult, bre + [TcL.b], [m1.b])
            self.tt(m2.ap, Ei, TsL.ap, ALU.mult, bim + [TsL.b], [m2.b])
            self.tt(m3.ap, Ei, TcL.ap, ALU.mult, bim + [TcL.b], [m3.b])
            self.tt(m4.ap, Er, TsL.ap, ALU.mult, bre + [TsL.b], [m4.b])
            self.tt(Zr.ap, m1.ap, m2.ap, ALU.add, [m1.b, m2.b], [Zr.b], eng="pool")
            self.tt(Zi.ap, m3.ap, m4.ap, ALU.subtract, [m3.b, m4.b], [Zi.b], eng="pool")
            for si, (a, b_) in enumerate(segs):
                n = b_ - a
                for (Z, Y, s0) in ((Zr, Yr, s0r), (Zi, Yi, s0i)):
                    init = s0[:, l, q:q + 1] if si == 0 else 0.0
                    rb = rho.ap[:, q:q + 1].to_broadcast([128, n])
                    if dr == 0:
                        o_, i_ = Y.ap[:, a:b_], Z.ap[:, a:b_]
                    else:
                        o_, i_ = fv(Y.ap, [[-1, n]], off=b_ - 1), fv(Z.ap, [[-1, n]], off=b_ - 1)
                    S.op("dve", lambda e, o_=o_, rb=rb, i_=i_, init=init: e.tensor_tensor_scan(o_, rb, i_, init, ALU.mult, ALU.add),
                         [Z.b, pb, SM.b], [Y.b])
            self.tt(m1.ap, Yr.ap, TcL.ap, ALU.mult, [Yr.b, TcL.b], [m1.b])
            self.tt(m2.ap, Yi.ap, TsL.ap, ALU.mult, [Yi.b, TsL.b], [m2.b])
            self.tt(m3.ap, Yr.ap, TsL.ap, ALU.mult, [Yr.b, TsL.b], [m3.b], eng="pool")
            self.tt(m4.ap, Yi.ap, TcL.ap, ALU.mult, [Yi.b, TcL.b], [m4.b], eng="pool")
            for si, (a, b_) in enumerate(segs):
                if dr == 0:
                    osl, isl = slice(a + 1, b_), slice(a, b_ - 1)
                    s0col, fcol = a, b_ - 1
                else:
                    osl, isl = slice(a, b_ - 1), slice(a + 1, b_)
                    s0col, fcol = b_ - 1, a
                self.tt(SNr.ap[:, q, osl], m1.ap[:, isl], m2.ap[:, isl], ALU.subtract, [m1.b, m2.b], [SNr.b])
                self.tt(SNi.ap[:, q, osl], m3.ap[:, isl], m4.ap[:, isl], ALU.add, [m3.b, m4.b], [SNi.b])
                if si == 0:
                    self.cp(SNr.ap[:, q, s0col:s0col + 1], s0r[:, l, q:q + 1], [SM.b], [SNr.b])
                    self.cp(SNi.ap[:, q, s0col:s0col + 1], s0i[:, l, q:q + 1], [SM.b], [SNi.b])
                else:
                    pr = si - 1
                    self.tt(FS.ap[:, 0, pr, q:q + 1], m1.ap[:, fcol:fcol + 1], m2.ap[:, fcol:fcol + 1], ALU.subtract, [m1.b, m2.b], [FS.b])
                    self.tt(FS.ap[:, 1, pr, q:q + 1], m3.ap[:, fcol:fcol + 1], m4.ap[:, fcol:fcol + 1], ALU.add, [m3.b, m4.b], [FS.b])
        self.st(d['o_ss'][l], FS, W=[self.dbuf('o_ss')])
        if 'stopS2' in self.dbg:
            return
        WG = A.alloc([2, 256], BF16, "WGLU")
        self.ld(WG, d['w_glu'][l].rearrange("(i p) n -> p i n", p=128), q="pool")
        TM = A.alloc([8, 256], BF16, "TM")
        GF = [A.alloc([2, 512], BF16, "GF%d" % i) for i in range(2)]
        x2 = [A.alloc([512], F32, "gx2%d" % i) for i in range(2)]
        ux = [A.alloc([512], F32, "gux%d" % i) for i in range(2)]
        sg = [A.alloc([512], F32, "gsg%d" % i) for i in range(2)]
        oc = [A.alloc([512], BF16, "oc%d" % i) for i in range(2)]
        bglu = self.sm('bglu')
        for t in range(self.NTILE):
            tok = slice(t * 512, (t + 1) * 512)
            bc = slice(t * 64, (t + 1) * 64)
            for kb in range(4):
                ps = self.psn()
                g2 = kb % 2
                rows = slice(g2 * 64, g2 * 64 + 64)
                for gi in range(4):
                    gp = 4 * (kb // 2) + gi
                    g = 2 * gp + g2
                    o_ = ps.ap[0:64, gi * 128:(gi + 1) * 128]
                    n_mm = 0
                    for dr in range(2):
                        q = dr * 8 + gp
                        if dr == 0:
                            rr_ = fv(CAr.ap[rows, q, 1, :], [[1, 128]])
                            ri_ = fv(CAi.ap[rows, q, 1, :], [[1, 128]])
                        else:
                            rr_ = fv(CAr.ap[rows, q, 0, :], [[1, 128]])
                            ri_ = fv(CAi.ap[rows, q, 0, :], [[1, 128]])
                        for (lh, rh, Rb) in ((SNr.ap[rows, q, bc], rr_, [SNr.b, wb]), (SNi.ap[rows, q, bc], ri_, [SNi.b, wb]),
                                             (UT.ap[:, g, bc], WL.ap[:, dr * 16 + g, :], [UT.b, wb])):
                            self.mm(o_, lh, rh, n_mm == 0, n_mm == 5, Rb, [ps.b], inc=(gi == 3 and n_mm == 5))
                            n_mm += 1
                i2 = kb % 2
                self.act(x2[i2].ap[0:64], ps.ap[0:64], AF.Square, [ps.b], [x2[i2].b])
                self.ts(x2[i2].ap[0:64], x2[i2].ap[0:64], 0.044715, 1.0, ALU.mult, ALU.add, [x2[i2].b], [x2[i2].b])
                self.tt(ux[i2].ap[0:64], x2[i2].ap[0:64], ps.ap[0:64], ALU.mult, [x2[i2].b, ps.b], [ux[i2].b])
                self.act(sg[i2].ap[0:64], ux[i2].ap[0:64], AF.Sigmoid, [ux[i2].b], [sg[i2].b], scale=1.5957691216057308)
                ch0 = (8 * (kb // 2) + g2) * 16
                self.tt(fv(TM.ap[0:64], [[32, 4], [256, 8], [1, 16]], off=ch0),
                        fv(ps.ap[0:64], [[128, 4], [16, 8], [1, 16]]),
                        fv(sg[i2].ap[0:64], [[128, 4], [16, 8], [1, 16]]), ALU.mult, [ps.b, sg[i2].b], [TM.b])
            pT = self.psn()
            pTb = pT.ap.bitcast(BF16)
            for j in range(8):
                for ct in range(2):
                    self.tr(pTb[:, (j * 2 + ct) * 64:(j * 2 + ct + 1) * 64], TM.ap[0:64, j, ct * 128:(ct + 1) * 128], self.ident[0:64, 0:64],
                            [TM.b, self.CB.b], [pT.b], inc=(j == 7 and ct == 1))
            gf = GF[t % 2]
            for ct in range(2):
                self.cp(fv(gf.ap[:, ct, :], [[1, 8], [8, 64]]), fv(pTb, [[128, 8], [1, 64]], off=ct * 64), [pT.b], [gf.b],
                        eng=("act" if ct else "dve"))
            for mt in range(2):
                ps = self.psn()
                for ct in range(2):
                    self.mm(ps.ap, WG.ap[:, ct, mt * 128:(mt + 1) * 128], gf.ap[:, ct, :], ct == 0, ct == 1, [WG.b, gf.b], [ps.b], inc=(ct == 1))
                self.act(sg[mt].ap, ps.ap, AF.Sigmoid, [ps.b, SM.b], [sg[mt].b], bias=bglu[:, l, mt:mt + 1])
                self.tt(oc[mt].ap, gf.ap[:, mt, :], sg[mt].ap, ALU.mult, [gf.b, sg[mt].b], [oc[mt].b])
                self.st(d['mix'][6 + mt, :, tok], oc[mt], W=[self.dbuf('mix')])

    def attn_bufs(self):
        A = self.A
        self.attn_sbanks = [self.PS[0], self.PS[1], self.PS[2], self.PS[3]]
        self.sbi = 0
        self.pbi = 0
        self.rci = 0
        self.obi = 0
        self.fin_pending = None
        Pb = [A.alloc([2, 512], BF16, "P%d" % i) for i in range(5)]
        rcs = [(A.alloc([512], F32, "rcf%d" % i), A.alloc([512], BF16, "rch%d" % i), A.alloc([512], BF16, "rcl%d" % i)) for i in range(4)]
        onf = [(A.alloc([512], F32, "bcs%d" % i), A.alloc([512], BF16, "on%d" % i)) for i in range(4)]
        return Pb, rcs, onf

    def ffn_weight_cast(self):
        d, l = self.d, self.l
        w1v = d['w_f1'][l].rearrange("(kt p) n -> p kt n", p=128)
        w2v = d['w_f2'][l].rearrange("(f p) n -> p f n", p=128)
        for fc in range(FT // 2):
            for gu in range(2):
                c0 = gu * DFF + fc * 256
                self.S.dma("pool", d['w1s'][fc, :, :, gu, :], w1v[:, :, c0:c0 + 256])
                yield
        for m in range(8):
            self.S.dma("pool", d['w2s'][m], w2v[:, :, m * 128:(m + 1) * 128])
            yield

    def cast_pump(self, n):
        for _ in range(n):
            if self.castgen is None:
                return
            try:
                next(self.castgen)
            except StopIteration:
                self.castgen = None

    def phaseGA(self):
        A, S, d = self.A, self.S, self.d
        l, LS, NS, NT, NKEY = self.l, self.LS, self.NS, self.NT, self.NKEY
        self.castgen = self.ffn_weight_cast()
        NKT = NKEY // 128
        nlat = LS // 128
        Kt = A.alloc([NKEY], BF16, "Kt")
        self.ld(Kt, d['ka'][:, 0:LS], out_ap=Kt.ap[:, 0:LS], R=[self.dbuf('ka')])
        self.ld(Kt, d['ka'][:, LS + PAST:NKEY], out_ap=Kt.ap[:, LS + PAST:NKEY], R=[self.dbuf('ka')])
        self.ld(Kt, d['cK'][l], out_ap=Kt.ap[:, LS:LS + PAST], q="pool")
        Vt = A.alloc([NKT, 2, 65], BF16, "Vt")
        for g in range(2):
            self.ld(Vt, d['va'][0:LS, g * 64:(g + 1) * 64].rearrange("(kt p) e -> p kt e", p=128), out_ap=Vt.ap[:, 0:nlat, g, 0:64],
                    R=[self.dbuf('va')])
            self.ld(Vt, d['va'][LS:NT, g * 64:(g + 1) * 64].rearrange("(kt p) e -> p kt e", p=128), out_ap=Vt.ap[:, nlat + 4:NKT, g, 0:64],
                    R=[self.dbuf('va')])
            self.ld(Vt, d['cV'][l][:, g * 64:(g + 1) * 64].rearrange("(kt p) e -> p kt e", p=128), out_ap=Vt.ap[:, nlat:nlat + 4, g, 0:64],
                    q="pool")
        self.memset(Vt.ap[:, :, :, 64:65], 1.0, [Vt.b])
        QT = [A.alloc([3, 512], BF16, "QT%d" % i) for i in range(2)]
        Pb, rcs, onf = self.attn_bufs()
        scale = 64 ** -0.5
        for t in range(self.NTILE):
            prm = (t == NS)
            tok0 = t * 512
            Q = QT[t % 2]
            self.ld(Q, d['qa'][:, :, tok0:tok0 + 512].rearrange("i p n -> p i n"), R=[self.dbuf('qa')])
            if not prm:
                jobs = [(0, 512, list(range(0, nlat + 4)))]
            else:
                jobs = [(p * 256, 256, [nlat + 4 + 2 * p, nlat + 4 + 2 * p + 1]) for p in range(2)]
            for (c0, N, kts) in jobs:
                for i in range(3):
                    heads = []
                    for s_, hh in enumerate((i, i + 3)):
                        heads.append((self.PS[6 + s_], d['mix'][hh // 2, (hh % 2) * 64:(hh % 2) * 64 + 64, tok0 + c0:tok0 + c0 + N]))
                    steps = []
                    for kt in kts:
                        st_ = []
                        for s_ in range(2):
                            rows = slice(s_ * 64, s_ * 64 + 64)
                            st_.append((Kt.ap[rows, kt * 128:(kt + 1) * 128], Q.ap[rows, i, c0:c0 + N], Vt.ap[:, kt, s_, :], s_, [Kt.b, Q.b], [Vt.b]))
                        steps.append(st_)
                    self.attn_core3(steps, heads, N, scale, Pb, rcs, onf)
                    self.cast_pump(2)
        self.attn_flush()
        self.cast_pump(1000)

    def attn_core3(self, steps, heads, N, scale, Pb, rcs, onf):
        nsteps = len(steps)
        started = [False] * len(heads)
        last_step_of = [max(si for si, st_ in enumerate(steps) for sl in st_ if sl[3] == hi) for hi in range(len(heads))]
        prev = None

        def pv(si, st_, P_):
            for j, sl in enumerate(st_):
                hi = sl[3]
                O = heads[hi][0]
                is_last = (si == last_step_of[hi]) and all(s2[3] != hi for s2 in st_[j + 1:])
                self.mm(O.ap[0:65, 0:N], sl[2], P_.ap[:, j, 0:N], not started[hi], is_last, [P_.b] + sl[5], [O.b], inc=is_last)
                started[hi] = True

        fin_prev = self.fin_pending
        self.fin_pending = None
        pend = []
        for si, st_ in enumerate(steps):
            if si == 2 and fin_prev is not None:
                fin_prev()
                fin_prev = None
            di = self.sbi % 3
            self.sbi += 1
            Sd = self.PST[di]
            sb = [self.PS[2 * di].b, self.PS[2 * di + 1].b]
            for j, sl in enumerate(st_):
                self.mm(Sd[:, j * 512:j * 512 + N], sl[0], sl[1], True, True, sl[4], sb, inc=(j == len(st_) - 1))
            if len(pend) >= 2:
                pv(*pend.pop(0))
            P_ = Pb[self.pbi % len(Pb)]
            self.pbi += 1
            nj = len(st_)
            self.act(fv(P_.ap[:, 0, :], [[512, nj], [1, N]]), fv(Sd, [[512, nj], [1, N]]), AF.Exp, sb, [P_.b], scale=scale)
            pend.append((si, st_, P_))
        if fin_prev is not None:
            fin_prev()
            fin_prev = None
        while pend:
            pv(*pend.pop(0))
        parts = []
        for hi, (O, dst) in enumerate(heads):
            rcf, rch, rcl = rcs[self.rci % len(rcs)]
            bcs, on = onf[self.rci % len(onf)]
            self.rci += 1
            self.recip(rcf.ap[64:65, 0:N], O.ap[64:65, 0:N], [O.b], [rcf.b])
            self.cp(rch.ap[64:65, 0:N], rcf.ap[64:65, 0:N], [rcf.b], [rch.b])
            self.tt(rcl.ap[64:65, 0:N], rcf.ap[64:65, 0:N], rch.ap[64:65, 0:N], ALU.subtract, [rcf.b, rch.b], [rcl.b])
            parts.append((O, dst, rch, rcl, bcs, on))

        def fin_b(parts=parts, N=N):
            for (O, dst, rch, rcl, bcs, on) in parts:
                bcp = self.PS[2 * (self.sbi % 3)]
                self.sbi += 1
                self.mm(bcp.ap[0:64, 0:N], self.ones[64:65, 0:64], rch.ap[64:65, 0:N], True, False, [rch.b, self.CB.b], [bcp.b], inc=False)
                self.mm(bcp.ap[0:64, 0:N], self.ones[64:65, 0:64], rcl.ap[64:65, 0:N], False, True, [rcl.b, self.CB.b], [bcp.b])
                self.cp(bcs.ap[0:64, 0:N], bcp.ap[0:64, 0:N], [bcp.b], [bcs.b])
                self.tt(on.ap[0:64, 0:N], O.ap[0:64, 0:N], bcs.ap[0:64, 0:N], ALU.mult, [O.b, bcs.b], [on.b])
                self.st(dst, on, in_ap=on.ap[0:64, 0:N])
        self.fin_pending = fin_b

    def attn_flush(self):
        if self.fin_pending is not None:
            self.fin_pending()
            self.fin_pending = None

    def phaseMA(self):
        A, S, d = self.A, self.S, self.d
        l, LS, NS, NT, NKEY = self.l, self.LS, self.NS, self.NT, self.NKEY
        NKT = NKEY // 128
        nlat = LS // 128
        KH = [A.alloc([NKEY], BF16, "KH%d" % i) for i in range(2)]
        VH = [A.alloc([NKT, 65], BF16, "VH%d" % i) for i in range(2)]
        for i in range(2):
            self.memset(VH[i].ap[:, :, 64:65], 1.0, [VH[i].b])
        QH = [A.alloc([512], BF16, "QH%d" % i) for i in range(3)]
        Pb, rcs, onf = self.attn_bufs()
        scale = 96 ** -0.5
        qi = 0
        for hh in range(6):
            Kh = KH[hh % 2]
            Vh = VH[hh % 2]
            self.ld(Kh, d['kb'][hh], out_ap=Kh.ap[0:96, :], R=[self.dbuf('kb')])
            self.ld(Vh, d['vb'][:, hh * 64:(hh + 1) * 64].rearrange("(kt p) e -> p kt e", p=128), out_ap=Vh.ap[:, :, 0:64], R=[self.dbuf('vb')])
            for t in range(self.NTILE):
                prm = (t == NS)
                tok0 = t * 512
                Q = QH[qi % 3]
                qi += 1
                self.ld(Q, d['qb'][hh, :, tok0:tok0 + 512], out_ap=Q.ap[0:96, :], R=[self.dbuf('qb')])
                if not prm:
                    jobs = [(0, 512, list(range(0, nlat + 4)))]
                else:
                    jobs = [(p * 256, 256, [nlat + 4 + 2 * p, nlat + 4 + 2 * p + 1]) for p in range(2)]
                for (c0, N, kts) in jobs:
                    Ob = self.PS[6 + (self.obi % 2)]
                    self.obi += 1
                    dst = d['mix'][3 + hh // 2, (hh % 2) * 64:(hh % 2) * 64 + 64, tok0 + c0:tok0 + c0 + N]
                    steps = []
                    for k2 in range(0, len(kts), 2):
                        st_ = []
                        for kt in kts[k2:k2 + 2]:
                            st_.append((Kh.ap[0:96, kt * 128:(kt + 1) * 128], Q.ap[0:96, c0:c0 + N], Vh.ap[:, kt, :], 0, [Kh.b, Q.b], [Vh.b]))
                        steps.append(st_)
                    self.attn_core3(steps, [(Ob, dst)], N, scale, Pb, rcs, onf)
        self.attn_flush()

    def phaseO(self, xcur):
        A, S, d = self.A, self.S, self.d
        l, NS = self.l, self.NS
        WO = A.alloc([8, D], BF16, "WO")
        self.ld(WO, d['w_out'][l].rearrange("(kt p) n -> p kt n", p=128), q="pool")
        MX = [A.alloc([8, 512], BF16, "mx%d" % i) for i in range(2)]
        XT = [A.alloc([8, 512], F32, "xo%d" % i) for i in range(2)]
        XM = [A.alloc([8, 512], F32, "xmo%d" % i) for i in range(2)]
        xv = xcur.rearrange("(kt p) n -> p kt n", p=128)
        xmv = d['xm'].rearrange("(kt p) n -> p kt n", p=128)
        MOD = self.MOD
        for t in range(self.NTILE):
            c = 1 if t == NS else 0
            tok = slice(t * 512, (t + 1) * 512)
            mx, xt, xm = MX[t % 2], XT[t % 2], XM[t % 2]
            self.ld(mx, d['mix'][:, :, tok].rearrange("k p n -> p k n"), R=[self.dbuf('mix')])
            self.ld(xt, xv[:, :, tok], R=[self.dbuf('xcur')])
            for m in range(8):
                ps = self.psn()
                for kt in range(8):
                    self.mm(ps.ap, WO.ap[:, kt, m * 128:(m + 1) * 128], mx.ap[:, kt, :], kt == 0, kt == 7, [WO.b, mx.b], [ps.b], inc=(kt == 7))
                self.stt(xm.ap[:, m, :], ps.ap, MOD.ap[:, l, 2, m, c:c + 1], xt.ap[:, m, :], ALU.mult, ALU.add, [ps.b, xt.b, MOD.b], [xm.b])
            self.st(xmv[:, :, tok], xm, W=[self.dbuf('xm')])

    def phaseF(self, xnext, last):
        A, S, d = self.A, self.S, self.d
        l, NS = self.l, self.NS
        MOD = self.MOD
        TBT = 3
        X = A.alloc([TBT, 8, 512], F32, "Xf")
        H2 = A.alloc([TBT, 8, 512], BF16, "H2")
        ACTV = A.alloc([FT, TBT * 512], BF16, "ACTV")
        W1 = [A.alloc([8, 2, 256], BF16, "w1c%d" % i) for i in range(2)]
        W2 = [A.alloc([FT, 128], BF16, "w2%d" % i) for i in range(2)]
        sq8 = A.alloc([8, 512], BF16, "sq8f")
        tmp8 = T(fv(ACTV.ap, [[1, 8192]]).bitcast(F32).rearrange("p (k n) -> p k n", k=8), ACTV.b)
        rstd = A.alloc([512], F32, "rstdf")
        sgb = [A.alloc([512], F32, "sgf%d" % i) for i in range(3)]
        xmv = d['xm'].rearrange("(kt p) n -> p kt n", p=128)
        xnv = xnext.rearrange("(kt p) n -> p kt n", p=128)
        w1v = d['w_f1'][l].rearrange("(kt p) n -> p kt n", p=128)
        w2v = d['w_f2'][l].rearrange("(f p) n -> p f n", p=128)
        fing = self.sm('fing')
        blocks = []
        t = 0
        while t < self.NTILE:
            blocks.append(list(range(t, min(t + TBT, self.NTILE))))
            t += TBT
        wi = 0
        w2i = 0
        sgi = 0
        for blk in blocks:
            nt = len(blk)
            for ti, t in enumerate(blk):
                c = 1 if t == NS else 0
                tok = slice(t * 512, (t + 1) * 512)
                xt = T(X.ap[:, ti], X.b)
                ht = T(H2.ap[:, ti], H2.b)
                self.ld(xt, xmv[:, :, tok], R=[self.dbuf('xm')])
                self.norm_tile(xt, ht, 4, 3, c, sq8, rstd, tmp8)
            for fc in range(FT // 2):
                w1c = W1[wi % 2]
                wi += 1
                self.ld(w1c, d['w1s'][fc])
                wg = T(w1c.ap[:, :, 0, :], w1c.b)
                wu = T(w1c.ap[:, :, 1, :], w1c.b)
                for fi in range(2):
                    f = fc * 2 + fi
                    for ti in range(nt):
                        gps = self.psn()
                        ups = self.psn()
                        for kt in range(8):
                            self.mm(gps.ap, wg.ap[:, kt, fi * 128:(fi + 1) * 128], H2.ap[:, ti, kt, :], kt == 0, kt == 7, [wg.b, H2.b], [gps.b], inc=(kt == 7))
                        for kt in range(8):
                            self.mm(ups.ap, wu.ap[:, kt, fi * 128:(fi + 1) * 128], H2.ap[:, ti, kt, :], kt == 0, kt == 7, [wu.b, H2.b], [ups.b], inc=(kt == 7))
                        sg = sgb[sgi % 3]
                        sgi += 1
                        self.act(sg.ap, gps.ap, AF.Silu, [gps.b], [sg.b])
                        self.tt(ACTV.ap[:, f, ti * 512:(ti + 1) * 512], sg.ap, ups.ap, ALU.mult, [sg.b, ups.b], [ACTV.b])
            for m in range(8):
                w2 = W2[w2i % 2]
                w2i += 1
                self.ld(w2, d['w2s'][m])
                for ti, t in enumerate(blk):
                    c = 1 if t == NS else 0
                    ps = self.psn()
                    for f in range(FT):
                        self.mm(ps.ap, w2.ap[:, f, :], ACTV.ap[:, f, ti * 512:(ti + 1) * 512], f == 0, f == FT - 1, [w2.b, ACTV.b], [ps.b], inc=(f == FT - 1))
                    self.stt(X.ap[:, ti, m, :], ps.ap, MOD.ap[:, l, 5, m, c:c + 1], X.ap[:, ti, m, :], ALU.mult, ALU.add, [ps.b, X.b, MOD.b], [X.b])
            for ti, t in enumerate(blk):
                tok = slice(t * 512, (t + 1) * 512)
                xt = T(X.ap[:, ti], X.b)
                if not last:
                    self.st(xnv[:, :, tok], xt, W=[self.dbuf('xcur')])
                else:
                    self.act(sq8.ap, xt.ap, AF.Square, [xt.b], [sq8.b])
                    ps = self.psn()
                    for kt in range(8):
                        self.mm(ps.ap, self.ones, sq8.ap[:, kt, :], kt == 0, kt == 7, [sq8.b, self.CB.b], [ps.b], inc=(kt == 7))
                    self.act(rstd.ap, ps.ap, AF.Sqrt, [ps.b, self.eps.b], [rstd.b], bias=self.eps.ap, scale=1.0 / D)
                    self.recip(rstd.ap, rstd.ap, [rstd.b], [rstd.b])
                    self.tt(tmp8.ap, xt.ap, fv(rstd.ap, [[0, 8], [1, 512]]), ALU.mult, [xt.b, rstd.b], [tmp8.b])
                    for kt in range(8):
                        self.act(tmp8.ap[:, kt, :], tmp8.ap[:, kt, :], AF.Identity, [tmp8.b, self.SM.b], [tmp8.b], scale=fing[:, kt:kt + 1])
                    self.st(xnv[:, :, tok], tmp8, W=[self.dbuf('yT')])


def _rope_tables(LS):
    t = np.arange(LS)
    row = (t // 64).astype(np.float32)
    col = (t % 64).astype(np.float32)

    def tab(rot_dim):
        quarter = rot_dim // 4
        inv = (np.float32(10000.0) ** (-np.arange(quarter, dtype=np.float32) / np.float32(quarter))).astype(np.float32)
        ang = np.concatenate([row[:, None] * inv, col[:, None] * inv], axis=-1).astype(np.float32)
        c = np.cos(ang.astype(np.float64)).astype(np.float32)
        s = np.sin(ang.astype(np.float64)).astype(np.float32)
        c = np.concatenate([c, c], axis=-1)
        s = np.concatenate([s, s], axis=-1)
        return c.T, s.T

    cg, sg = tab(64)
    cm, sm_ = tab(32)
    ropeG = np.stack([np.tile(cg, (2, 1)), np.tile(sg, (2, 1))], axis=1)
    ropeM = np.stack([np.tile(cm, (4, 1)), np.tile(sm_, (4, 1))], axis=1)
    return np.ascontiguousarray(ropeG, np.float32), np.ascontiguousarray(ropeM, np.float32)


def _consts():
    ident = np.eye(128, dtype=np.float32)
    ones = np.ones((128, 128), np.float32)
    bd2 = np.zeros((128, 128), np.float32)
    bd2[0:64, 0:64] = 1
    bd2[64:128, 64:128] = 1

    def rot(n_heads, hd):
        m = np.zeros((128, 128), np.float32)
        half = hd // 2
        for hh in range(n_heads):
            b = hh * hd
            for dd in range(half):
                m[b + dd + half, b + dd] = -1.0
                m[b + dd, b + dd + half] = 1.0
        return m

    cb = np.concatenate([ident, ones, bd2, rot(2, 64), rot(4, 32)], axis=1)
    return cb.astype(ml_dtypes.bfloat16)


def _pack_small(inp, core, NL, SL, NSP):
    a = np.zeros((128, NSP), np.float32)

    def put(name, arr):
        off, shape = SL[name]
        arr = np.asarray(arr, np.float32).reshape(128, prod(shape))
        a[:, off:off + prod(shape)] = arr

    def fm(v, nt):
        return np.asarray(v)[:NL].reshape(NL, nt, 128).transpose(2, 0, 1)

    put('n1g', fm(inp['norm1_g'], 8))
    put('n2g', fm(inp['norm2_g'], 8))
    put('bada', fm(inp['b_ada'], 48))
    put('qn', np.tile(np.asarray(inp['gqa_q_norm'])[:NL].T, (2, 1)))
    put('kn', np.tile(np.asarray(inp['gqa_k_norm'])[:NL].T, (2, 1)))
    put('mqn', fm(inp['mla_q_norm'], 2))
    put('mkvn', fm(inp['mla_kv_norm'], 1))
    put('bglu', fm(inp['ssm_b_glu'], 2))
    put('fing', np.asarray(inp['final_g']).reshape(8, 128).T)
    cond = np.stack([np.asarray(inp['c'])[core], np.asarray(inp['c_ctx'])], axis=-1)
    put('cond', cond.reshape(8, 128, 2).transpose(1, 0, 2))

    def sq(v):
        v = np.asarray(v)[:NL].reshape(NL, 2, 8, 2, 64)
        return v.transpose(3, 4, 0, 1, 2).reshape(128, NL, 16)

    put('lre', sq(inp['ssm_lam_re']))
    put('lim', sq(inp['ssm_lam_im']))
    ldt = np.broadcast_to(np.asarray(inp['ssm_log_dt'])[:NL, :, :, None], (NL, 2, 16, 64))
    put('ldt', sq(ldt))
    put('s0re', sq(np.asarray(inp['state_ssm_re'])[core]))
    put('s0im', sq(np.asarray(inp['state_ssm_im'])[core]))
    dd = np.asarray(inp['ssm_d'])[:NL].reshape(NL, 16, 16)
    drep = np.broadcast_to(dd.transpose(2, 0, 1)[None], (8, 16, NL, 16)).reshape(128, NL, 16)
    put('drep', drep)
    return a


def _pack_ssmbc(inp, NL):
    def bq(v):
        v = np.asarray(v)[:NL].reshape(NL, 2, 8, 2, 64, 16)
        return v.transpose(3, 4, 0, 1, 2, 5).reshape(128, NL, 256)

    def cq(v):
        v = np.asarray(v)[:NL].reshape(NL, 2, 8, 2, 16, 64)
        return v.transpose(3, 5, 0, 1, 2, 4).reshape(128, NL, 256)

    a = np.stack([bq(inp['ssm_b_re']), bq(inp['ssm_b_im']), cq(inp['ssm_c_re']), cq(inp['ssm_c_im'])], axis=2)
    return np.ascontiguousarray(a.transpose(1, 0, 2, 3), np.float32)


def make_in_maps(inp, LS, NL, n_cores=8):
    SL, NSP = small_layout(NL)
    ropeG, ropeM = _rope_tables(LS)
    cbf = _consts()
    ssmbc = _pack_ssmbc(inp, NL)
    f = lambda k: np.ascontiguousarray(np.asarray(inp[k])[:NL], np.float32)
    shared = {
        'ssmbc': ssmbc, 'cbf': cbf, 'ropeG': ropeG, 'ropeM': ropeM,
        'w_ada': f('w_ada'), 'w_in': f('w_in'), 'w_uq': f('mla_w_uq'), 'w_uk': f('mla_w_uk'), 'w_uv': f('mla_w_uv'),
        'w_glu': f('ssm_w_glu'), 'w_out': f('w_out'), 'w_f1': f('w_ffn_in'), 'w_f2': f('w_ffn_out'),
    }
    maps = []
    xs = np.asarray(inp['x_sample'])
    xp = np.asarray(inp['x_prompt'])
    for c in range(n_cores):
        m = dict(shared)
        x = np.concatenate([xs[c, :LS], xp[2 * c], xp[2 * c + 1]], axis=0)
        m['xT'] = np.ascontiguousarray(x.T, np.float32)
        m['smallp'] = _pack_small(inp, c, NL, SL, NSP)
        m['cK'] = np.ascontiguousarray(np.asarray(inp['cache_gqa_k'])[c, :NL].reshape(NL, PAST, 128).transpose(0, 2, 1), np.float32)
        m['cV'] = np.ascontiguousarray(np.asarray(inp['cache_gqa_v'])[c, :NL].reshape(NL, PAST, 128), np.float32)
        m['cCKV'] = np.ascontiguousarray(np.asarray(inp['cache_mla_ckv'])[c, :NL].transpose(0, 2, 1), np.float32)
        m['cKR'] = np.ascontiguousarray(np.asarray(inp['cache_mla_krope'])[c, :NL].transpose(0, 2, 1), np.float32)
        maps.append(m)
    return maps


def assemble(results, LS, NL, n_cores=8):
    B = 2 * n_cores
    y_prompt = np.zeros((B, LP, D), np.float32)
    y_sample = np.zeros((n_cores, LS, D), np.float32)
    nk = np.zeros((B, NL, LP, 2, 64), np.float32)
    nv = np.zeros((B, NL, LP, 2, 64), np.float32)
    nckv = np.zeros((B, NL, LP, 128), np.float32)
    nkr = np.zeros((B, NL, LP, 32), np.float32)
    sre = np.zeros((B, NL, 2, 16, 64), np.float32)
    sim = np.zeros((B, NL, 2, 16, 64), np.float32)
    for c in range(n_cores):
        r = results[c]
        y = np.asarray(r['yT']).T
        y_sample[c] = y[:LS]
        for p in range(2):
            b = 2 * c + p
            y_prompt[b] = y[LS + p * LP: LS + (p + 1) * LP]
            sl = slice(p * LP, (p + 1) * LP)
            nk[b] = np.asarray(r['o_k'])[:, :, sl].transpose(0, 2, 1).reshape(NL, LP, 2, 64)
            nv[b] = np.asarray(r['o_v'])[:, sl, :].reshape(NL, LP, 2, 64)
            nckv[b] = np.asarray(r['o_ckv'])[:, :, sl].transpose(0, 2, 1)
            nkr[b] = np.asarray(r['o_kr'])[:, :, sl].transpose(0, 2, 1)
            ss = np.asarray(r['o_ss'])
            for ri, dst in ((0, sre), (1, sim)):
                v = ss[:, :, ri, p, :].reshape(NL, 2, 64, 2, 8)
                dst[b] = v.transpose(0, 3, 4, 1, 2).reshape(NL, 2, 16, 64)
    return (y_prompt, y_sample, nk, nv, nckv, nkr, sre, sim)


_CACHE = {}


def kernel(**inputs):
    LS, NL = 4096, 4
    key = (LS, NL)
    if key not in _CACHE:
        _CACHE[key] = Prog(LS, NL).build()
    nc = _CACHE[key]
    maps = make_in_maps(inputs, LS, NL)
    res = run_bass_kernel_spmd(nc, maps, core_ids=list(range(8)))
    return assemble(res.results, LS, NL)
```
